# Optimizing a Trainium2 kernel written in Bass

```python
import math
import jax, jax.numpy as jnp
from jax import lax
import numpy as np

D_MODEL = 1024
BATCH = 16
SEQ = 2048
DEPTH = 4

N_MIXERS = 4
GRID_W = 64
Q_BLOCK = 128
LN_EPS = 1e-5
RMS_EPS = 1e-6
DEEPNORM_ALPHA = (2 * DEPTH) ** 0.25
DEEPNORM_BETA = (8 * DEPTH) ** -0.25
D_FF = int(math.ceil(8 * D_MODEL / 3 / 256)) * 256
CONV_WIDTH = 3
DA_HEADS = 8
DA_HEAD_DIM = D_MODEL // (2 * DA_HEADS)
NA_HEADS = 16
NA_HEAD_DIM = D_MODEL // NA_HEADS
NA_MAX_ROWS = 8
NA_WIN_COLS = 16
MLA_HEADS = 16
MLA_Q_RANK = 256
MLA_KV_RANK = 128
MLA_NOPE = 64
MLA_ROPE = 32
MLA_V = 64
ROPE_THETA = 10000.0

kernel_name = "hybrid_interleaved_encoder_block"


def _n_uses(m):
    return len(range(m, DEPTH, N_MIXERS))


def _layer_norm(x, g, b):
    xf = x.astype(jnp.float32)
    mu = jnp.mean(xf, axis=-1, keepdims=True)
    var = jnp.mean(jnp.square(xf - mu), axis=-1, keepdims=True)
    y = (xf - mu) * lax.rsqrt(var + LN_EPS) * g.astype(jnp.float32) + b.astype(jnp.float32)
    return y.astype(x.dtype)


def _rms_norm(x, g):
    xf = x.astype(jnp.float32)
    y = xf * lax.rsqrt(jnp.mean(jnp.square(xf), axis=-1, keepdims=True) + RMS_EPS) * g.astype(jnp.float32)
    return y.astype(x.dtype)


def _to_blocks(t):
    b, s = t.shape[:2]
    return jnp.moveaxis(t.reshape((b, s // Q_BLOCK, Q_BLOCK) + t.shape[2:]), 1, 0)


def _from_blocks(t):
    t = jnp.moveaxis(t, 0, 1)
    return t.reshape((t.shape[0], t.shape[1] * t.shape[2]) + t.shape[3:])


def _alibi_slopes(n):
    return np.array([2.0 ** (-8.0 * (h + 1) / n) for h in range(n)], dtype=np.float32)


def _short_conv(x, w_in, conv_w, w_out):
    s = x.shape[1]
    bg, cg, h = jnp.split(x @ w_in, 3, axis=-1)
    u = jnp.pad(cg * h, ((0, 0), (1, 1), (0, 0)))
    y = conv_w[0] * u[:, 0:s] + conv_w[1] * u[:, 1:s + 1] + conv_w[2] * u[:, 2:s + 2]
    return (bg * y) @ w_out


def _diff_attention(x, w_qkv, lam, subln_g, w_out, layer_idx):
    b, s, _ = x.shape
    q, k, v = jnp.split(x @ w_qkv, 3, axis=-1)
    q = q.reshape(b, s, DA_HEADS, 2, DA_HEAD_DIM)
    k = k.reshape(b, s, DA_HEADS, 2, DA_HEAD_DIM)
    v = v.reshape(b, s, DA_HEADS, 2 * DA_HEAD_DIM)
    lam_init = 0.8 - 0.6 * math.exp(-0.3 * layer_idx)
    lamf = lam.astype(jnp.float32)
    lam_full = jnp.exp(jnp.sum(lamf[0] * lamf[1])) - jnp.exp(jnp.sum(lamf[2] * lamf[3])) + lam_init
    slopes = jnp.asarray(_alibi_slopes(DA_HEADS))[:, None, None, None]
    pos = jnp.arange(s)
    scale = DA_HEAD_DIM ** -0.5

    def block(args):
        qb, pb = args
        sc = jnp.einsum('bqhmd,bkhmd->bhmqk', qb, k).astype(jnp.float32) * scale
        dist = jnp.abs(pb[:, None] - pos[None, :]).astype(jnp.float32)
        p = jax.nn.softmax(sc - slopes * dist, axis=-1)
        a = p[:, :, 0] - lam_full * p[:, :, 1]
        return jnp.einsum('bhqk,bkhe->bqhe', a.astype(v.dtype), v)

    o = _from_blocks(lax.map(block, (_to_blocks(q), pos.reshape(-1, Q_BLOCK))))
    o = _rms_norm(o, subln_g) * (1.0 - lam_init)
    return o.reshape(b, s, -1) @ w_out


def _neighborhood_attention(x, w_qkv, rpb, w_out):
    b, s, _ = x.shape
    rows = s // GRID_W
    kr = min(NA_MAX_ROWS, rows)
    kc = NA_WIN_COLS
    q, k, v = jnp.split(x @ w_qkv, 3, axis=-1)
    grid = (b, rows, GRID_W, NA_HEADS, NA_HEAD_DIM)
    q, k, v = q.reshape(grid), k.reshape(grid), v.reshape(grid)
    col = np.arange(GRID_W)
    cs = np.clip(col - kc // 2, 0, GRID_W - kc)
    col_mask = jnp.asarray((col[None, :] >= cs[:, None]) & (col[None, :] < cs[:, None] + kc))
    dc = np.clip(col[None, :] - col[:, None], -(kc - 1), kc - 1) + (kc - 1)
    rpb_c = rpb[:, :, dc]
    scale = NA_HEAD_DIM ** -0.5

    def row_block(args):
        qr, r = args
        rs = jnp.clip(r - kr // 2, 0, rows - kr)
        kb = lax.dynamic_slice_in_dim(k, rs, kr, axis=1)
        vb = lax.dynamic_slice_in_dim(v, rs, kr, axis=1)
        sc = jnp.einsum('bqhd,bjkhd->bhqjk', qr, kb).astype(jnp.float32) * scale
        dr = rs + jnp.arange(kr) - r + (NA_MAX_ROWS - 1)
        bias = jnp.transpose(rpb_c[:, dr], (0, 2, 1, 3)).astype(jnp.float32)
        sc = jnp.where(col_mask[:, None, :], sc + bias, -jnp.inf)
        p = jax.nn.softmax(sc.reshape(sc.shape[:3] + (kr * GRID_W,)), axis=-1).reshape(sc.shape)
        return jnp.einsum('bhqjk,bjkhd->bqhd', p.astype(vb.dtype), vb)

    o = lax.map(row_block, (jnp.moveaxis(q, 1, 0), jnp.arange(rows)))
    o = jnp.moveaxis(o, 0, 1).reshape(b, s, NA_HEADS * NA_HEAD_DIM)
    return o @ w_out


def _rope(t, cos, sin):
    half = t.shape[-1] // 2
    t1, t2 = t[..., :half], t[..., half:]
    c, sn = cos[:, None, :], sin[:, None, :]
    return jnp.concatenate([t1 * c - t2 * sn, t1 * sn + t2 * c], axis=-1)


def _mla(x, w_a, g_q, g_kv, w_uq, w_ukv, w_out):
    b, s, _ = x.shape
    cq, ckv, k_rope = jnp.split(x @ w_a, [MLA_Q_RANK, MLA_Q_RANK + MLA_KV_RANK], axis=-1)
    cq = _rms_norm(cq, g_q)
    ckv = _rms_norm(ckv, g_kv)
    q = (cq @ w_uq).reshape(b, s, MLA_HEADS, MLA_NOPE + MLA_ROPE)
    kv = (ckv @ w_ukv).reshape(b, s, MLA_HEADS, MLA_NOPE + MLA_V)
    q_nope, q_rope = q[..., :MLA_NOPE], q[..., MLA_NOPE:]
    k_nope, v = kv[..., :MLA_NOPE], kv[..., MLA_NOPE:]
    inv_freq = 1.0 / (ROPE_THETA ** (jnp.arange(0, MLA_ROPE, 2, dtype=jnp.float32) / MLA_ROPE))
    ang = jnp.arange(s, dtype=jnp.float32)[:, None] * inv_freq[None, :]
    cos, sin = jnp.cos(ang).astype(x.dtype), jnp.sin(ang).astype(x.dtype)
    q = jnp.concatenate([q_nope, _rope(q_rope, cos, sin)], axis=-1)
    k_r = jnp.broadcast_to(_rope(k_rope[:, :, None, :], cos, sin), (b, s, MLA_HEADS, MLA_ROPE))
    k = jnp.concatenate([k_nope, k_r], axis=-1)
    scale = (MLA_NOPE + MLA_ROPE) ** -0.5

    def block(qb):
        sc = jnp.einsum('bqhd,bkhd->bhqk', qb, k).astype(jnp.float32) * scale
        p = jax.nn.softmax(sc, axis=-1)
        return jnp.einsum('bhqk,bkhd->bqhd', p.astype(v.dtype), v)

    o = _from_blocks(lax.map(block, _to_blocks(q)))
    return o.reshape(b, s, MLA_HEADS * MLA_V) @ w_out


def _swiglu(x, w_gu, w_down):
    g, u = jnp.split(x @ w_gu, 2, axis=-1)
    return (jax.nn.silu(g) * u) @ w_down


def setup_inputs(seed: int = 0) -> dict:
    key = jax.random.key(seed)
    keys = iter(jax.random.split(key, 32))

    def nrm(shape, scale):
        return jax.random.normal(next(keys), shape, jnp.float32) * scale

    def gain(shape):
        return 1.0 + nrm(shape, 0.01)

    d = D_MODEL
    n0, n1, n2, n3 = _n_uses(0), _n_uses(1), _n_uses(2), _n_uses(3)
    beta = DEEPNORM_BETA
    return {
        "x": nrm((BATCH, SEQ, d), 1.0),
        "conv_w_in": nrm((n0, d, 3 * d), d ** -0.5),
        "conv_w": nrm((n0, CONV_WIDTH, d), CONV_WIDTH ** -0.5),
        "conv_w_out": nrm((n0, d, d), d ** -0.5 * beta),
        "diff_w_qkv": nrm((n1, d, 3 * d), d ** -0.5),
        "diff_lambda": nrm((n1, 4, DA_HEAD_DIM), 0.1),
        "diff_subln_g": gain((n1, 2 * DA_HEAD_DIM)),
        "diff_w_out": nrm((n1, d, d), d ** -0.5 * beta),
        "na_w_qkv": nrm((n2, d, 3 * d), d ** -0.5),
        "na_rpb": nrm((n2, NA_HEADS, 2 * NA_MAX_ROWS - 1, 2 * NA_WIN_COLS - 1), 0.05),
        "na_w_out": nrm((n2, d, d), d ** -0.5 * beta),
        "mla_w_a": nrm((n3, d, MLA_Q_RANK + MLA_KV_RANK + MLA_ROPE), d ** -0.5),
        "mla_g_q": gain((n3, MLA_Q_RANK)),
        "mla_g_kv": gain((n3, MLA_KV_RANK)),
        "mla_w_uq": nrm((n3, MLA_Q_RANK, MLA_HEADS * (MLA_NOPE + MLA_ROPE)), MLA_Q_RANK ** -0.5),
        "mla_w_ukv": nrm((n3, MLA_KV_RANK, MLA_HEADS * (MLA_NOPE + MLA_V)), MLA_KV_RANK ** -0.5),
        "mla_w_out": nrm((n3, MLA_HEADS * MLA_V, d), (MLA_HEADS * MLA_V) ** -0.5 * beta),
        "ln1_g": gain((DEPTH, d)),
        "ln1_b": nrm((DEPTH, d), 0.01),
        "ffn_w_gu": nrm((DEPTH, d, 2 * D_FF), d ** -0.5),
        "ffn_w_down": nrm((DEPTH, D_FF, d), D_FF ** -0.5 * beta),
        "ln2_g": gain((DEPTH, d)),
        "ln2_b": nrm((DEPTH, d), 0.01),
    }


def reference(x, conv_w_in, conv_w, conv_w_out, diff_w_qkv, diff_lambda, diff_subln_g, diff_w_out,
              na_w_qkv, na_rpb, na_w_out, mla_w_a, mla_g_q, mla_g_kv, mla_w_uq, mla_w_ukv, mla_w_out,
              ln1_g, ln1_b, ffn_w_gu, ffn_w_down, ln2_g, ln2_b):
    for i in range(DEPTH):
        m, j = i % N_MIXERS, i // N_MIXERS
        if m == 0:
            h = _short_conv(x, conv_w_in[j], conv_w[j], conv_w_out[j])
        elif m == 1:
            h = _diff_attention(x, diff_w_qkv[j], diff_lambda[j], diff_subln_g[j], diff_w_out[j], i)
        elif m == 2:
            h = _neighborhood_attention(x, na_w_qkv[j], na_rpb[j], na_w_out[j])
        else:
            h = _mla(x, mla_w_a[j], mla_g_q[j], mla_g_kv[j], mla_w_uq[j], mla_w_ukv[j], mla_w_out[j])
        x = _layer_norm(DEEPNORM_ALPHA * x + h, ln1_g[i], ln1_b[i])
        x = _layer_norm(DEEPNORM_ALPHA * x + _swiglu(x, ffn_w_gu[i], ffn_w_down[i]), ln2_g[i], ln2_b[i])
    return x
```

```python
import math
import os
from contextlib import ExitStack
import numpy as np
import ml_dtypes
import concourse.bass as bass
import concourse.mybir as mybir
from concourse.bass_utils import run_bass_kernel_spmd

F32 = mybir.dt.float32
BF16 = mybir.dt.bfloat16
AF = mybir.ActivationFunctionType
ALU = mybir.AluOpType

D = 1024
S_LEN = 2048
NT = 16
NCH = 8
DFF = 2816
NJ = 22
ALPHA = 8.0 ** 0.25
LN_EPS = 1e-5
RMS_EPS = 1e-6
NCORES = 8
SEQ_PER_CORE = 2
NEG = -30000.0


class Tile:
    __slots__ = ("name", "writer", "readers")

    def __init__(self, name):
        self.name = name
        self.writer = None
        self.readers = {}


class DSem:
    __slots__ = ("key", "total")

    def __init__(self, key):
        self.key = key
        self.total = 0


class Sched:
    LIMIT = 24000

    def __init__(self, nc):
        self.nc = nc
        self.E = dict(pe=nc.tensor, act=nc.scalar, dve=nc.vector, pool=nc.gpsimd, sp=nc.sync)
        self.sems = {}
        self.nsem = 0
        self.cur = {}
        self.waited = {e: {} for e in self.E}
        self.dsems = []
        self.nwaits = 0
        self.nops = 0
        for e in ("pe", "act", "dve", "pool"):
            self._new_eng_sem(e)

    def _alloc(self, name):
        k = self.nsem
        self.nsem += 1
        self.sems[k] = self.nc.alloc_semaphore(f"s{k}_{name}")
        return k

    def _new_eng_sem(self, e):
        self.cur[e] = [self._alloc(e), 0]

    def dsem(self, name="d"):
        d = DSem(self._alloc(name))
        self.dsems.append(d)
        return d

    def pool_reset(self):
        self.pool_i = 0

    def pds(self, name="p"):
        if not hasattr(self, "pool"):
            self.pool, self.pool_i = [], 0
        if self.pool_i >= len(self.pool):
            self.pool.append(self.dsem(name))
        d = self.pool[self.pool_i]
        self.pool_i += 1
        return d

    def _wait(self, eng, tok):
        key, val = tok[0], tok[1]
        if self.waited[eng].get(key, 0) >= val:
            return
        self.E[eng].wait_ge(self.sems[key], val)
        self.waited[eng][key] = val
        self.nwaits += 1

    def _deps(self, eng, r, w, is_dma):
        deps = []
        for t in r:
            wr = t.writer
            if wr is not None:
                if wr[2] == eng and not is_dma and eng == "pe":
                    continue
                deps.append(wr)
        for t in w:
            wr = t.writer
            if wr is not None and (is_dma or wr[2] != eng):
                deps.append(wr)
            for tok in t.readers.values():
                if is_dma or tok[2] != eng:
                    deps.append(tok)
        return deps

    def _commit(self, tok, r, w):
        for t in r:
            t.readers[tok[0]] = tok
        for t in w:
            t.writer = tok
            t.readers = {}

    def op(self, eng, fn, r=(), w=()):
        for tok in self._deps(eng, r, w, False):
            self._wait(eng, tok)
        cur = self.cur[eng]
        if cur[1] >= self.LIMIT:
            self._new_eng_sem(eng)
            cur = self.cur[eng]
        ins = fn(self.E[eng])
        cur[1] += 1
        ins.then_inc(self.sems[cur[0]], 1)
        tok = (cur[0], cur[1], eng)
        self._commit(tok, r, w)
        self.nops += 1
        return tok

    def dma(self, q, out, in_, ds, r=(), w=(), **kw):
        for tok in self._deps(q, r, w, True):
            self._wait(q, tok)
        if ds.total + 16 > self.LIMIT:
            ds.key = self._alloc("d")
            ds.total = 0
        ins = self.E[q].dma_start(out=out, in_=in_, **kw)
        ds.total += 16
        ins.then_inc(self.sems[ds.key], 16)
        tok = (ds.key, ds.total, "dma")
        self._commit(tok, r, w)
        self.nops += 1
        return tok

    def group_done(self, ds, tiles):
        tok = (ds.key, ds.total, "dma")
        for t in tiles:
            t.writer = tok

    def barrier(self):
        toks = [(c[0], c[1], e) for e, c in self.cur.items() if c[1] > 0]
        toks += [(d.key, d.total, "dma") for d in self.dsems if d.total > 0]
        for e in self.E:
            for tok in toks:
                if tok[2] == e and e == "pe":
                    continue
                self._wait(e, tok)
        self.pool_reset()


def _consts():
    c = {}
    c["ident"] = np.eye(128, dtype=np.float32).astype(ml_dtypes.bfloat16)
    p = np.arange(128)[:, None]
    cc = np.arange(3968)[None, :]
    c["alibi"] = np.abs(p - cc + 1920).astype(np.float32)
    inv_freq = (1.0 / (10000.0 ** (np.arange(0, 32, 2, dtype=np.float32) / np.float32(32)))).astype(np.float32)
    ang = (np.arange(S_LEN, dtype=np.float32)[:, None] * inv_freq[None, :]).astype(np.float32)
    cos, sin = np.cos(ang).astype(np.float32), np.sin(ang).astype(np.float32)
    c["rope_cos_tm"] = np.ascontiguousarray(cos.reshape(NT, 128, 16).transpose(1, 0, 2))
    c["rope_sin_tm"] = np.ascontiguousarray(sin.reshape(NT, 128, 16).transpose(1, 0, 2))
    cf = np.zeros((128, S_LEN), np.float32)
    sf = np.zeros((128, S_LEN), np.float32)
    for i in range(32):
        cf[64 + i] = cos[:, i % 16]
        sf[64 + i] = sin[:, i % 16]
    c["rope_cos_fm"] = cf
    c["rope_sin_fm"] = sf
    col = np.arange(64)
    cs = np.clip(col - 8, 0, 48)
    valid = (col[None, :] >= cs[:, None]) & (col[None, :] < cs[:, None] + 16)
    madd = np.where(valid.T, 0.0, NEG).astype(np.float32)
    c["na_mask"] = np.concatenate([madd, madd], axis=0)
    return c


CONST_SPECS = {
    "ident": ([128, 128], BF16),
    "alibi": ([128, 3968], F32),
    "rope_cos_tm": ([128, NT, 16], F32),
    "rope_sin_tm": ([128, NT, 16], F32),
    "rope_cos_fm": ([128, S_LEN], F32),
    "rope_sin_fm": ([128, S_LEN], F32),
    "na_mask": ([128, 64], F32),
}

INPUT_SHAPES = {
    "conv_w_in": [1, 1024, 3072], "conv_w": [1, 3, 1024], "conv_w_out": [1, 1024, 1024],
    "diff_w_qkv": [1, 1024, 3072], "diff_lambda": [1, 4, 64], "diff_subln_g": [1, 128], "diff_w_out": [1, 1024, 1024],
    "na_w_qkv": [1, 1024, 3072], "na_rpb": [1, 16, 15, 31], "na_w_out": [1, 1024, 1024],
    "mla_w_a": [1, 1024, 416], "mla_g_q": [1, 256], "mla_g_kv": [1, 128], "mla_w_uq": [1, 256, 1536],
    "mla_w_ukv": [1, 128, 2048], "mla_w_out": [1, 1024, 1024],
    "ln1_g": [4, 1024], "ln1_b": [4, 1024], "ffn_w_gu": [4, 1024, 5632], "ffn_w_down": [4, 2816, 1024],
    "ln2_g": [4, 1024], "ln2_b": [4, 1024],
}
DERIVED_SHAPES = {"na_rpbg": [15, 64, 16, 64]}


class Prog:
    def __init__(self, layers=(0, 1, 2, 3), nseq=SEQ_PER_CORE, do_ffn=True):
        self.layers = tuple(layers)
        self.nseq = nseq
        self.do_ffn = do_ffn
        nc = self.nc = bass.Bass("TRN2", target_bir_lowering=False)
        self.S = Sched(nc)
        self.I = {}
        self.I["x"] = nc.dram_tensor("x", [nseq, S_LEN, D], F32, kind="ExternalInput").ap()
        for k, shp in INPUT_SHAPES.items():
            self.I[k] = nc.dram_tensor(k, shp, F32, kind="ExternalInput").ap()
        for k, (shp, dt) in CONST_SPECS.items():
            self.I[k] = nc.dram_tensor("c_" + k, shp, dt, kind="ExternalInput").ap()
        for k, shp in DERIVED_SHAPES.items():
            self.I[k] = nc.dram_tensor(k, shp, F32, kind="ExternalInput").ap()
        self.out = nc.dram_tensor("out", [nseq, S_LEN, D], F32, kind="ExternalOutput").ap()
        self.uid = 0
        self.build()

    def name(self, p):
        self.uid += 1
        return f"{p}{self.uid}"

    def sb(self, st, shape, dt, name="sb"):
        return st.enter_context(self.nc.sbuf_tensor(self.name(name), shape, dt)).ap()

    def ps(self, st, shape, dt, name="ps"):
        return st.enter_context(self.nc.psum_tensor(self.name(name), shape, dt)).ap()

    def scratch(self, shape, dt=BF16, name="scr"):
        return self.nc.dram_tensor(self.name(name), shape, dt, kind="Internal").ap()

    def conv_lhsT(self, w2d, K, N, name):
        S = self.S
        kc, nj = K // 128, N // 128
        scr = self.scratch([nj, 128, kc, 128], name=name)
        ds = S.dsem(name)
        t = Tile(name)
        for j in range(nj):
            src = w2d[:, j * 128:(j + 1) * 128].rearrange("(kc p) n -> p kc n", p=128)
            S.dma("pool", scr[j], src, ds, w=[t])
        t.writer = (ds.key, ds.total, "dma")
        return scr, t

    def conv_rhs(self, w2d, K, N, name):
        S = self.S
        kc = K // 128
        scr = self.scratch([128, kc, N], name=name)
        ds = S.dsem(name)
        t = Tile(name)
        for k in range(kc):
            S.dma("pool", scr[:, k, :], w2d[k * 128:(k + 1) * 128, :], ds, w=[t])
        t.writer = (ds.key, ds.total, "dma")
        return scr, t

    def build(self):
        nc, S, I = self.nc, self.S, self.I
        with ExitStack() as gst:
            cds = S.dsem("const")
            self.t_const = Tile("const")
            self.ident = self.sb(gst, [128, 128], BF16, "ident")
            S.dma("sp", self.ident, I["ident"], cds, w=[self.t_const])
            self.W = {}
            for L in self.layers:
                self.convert_layer(L)
            self.x = self.sb(gst, [128, NT, D], F32, "x")
            self.xT = self.sb(gst, [128, NCH, S_LEN], BF16, "xT")
            self.x_t = [Tile(f"x{t}") for t in range(NT)]
            self.xT_t = [Tile(f"xT{t}") for t in range(NT)]
            self.oT_t = [Tile(f"oT{t}") for t in range(NT)]
            self.out_ds = [S.dsem("out") for _ in range(4)]
            self.xin_ds = [S.dsem("xin") for _ in range(4)]
            for s in range(self.nseq):
                self.load_x(s)
                for L in self.layers:
                    [self.layer_conv, self.layer_diff, self.layer_na, self.layer_mla][L](s)
                    if self.do_ffn:
                        self.ffn(L, s, last=(L == self.layers[-1]))
                if not self.do_ffn:
                    self.store_x(s)
                S.barrier()
            S.barrier()

    def convert_layer(self, L):
        I, W = self.I, self.W
        if L == 0:
            W["conv_in"] = self.conv_lhsT(I["conv_w_in"][0], 1024, 3072, "cwin")
            W["conv_out"] = self.conv_rhs(I["conv_w_out"][0], 1024, 1024, "cwout")
        elif L == 1:
            W["diff_qkv"] = self.conv_lhsT(I["diff_w_qkv"][0], 1024, 3072, "dqkv")
            W["diff_out"] = self.conv_rhs(I["diff_w_out"][0], 1024, 1024, "dwout")
        elif L == 2:
            W["na_qkv"] = self.conv_lhsT(I["na_w_qkv"][0], 1024, 3072, "nqkv")
            W["na_out"] = self.conv_rhs(I["na_w_out"][0], 1024, 1024, "nwout")
        elif L == 3:
            W["mla_a"] = self.conv_rhs(I["mla_w_a"][0], 1024, 416, "mwa")
            W["mla_uq"] = self.conv_rhs(I["mla_w_uq"][0], 256, 1536, "muq")
            W["mla_ukv"] = self.conv_rhs(I["mla_w_ukv"][0], 128, 2048, "mukv")
            W["mla_out"] = self.conv_rhs(I["mla_w_out"][0], 1024, 1024, "mwout")
        if self.do_ffn:
            S = self.S
            scr = self.scratch([NJ, 128, NCH, 256], name=f"wgu{L}")
            ds = S.dsem("wgu")
            t = Tile("wgu")
            w = I["ffn_w_gu"][L]
            for j in range(NJ):
                for half in range(2):
                    src = w[:, half * DFF + j * 128: half * DFF + (j + 1) * 128].rearrange("(kc p) n -> p kc n", p=128)
                    S.dma("pool", scr[j, :, :, half * 128:(half + 1) * 128], src, ds, w=[t])
            t.writer = (ds.key, ds.total, "dma")
            W[f"gu{L}"] = (scr, t)
            W[f"down{L}"] = self.conv_rhs(I["ffn_w_down"][L], DFF, 1024, f"wdn{L}")

    def load_lnp(self, st, L, which):
        S, I = self.S, self.I
        self.lnp = self.sb(st, [128, 2, D], F32, "lnp")
        self.t_lnp = Tile("lnp")
        ds = S.pds("lnp")
        for i, k in enumerate([f"ln{which}_g", f"ln{which}_b"]):
            S.dma("sp", self.lnp[:, i, :], I[k][L].partition_broadcast(128), ds, w=[self.t_lnp])

    def load_x(self, s):
        S, I = self.S, self.I
        xs = I["x"][s].rearrange("(t p) d -> p t d", p=128)
        for t in range(NT):
            S.dma("sp", self.x[:, t, :], xs[:, t, :], self.xin_ds[0], w=[self.x_t[t]])
        S.group_done(self.xin_ds[0], self.x_t)
        with ExitStack() as st:
            xb = [self.sb(st, [128, D], BF16, "xb") for _ in range(2)]
            xb_t = [Tile("xb") for _ in range(2)]
            tp = [self.ps(st, [128, NCH, 128], BF16, "tp") for _ in range(2)]
            tp_t = [Tile("tp") for _ in range(2)]
            for t in range(NT):
                b = t % 2
                self.to_featmajor(t, xb[b], xb_t[b], tp[b], tp_t[b], "act" if t % 2 else "dve")
            S.barrier()

    def to_featmajor(self, t, xb, xb_t, tp, tp_t, eng):
        S = self.S
        if eng == "act":
            S.op("act", lambda e: e.copy(out=xb, in_=self.x[:, t, :]), r=[self.x_t[t]], w=[xb_t])
        else:
            S.op("dve", lambda e: e.tensor_copy(out=xb, in_=self.x[:, t, :]), r=[self.x_t[t]], w=[xb_t])
        for c in range(NCH):
            S.op("pe", lambda e, c=c: e.transpose(out=tp[:, c, :], in_=xb[:, c * 128:(c + 1) * 128], identity=self.ident),
                 r=[xb_t, self.t_const], w=[tp_t])
        dst = self.xT[:, :, t * 128:(t + 1) * 128]
        if eng == "act":
            S.op("dve", lambda e: e.tensor_copy(out=dst, in_=tp), r=[tp_t], w=[self.xT_t[t]])
        else:
            S.op("act", lambda e: e.copy(out=dst, in_=tp), r=[tp_t], w=[self.xT_t[t]])

    def store_x(self, s):
        S = self.S
        os_ = self.out[s].rearrange("(t p) d -> p t d", p=128)
        for t in range(NT):
            S.dma("sp", os_[:, t, :], self.x[:, t, :], self.out_ds[t % 4], r=[self.x_t[t]])

    def ln_epilogue(self, t, y_ps, y_t, gi, L, W, store=None):
        S = self.S
        k = W["k"]
        W["k"] += 1
        b = k % 3
        z, z_t = W["z"][b], W["z_t"][b]
        st6, st6_t = W["st"][b], W["st_t"][b]
        xt = self.x[:, t, :]
        S.op("dve", lambda e: e.scalar_tensor_tensor(out=z, in0=xt, scalar=ALPHA, in1=y_ps, op0=ALU.mult, op1=ALU.add),
             r=[self.x_t[t]] + y_t, w=[z_t])
        for h in range(2):
            S.op("dve", lambda e, h=h: e.bn_stats(out=st6[:, h * 6:(h + 1) * 6], in_=z[:, h * 512:(h + 1) * 512]),
                 r=[z_t], w=[st6_t])
        S.op("dve", lambda e: e.bn_aggr(out=st6[:, 12:14], in_=st6[:, 0:12]), r=[st6_t], w=[st6_t])
        S.op("dve", lambda e: e.tensor_scalar(out=st6[:, 13:14], in0=st6[:, 13:14], scalar1=LN_EPS, scalar2=None,
                                              op0=ALU.add), r=[st6_t], w=[st6_t])
        S.op("act", lambda e: e.activation(out=st6[:, 14:15], in_=st6[:, 13:14], func=AF.Ln), r=[st6_t], w=[st6_t])
        S.op("act", lambda e: e.activation(out=st6[:, 14:15], in_=st6[:, 14:15], func=AF.Exp, scale=-0.5),
             r=[st6_t], w=[st6_t])
        S.op("dve", lambda e: e.scalar_tensor_tensor(out=st6[:, 15:16], in0=st6[:, 12:13], scalar=-1.0, in1=st6[:, 14:15],
                                                     op0=ALU.mult, op1=ALU.mult), r=[st6_t], w=[st6_t])
        S.op("act", lambda e: e.activation(out=z, in_=z, func=AF.Identity, bias=st6[:, 15:16], scale=st6[:, 14:15]),
             r=[z_t, st6_t], w=[z_t])
        g = self.lnp[:, 0, :]
        bb = self.lnp[:, 1, :]
        S.op("pool", lambda e: e.tensor_tensor(out=z, in0=z, in1=g, op=ALU.mult), r=[z_t, self.t_lnp], w=[z_t])
        S.op("dve", lambda e: e.tensor_tensor(out=xt, in0=z, in1=bb, op=ALU.add), r=[z_t, self.t_lnp], w=[self.x_t[t]])
        if store is not None:
            s, dsl = store
            os_ = self.out[s].rearrange("(t p) d -> p t d", p=128)
            S.dma("sp", os_[:, t, :], xt, dsl[t % 4], r=[self.x_t[t]])
            return None
        xb, xb_t = W["xb"][b], W["xb_t"][b]
        S.op("act", lambda e: e.copy(out=xb, in_=xt), r=[self.x_t[t]], w=[xb_t])

        def part_b():
            kb = W["kb"]
            W["kb"] += 1
            tp, tp_t = W["tp"][kb % 2], W["tp_t"][kb % 2]
            for c in range(NCH):
                S.op("pe", lambda e, c=c: e.transpose(out=tp[:, c, :], in_=xb[:, c * 128:(c + 1) * 128], identity=self.ident),
                     r=[xb_t, self.t_const], w=[tp_t])
            dst = self.xT[:, :, t * 128:(t + 1) * 128]
            if kb % 2:
                S.op("dve", lambda e: e.tensor_copy(out=dst, in_=tp), r=[tp_t], w=[self.xT_t[t]])
            else:
                S.op("act", lambda e: e.copy(out=dst, in_=tp), r=[tp_t], w=[self.xT_t[t]])
        return part_b

    def ln_scratch(self, st):
        W = {"k": 0, "kb": 0, "pending": []}
        W["z"] = [self.sb(st, [128, D], F32, "z") for _ in range(3)]
        W["z_t"] = [Tile("z") for _ in range(3)]
        W["st"] = [self.sb(st, [128, 16], F32, "st") for _ in range(3)]
        W["st_t"] = [Tile("st") for _ in range(3)]
        W["xb"] = [self.sb(st, [128, D], BF16, "xb") for _ in range(3)]
        W["xb_t"] = [Tile("xb") for _ in range(3)]
        W["tp"] = [self.ps(st, [128, NCH, 128], BF16, "tp") for _ in range(2)]
        W["tp_t"] = [Tile("tp") for _ in range(2)]
        return W

    def ln_push(self, W, pb):
        if pb is not None:
            W["pending"].append(pb)
        while len(W["pending"]) > 2:
            W["pending"].pop(0)()

    def ln_pop(self, W, n=1):
        for _ in range(n):
            if W["pending"]:
                W["pending"].pop(0)()

    def out_proj_ln(self, wkey, gi, L):
        S = self.S
        scr, wt = self.W[wkey]
        with ExitStack() as st:
            self.load_lnp(st, L, 1)
            w_sb = self.sb(st, [128, NCH, D], BF16, "wout")
            w_t = Tile("wout")
            ds = S.pds("wout")
            for c in range(0, NCH, 2):
                S.dma("sp", w_sb[:, c:c + 2, :], scr[:, c:c + 2, :], ds, r=[wt], w=[w_t])
            LW = self.ln_scratch(st)
            yps = [self.ps(st, [128, D], F32, "y") for _ in range(2)]
            y_t = [[Tile("y0"), Tile("y1")] for _ in range(2)]
            for t in range(NT):
                b = t % 2
                for h in range(2):
                    for c in range(NCH):
                        S.op("pe", lambda e, c=c, h=h: e.matmul(yps[b][:, h * 512:(h + 1) * 512],
                                                              self.oT[:, c, t * 128:(t + 1) * 128],
                                                              w_sb[:, c, h * 512:(h + 1) * 512],
                                                              start=(c == 0), stop=(c == NCH - 1)),
                             r=[self.oT_t[t], w_t], w=[y_t[b][h]])
                self.ln_push(LW, self.ln_epilogue(t, yps[b], y_t[b], gi, L, LW))
            self.ln_pop(LW, 2)
            S.barrier()

    def ffn(self, L, s, last):
        S = self.S
        gscr, gt = self.W[f"gu{L}"]
        dscr, dt_ = self.W[f"down{L}"]
        with ExitStack() as st:
            self.load_lnp(st, L, 2)
            wd = self.sb(st, [128, NJ, D], BF16, "wd")
            wd_t = Tile("wd")
            ds = S.pds("wd")
            for j in range(0, NJ, 2):
                S.dma("sp", wd[:, j:j + 2, :], dscr[:, j:j + 2, :], ds, r=[dt_], w=[wd_t])
            NSLOT = 3
            ring = [self.sb(st, [128, NCH, 256], BF16, "wgu") for _ in range(NSLOT)]
            ring_t = [Tile("wgu") for _ in range(NSLOT)]
            ring_ds = [S.pds("wgu") for _ in range(NSLOT)]
            hT = self.sb(st, [128, NJ, 512], BF16, "hT")
            hT_t = [Tile("hT") for _ in range(4)]
            sg = [self.sb(st, [128, 512], F32, "sg") for _ in range(2)]
            sg_t = [Tile("sg") for _ in range(2)]
            gps = self.ps(st, [128, 512], F32, "g")
            ups = self.ps(st, [128, 512], F32, "u")
            g_t, u_t = Tile("g"), Tile("u")
            LW = self.ln_scratch(st)
            yps = [self.ps(st, [128, D], F32, "y") for _ in range(2)]
            y_t = [[Tile("y0"), Tile("y1")] for _ in range(2)]
            nload = 0
            total = 4 * NJ

            def issue(i):
                j = i % NJ
                sl = i % NSLOT
                S.dma("sp", ring[sl], gscr[j], ring_ds[sl], r=[gt], w=[ring_t[sl]])

            for i in range(min(NSLOT - 1, total)):
                issue(i)
                nload += 1
            it = 0
            for blk in range(4):
                xts = self.xT_t[blk * 4:(blk + 1) * 4]
                for j in range(NJ):
                    if nload < total:
                        issue(nload)
                        nload += 1
                    sl = it % NSLOT
                    it += 1
                    for c in range(NCH):
                        S.op("pe", lambda e, c=c: e.matmul(gps, ring[sl][:, c, 0:128], self.xT[:, c, blk * 512:(blk + 1) * 512],
                                                         start=(c == 0), stop=(c == NCH - 1)),
                             r=[ring_t[sl]] + xts, w=[g_t])
                    for c in range(NCH):
                        S.op("pe", lambda e, c=c: e.matmul(ups, ring[sl][:, c, 128:256], self.xT[:, c, blk * 512:(blk + 1) * 512],
                                                         start=(c == 0), stop=(c == NCH - 1)),
                             r=[ring_t[sl]] + xts, w=[u_t])
                    if j in (2, 5):
                        self.ln_pop(LW, 1)
                    b = j % 2
                    S.op("act", lambda e: e.activation(out=sg[b], in_=gps, func=AF.Silu), r=[g_t], w=[sg_t[b]])
                    S.op("dve", lambda e: e.tensor_tensor(out=hT[:, j, :], in0=sg[b], in1=ups, op=ALU.mult),
                         r=[sg_t[b], u_t], w=hT_t)
                for tt in range(4):
                    t = blk * 4 + tt
                    b = t % 2
                    for h in range(2):
                        for j in range(NJ):
                            S.op("pe", lambda e, j=j, h=h: e.matmul(yps[b][:, h * 512:(h + 1) * 512],
                                                                  hT[:, j, tt * 128:(tt + 1) * 128],
                                                                  wd[:, j, h * 512:(h + 1) * 512],
                                                                  start=(j == 0), stop=(j == NJ - 1)),
                                 r=[hT_t[tt], wd_t], w=[y_t[b][h]])
                    self.ln_push(LW, self.ln_epilogue(t, yps[b], y_t[b], 2, L, LW, store=(s, self.out_ds) if last else None))
            self.ln_pop(LW, 2)
            S.barrier()

    def layer_conv(self, s):
        S, I = self.S, self.I
        scr, wt = self.W["conv_in"]
        with ExitStack() as ost:
          self.oT = self.sb(ost, [128, NCH, S_LEN], BF16, "oT")
          with ExitStack() as st:
            cw = self.sb(st, [128, 3, NCH], F32, "cw")
            cw_t = Tile("cw")
            ds = S.pds("cw")
            S.dma("sp", cw, I["conv_w"][0].rearrange("t (c p) -> p t c", p=128), ds, w=[cw_t],
                  allow_slow_non_contiguous=True)
            NSLOT = 6
            ring = [self.sb(st, [128, NCH, 128], BF16, "win") for _ in range(NSLOT)]
            ring_t = [Tile("win") for _ in range(NSLOT)]
            ring_ds = [S.pds("win") for _ in range(NSLOT)]
            u = [self.sb(st, [128, S_LEN + 2], F32, "u") for _ in range(2)]
            u_t = [Tile("u") for _ in range(2)]
            bg = [self.sb(st, [128, S_LEN], F32, "bg") for _ in range(2)]
            bg_t = [Tile("bg") for _ in range(2)]
            y = self.sb(st, [128, S_LEN], F32, "y")
            y_t = Tile("y")
            cgs = [self.sb(st, [128, 512], F32, "cgs") for _ in range(2)]
            cgs_t = [Tile("cgs") for _ in range(2)]
            pss = [[self.ps(st, [128, 512], F32, "cps") for _ in range(3)] for _ in range(2)]
            pss_t = [[Tile("cps") for _ in range(3)] for _ in range(2)]
            for b in range(2):
                S.op("pool", lambda e, b=b: e.memset(u[b][:, 0:1], 0.0), w=[u_t[b]])
                S.op("pool", lambda e, b=b: e.memset(u[b][:, S_LEN + 1:S_LEN + 2], 0.0), w=[u_t[b]])
            order = [(c, kind) for c in range(NCH) for kind in range(3)]

            def issue(i):
                c, kind = order[i]
                sl = i % NSLOT
                S.dma("sp", ring[sl], scr[kind * 8 + c], ring_ds[sl], r=[wt], w=[ring_t[sl]])

            nload = 0
            for i in range(NSLOT - 3):
                issue(i)
                nload += 1
            kk = 0
            for c in range(NCH):
                for _ in range(3):
                    if nload < len(order):
                        issue(nload)
                        nload += 1
                ub = c % 2
                for blk in range(4):
                    pb = kk % 2
                    kk += 1
                    xts = self.xT_t[blk * 4:(blk + 1) * 4]
                    for kind in range(3):
                        sl = (c * 3 + kind) % NSLOT
                        for k in range(NCH):
                            S.op("pe", lambda e, k=k, kind=kind, sl=sl: e.matmul(
                                pss[pb][kind], ring[sl][:, k, :], self.xT[:, k, blk * 512:(blk + 1) * 512],
                                start=(k == 0), stop=(k == NCH - 1)),
                                r=[ring_t[sl]] + xts, w=[pss_t[pb][kind]])
                    S.op("act", lambda e: e.copy(out=bg[ub][:, blk * 512:(blk + 1) * 512], in_=pss[pb][0]),
                         r=[pss_t[pb][0]], w=[bg_t[ub]])
                    S.op("act", lambda e: e.copy(out=cgs[pb], in_=pss[pb][1]), r=[pss_t[pb][1]], w=[cgs_t[pb]])
                    S.op("dve", lambda e: e.tensor_tensor(out=u[ub][:, 1 + blk * 512:1 + (blk + 1) * 512], in0=cgs[pb],
                                                          in1=pss[pb][2], op=ALU.mult),
                         r=[cgs_t[pb], pss_t[pb][2]], w=[u_t[ub]])
                uu = u[ub]
                S.op("act", lambda e: e.activation(out=y, in_=uu[:, 1:S_LEN + 1], func=AF.Copy, scale=cw[:, 1, c:c + 1]),
                     r=[u_t[ub], cw_t], w=[y_t])
                S.op("dve", lambda e: e.scalar_tensor_tensor(out=y, in0=uu[:, 0:S_LEN], scalar=cw[:, 0, c:c + 1], in1=y,
                                                             op0=ALU.mult, op1=ALU.add), r=[u_t[ub], cw_t, y_t], w=[y_t])
                S.op("dve", lambda e: e.scalar_tensor_tensor(out=y, in0=uu[:, 2:S_LEN + 2], scalar=cw[:, 2, c:c + 1], in1=y,
                                                             op0=ALU.mult, op1=ALU.add), r=[u_t[ub], cw_t, y_t], w=[y_t])
                S.op("pool", lambda e: e.tensor_tensor(out=self.oT[:, c, :], in0=bg[ub], in1=y, op=ALU.mult),
                     r=[bg_t[ub], y_t], w=self.oT_t)
            S.barrier()
          self.out_proj_ln("conv_out", 0, 0)

    def layer_diff(self, s):
        S, I = self.S, self.I
        scr, wt = self.W["diff_qkv"]
        lam_init = 0.8 - 0.6 * math.exp(-0.3 * 1)
        with ExitStack() as ost:
          self.oT = self.sb(ost, [128, NCH, S_LEN], BF16, "oT")
          with ExitStack() as st:
            cds = S.pds("dc")
            alibi = self.sb(st, [128, 3968], F32, "alibi")
            al_t = Tile("alibi")
            S.dma("sp", alibi, I["alibi"], cds, w=[al_t])
            lam_sb = self.sb(st, [128, 4, 64], F32, "lam")
            gsub = self.sb(st, [128, 128], F32, "gsub")
            sm = self.sb(st, [128, 8], F32, "sm")
            prm_t = Tile("prm")
            S.dma("sp", lam_sb, I["diff_lambda"][0].partition_broadcast(128), cds, w=[prm_t])
            S.dma("sp", gsub, I["diff_subln_g"][0].partition_broadcast(128), cds, w=[prm_t])
            S.group_done(cds, [al_t, prm_t])
            lp = self.sb(st, [128, 2, 64], F32, "lp")
            S.op("dve", lambda e: e.tensor_tensor(out=lp, in0=lam_sb[:, 0:4:2, :], in1=lam_sb[:, 1:4:2, :], op=ALU.mult),
                 r=[prm_t], w=[prm_t])
            S.op("dve", lambda e: e.reduce_sum(out=sm[:, 0:2], in_=lp, axis=mybir.AxisListType.X), r=[prm_t], w=[prm_t])
            S.op("act", lambda e: e.activation(out=sm[:, 2:4], in_=sm[:, 0:2], func=AF.Exp), r=[prm_t], w=[prm_t])
            S.op("dve", lambda e: e.tensor_tensor(out=sm[:, 4:5], in0=sm[:, 3:4], in1=sm[:, 2:3], op=ALU.subtract),
                 r=[prm_t], w=[prm_t])
            S.op("dve", lambda e: e.tensor_scalar(out=sm[:, 5:6], in0=sm[:, 4:5], scalar1=-lam_init, scalar2=None, op0=ALU.add),
                 r=[prm_t], w=[prm_t])
            S.op("dve", lambda e: e.tensor_scalar(out=gsub, in0=gsub, scalar1=1.0 - lam_init, scalar2=None, op0=ALU.mult),
                 r=[prm_t], w=[prm_t])
            neglam = sm[:, 5:6]

            NSLOT = 6
            ring = [self.sb(st, [128, NCH, 128], BF16, "wqkv") for _ in range(NSLOT)]
            ring_t = [Tile("wqkv") for _ in range(NSLOT)]
            ring_ds = [S.pds("wqkv") for _ in range(NSLOT)]
            qT = self.sb(st, [128, S_LEN], BF16, "qT")
            kT = self.sb(st, [128, S_LEN], BF16, "kT")
            vA = self.sb(st, [128, NT, 129], BF16, "vA")
            qT_t, kT_t, vA_t = Tile("qT"), Tile("kT"), Tile("vA")
            S.op("pool", lambda e: e.memset(vA[:, :, 128:129], 1.0), w=[vA_t])
            NSB = 3
            sbs = [self.sb(st, [128, 512], F32, "scs") for _ in range(NSB)]
            sbs_t = [Tile("scs") for _ in range(NSB)]
            NPT = 4
            pT = [self.sb(st, [128, 512], BF16, "pT") for _ in range(NPT)]
            pT_t = [Tile("pT") for _ in range(NPT)]
            scp = [self.ps(st, [128, 512], F32, "scp") for _ in range(4)]
            scp_t = [Tile("scp") for _ in range(4)]
            acc = [self.ps(st, [128, 2, 256], F32, "acc") for _ in range(4)]
            acc_t = [Tile("acc") for _ in range(4)]
            ep = self.sb(st, [128, 16], F32, "ep")
            ep_t = Tile("ep")
            t1 = self.sb(st, [128, 128], F32, "t1")
            o32 = self.sb(st, [128, 128], F32, "o32")
            junk = self.sb(st, [128, 128], F32, "junk")
            ob = [self.sb(st, [128, 128], BF16, "ob") for _ in range(2)]
            ob_t = [Tile("ob") for _ in range(2)]
            tpb = [self.ps(st, [128, 128], BF16, "tpb") for _ in range(0)]
            ework_t = Tile("ework")

            def issue(i):
                h, kind = divmod(i, 3)
                sl = i % NSLOT
                S.dma("sp", ring[sl], scr[kind * 8 + h], ring_ds[sl], r=[wt], w=[ring_t[sl]])

            for i in range(3):
                issue(i)
            rot = [0]

            def next_scp():
                i = rot[0] % 4
                rot[0] += 1
                return i

            for h in range(8):
                if h + 1 < 8:
                    for kind in range(3):
                        issue((h + 1) * 3 + kind)
                slq, slk, slv = (h * 3) % NSLOT, (h * 3 + 1) % NSLOT, (h * 3 + 2) % NSLOT
                slope = 2.0 ** (-(h + 1))
                for blk in range(4):
                    xts = self.xT_t[blk * 4:(blk + 1) * 4]
                    for (sl, dst, dst_t, sc) in ((slq, qT, qT_t, 0.125), (slk, kT, kT_t, 1.0)):
                        pi = next_scp()
                        for k in range(NCH):
                            S.op("pe", lambda e, k=k: e.matmul(scp[pi], ring[sl][:, k, :], self.xT[:, k, blk * 512:(blk + 1) * 512],
                                                             start=(k == 0), stop=(k == NCH - 1)),
                                 r=[ring_t[sl]] + xts, w=[scp_t[pi]])
                        S.op("act", lambda e: e.mul(out=dst[:, blk * 512:(blk + 1) * 512], in_=scp[pi], mul=sc),
                             r=[scp_t[pi]], w=[dst_t])
                    pi = next_scp()
                    for tt in range(4):
                        t = blk * 4 + tt
                        for k in range(NCH):
                            S.op("pe", lambda e, k=k: e.matmul(scp[pi][:, tt * 128:(tt + 1) * 128], self.xT[:, k, t * 128:(t + 1) * 128],
                                                             ring[slv][:, k, :], start=(k == 0), stop=(k == NCH - 1)),
                                 r=[ring_t[slv], self.xT_t[t]], w=[scp_t[pi]])
                    S.op("dve", lambda e: e.tensor_copy(out=vA[:, blk * 4:(blk + 1) * 4, 0:128],
                                                        in_=scp[pi].rearrange("p (a b) -> p a b", a=4)),
                         r=[scp_t[pi]], w=[vA_t])
                steps = [(qb, kc) for qb in range(4) for kc in range(16)]
                pend = None
                cnt = 0
                for i in range(len(steps) + 1):
                    cur = None
                    if i < len(steps):
                        qb, kc = steps[i]
                        c0 = 512 * qb - 128 * kc + 1920
                        cur = []
                        for m in range(2):
                            pi = next_scp()
                            S.op("pe", lambda e: e.matmul(scp[pi], kT[m * 64:(m + 1) * 64, kc * 128:(kc + 1) * 128],
                                                          qT[m * 64:(m + 1) * 64, qb * 512:(qb + 1) * 512], start=True, stop=True),
                                 r=[kT_t, qT_t], w=[scp_t[pi]])
                            si = cnt % NSB
                            pj = cnt % NPT
                            cnt += 1
                            S.op("dve", lambda e: e.scalar_tensor_tensor(out=sbs[si], in0=alibi[:, c0:c0 + 512], scalar=-slope,
                                                                         in1=scp[pi], op0=ALU.mult, op1=ALU.add),
                                 r=[al_t, scp_t[pi]], w=[sbs_t[si]])
                            S.op("act", lambda e: e.activation(out=pT[pj], in_=sbs[si], func=AF.Exp), r=[sbs_t[si]], w=[pT_t[pj]])
                            cur.append(pj)
                        cur = (qb, kc, cur)
                    if pend is not None:
                        pqb, pkc, pjs = pend
                        for m in range(2):
                            pj = pjs[m]
                            for qt in range(4):
                                S.op("pe", lambda e: e.matmul(acc[qt][:, m, 0:129], pT[pj][:, qt * 128:(qt + 1) * 128], vA[:, pkc, :],
                                                              start=(pkc == 0 and m == 0), stop=(pkc == 15 and m == 1),
                                                              skip_group_check=True),
                                     r=[pT_t[pj], vA_t], w=[acc_t[qt]])
                        if pkc == 15:
                            for qt in range(4):
                                tq = pqb * 4 + qt
                                a0 = acc[qt][:, 0, 0:128]
                                a1 = acc[qt][:, 1, 0:128]
                                S.op("dve", lambda e: e.reciprocal(out=ep[:, 0:2], in_=acc[qt][:, :, 128]),
                                     r=[acc_t[qt]], w=[ep_t])
                                S.op("dve", lambda e: e.tensor_tensor(out=ep[:, 2:3], in0=ep[:, 1:2], in1=neglam, op=ALU.mult),
                                     r=[ep_t, prm_t], w=[ep_t])
                                S.op("dve", lambda e: e.tensor_scalar(out=t1, in0=a1, scalar1=ep[:, 2:3], scalar2=None, op0=ALU.mult),
                                     r=[ep_t, acc_t[qt]], w=[ework_t])
                                S.op("dve", lambda e: e.scalar_tensor_tensor(out=o32, in0=a0, scalar=ep[:, 0:1], in1=t1,
                                                                             op0=ALU.mult, op1=ALU.add),
                                     r=[ep_t, acc_t[qt], ework_t], w=[ework_t])
                                S.op("act", lambda e: e.activation(out=junk, in_=o32, func=AF.Square, accum_out=ep[:, 4:5]),
                                     r=[ework_t], w=[ep_t])
                                S.op("dve", lambda e: e.tensor_scalar(out=ep[:, 5:6], in0=ep[:, 4:5], scalar1=1.0 / 128.0, scalar2=RMS_EPS,
                                                                      op0=ALU.mult, op1=ALU.add), r=[ep_t], w=[ep_t])
                                S.op("act", lambda e: e.activation(out=ep[:, 6:7], in_=ep[:, 5:6], func=AF.Ln), r=[ep_t], w=[ep_t])
                                S.op("act", lambda e: e.activation(out=ep[:, 7:8], in_=ep[:, 6:7], func=AF.Exp, scale=-0.5),
                                     r=[ep_t], w=[ep_t])
                                bi = tq % 2
                                S.op("dve", lambda e: e.scalar_tensor_tensor(out=ob[bi], in0=o32, scalar=ep[:, 7:8], in1=gsub,
                                                                             op0=ALU.mult, op1=ALU.mult),
                                     r=[ework_t, ep_t, prm_t], w=[ob_t[bi]])
                                pi = next_scp()
                                tview = scp[pi].bitcast(BF16)[:, 0:128]
                                S.op("pe", lambda e: e.transpose(out=tview, in_=ob[bi], identity=self.ident),
                                     r=[ob_t[bi], self.t_const], w=[scp_t[pi]])
                                S.op("act", lambda e: e.copy(out=self.oT[:, h, tq * 128:(tq + 1) * 128], in_=tview),
                                     r=[scp_t[pi]], w=[self.oT_t[tq]])
                    pend = cur
            S.barrier()
          self.out_proj_ln("diff_out", 0, 1)

    def layer_na(self, s):
        S, I = self.S, self.I
        scr, wt = self.W["na_qkv"]
        rpbg = I["na_rpbg"]
        with ExitStack() as ost:
          self.oT = self.sb(ost, [128, NCH, S_LEN], BF16, "oT")
          with ExitStack() as st:
            cds = S.pds("nc")
            M2 = self.sb(st, [128, 14, 16, 64], BF16, "M2")
            M2_t = Tile("M2")
            mask = self.sb(st, [128, 64], F32, "mask")
            mask_t = Tile("mask")
            S.dma("sp", mask, I["na_mask"], cds, w=[mask_t])
            NSLOT = 6
            ring = [self.sb(st, [128, NCH, 128], BF16, "wqkv") for _ in range(NSLOT)]
            ring_t = [Tile("wqkv") for _ in range(NSLOT)]
            ring_ds = [S.pds("wqkv") for _ in range(NSLOT)]
            qT = self.sb(st, [128, S_LEN], BF16, "qT")
            kT = self.sb(st, [128, S_LEN], BF16, "kT")
            vE = self.sb(st, [128, NT, 2, 65], BF16, "vE")
            vO = self.sb(st, [128, NT - 1, 2, 65], BF16, "vO")
            qT_t, kT_t, vE_t, vO_t = Tile("qT"), Tile("kT"), Tile("vE"), Tile("vO")
            S.op("pool", lambda e: e.memset(vE[:, :, :, 64:65], 1.0), w=[vE_t])
            S.op("pool", lambda e: e.memset(vO[:, :, :, 64:65], 1.0), w=[vO_t])
            NPT = 3
            pT = [self.sb(st, [128, 2, 4, 64], BF16, "pT") for _ in range(NPT)]
            pT_t = [Tile("pT") for _ in range(NPT)]
            ob = [self.sb(st, [64, 8, 128], BF16, "ob") for _ in range(2)]
            ob_t = [Tile("ob") for _ in range(2)]
            rz = self.sb(st, [64, 4], F32, "rz")
            rz_t = Tile("rz")
            scp = [self.ps(st, [128, 512], F32, "scp") for _ in range(4)]
            scp_t = [Tile("scp") for _ in range(4)]
            ops_ = [self.ps(st, [128, 4, 128], F32, "ops")[0:64, 0:2, :] for _ in range(2)]
            ops_t = [Tile("ops") for _ in range(2)]
            tps = [self.ps(st, [128, 1024], BF16, "tps")[:, 0:512] for _ in range(2)]
            tps_t = [Tile("tps") for _ in range(2)]
            with ExitStack() as st2:
                stage = [self.sb(st2, [128, 2, 1024], F32, "stage") for _ in range(1)]
                stage_t = [Tile("stage") for _ in range(1)]
                sds = [S.pds("stage") for _ in range(1)]
                for i in range(7):
                    d0 = 2 * i
                    b = 0
                    S.dma("sp", stage[b][0:64], rpbg[d0:d0 + 2].rearrange("d t h c -> t d (h c)"), sds[b], w=[stage_t[b]])
                    S.dma("sp", stage[b][64:128], rpbg[d0 + 1:d0 + 3].rearrange("d t h c -> t d (h c)"), sds[b], w=[stage_t[b]])
                    S.op("dve", lambda e: e.tensor_tensor(
                        out=M2[:, d0:d0 + 2, :, :].rearrange("p d h c -> p (d h) c"),
                        in0=stage[b].rearrange("p d (h c) -> p (d h) c", c=64),
                        in1=mask.unsqueeze(1).to_broadcast([128, 32, 64]), op=ALU.add),
                        r=[stage_t[b], mask_t], w=[M2_t])
                S.barrier()

            def issue(i):
                c, kind = divmod(i, 3)
                sl = i % NSLOT
                S.dma("sp", ring[sl], scr[kind * 8 + c], ring_ds[sl], r=[wt], w=[ring_t[sl]])

            for i in range(3):
                issue(i)
            rot = [0]

            def next_scp():
                i = rot[0] % 4
                rot[0] += 1
                return i

            ident = self.ident
            for c in range(NCH):
                if c + 1 < NCH:
                    for kind in range(3):
                        issue((c + 1) * 3 + kind)
                slq, slk, slv = (c * 3) % NSLOT, (c * 3 + 1) % NSLOT, (c * 3 + 2) % NSLOT
                for blk in range(4):
                    xts = self.xT_t[blk * 4:(blk + 1) * 4]
                    for (sl, dst, dst_t, sc) in ((slq, qT, qT_t, 0.125), (slk, kT, kT_t, 1.0)):
                        pi = next_scp()
                        for k in range(NCH):
                            S.op("pe", lambda e, k=k: e.matmul(scp[pi], ring[sl][:, k, :], self.xT[:, k, blk * 512:(blk + 1) * 512],
                                                             start=(k == 0), stop=(k == NCH - 1)),
                                 r=[ring_t[sl]] + xts, w=[scp_t[pi]])
                        S.op("act", lambda e: e.mul(out=dst[:, blk * 512:(blk + 1) * 512], in_=scp[pi], mul=sc),
                             r=[scp_t[pi]], w=[dst_t])
                    for (vbuf, vbuf_t, off, ntile) in ((vE, vE_t, 0, 4), (vO, vO_t, 64, 4 if blk < 3 else 3)):
                        pi = next_scp()
                        for tt in range(ntile):
                            t = blk * 4 + tt
                            tok0 = t * 128 + off
                            tl = sorted(set([tok0 // 128, (tok0 + 127) // 128]))
                            for k in range(NCH):
                                S.op("pe", lambda e, k=k: e.matmul(scp[pi][:, tt * 128:(tt + 1) * 128], self.xT[:, k, tok0:tok0 + 128],
                                                                 ring[slv][:, k, :], start=(k == 0), stop=(k == NCH - 1)),
                                     r=[ring_t[slv]] + [self.xT_t[i] for i in tl], w=[scp_t[pi]])
                        S.op("dve", lambda e: e.tensor_copy(
                            out=vbuf[:, blk * 4:blk * 4 + ntile, :, 0:64],
                            in_=scp[pi][:, 0:ntile * 128].rearrange("p (a h d) -> p a h d", a=ntile, h=2)),
                            r=[scp_t[pi]], w=[vbuf_t])
                for r in range(32):
                    rs = min(max(r - 4, 0), 24)
                    d0b = rs - r + 7
                    pj = r % NPT
                    for hh in range(2):
                        pi = next_scp()
                        scv = scp[pi][:, 0:256].rearrange("p (j c) -> p j c", j=4)
                        S.op("pe", lambda e: e.matmul(scv, ident, M2[:, d0b:d0b + 7:2, 2 * c + hh, :], start=True, stop=False,
                                                      skip_group_check=True),
                             r=[M2_t, self.t_const], w=[scp_t[pi]])
                        for j in range(4):
                            ks = (rs + 2 * j) * 64
                            S.op("pe", lambda e: e.matmul(scv[:, j, :], kT[hh * 64:(hh + 1) * 64, ks:ks + 128],
                                                          qT[hh * 64:(hh + 1) * 64, r * 64:(r + 1) * 64], start=False,
                                                          stop=(j == 3), skip_group_check=True),
                                 r=[kT_t, qT_t], w=[scp_t[pi]])
                        S.op("act", lambda e: e.activation(out=pT[pj][:, hh].rearrange("p j c -> p (j c)"), in_=scp[pi][:, 0:256],
                                                           func=AF.Exp),
                             r=[scp_t[pi]], w=[pT_t[pj]])
                    oi = r % 2
                    for j in range(4):
                        kr = rs + 2 * j
                        if kr % 2 == 0:
                            vb, vb_t, vt = vE, vE_t, kr // 2
                        else:
                            vb, vb_t, vt = vO, vO_t, (kr - 1) // 2
                        for hh in range(2):
                            S.op("pe", lambda e: e.matmul(ops_[oi][:, hh, 0:65], pT[pj][:, hh, j, :], vb[:, vt, hh, :],
                                                          start=(j == 0 and hh == 0), stop=(j == 3 and hh == 1), skip_group_check=True),
                                 r=[pT_t[pj], vb_t], w=[ops_t[oi]])
                    g8, r8 = divmod(r, 8)
                    bi = g8 % 2
                    S.op("dve", lambda e: e.reciprocal(out=rz[:, 0:2], in_=ops_[oi][:, :, 64]), r=[ops_t[oi]], w=[rz_t])
                    S.op("dve", lambda e: e.tensor_tensor(out=ob[bi][:, r8, :].rearrange("p (h d) -> p h d", h=2),
                                                          in0=ops_[oi][:, :, 0:64],
                                                          in1=rz[:, 0:2].unsqueeze(2).to_broadcast([64, 2, 64]), op=ALU.mult),
                         r=[ops_t[oi], rz_t], w=[ob_t[bi]])
                    S.op("pe", lambda e: e.transpose(out=tps[bi][:, r8 * 64:(r8 + 1) * 64], in_=ob[bi][:, r8, :],
                                                     identity=ident[0:64, 0:64]),
                         r=[ob_t[bi], self.t_const], w=[tps_t[bi]])
                    if r8 == 7:
                        S.op("act", lambda e: e.copy(out=self.oT[:, c, g8 * 512:(g8 + 1) * 512], in_=tps[bi]),
                             r=[tps_t[bi]], w=self.oT_t[g8 * 4:(g8 + 1) * 4])
            S.barrier()
          self.out_proj_ln("na_out", 0, 2)

    def layer_mla(self, s):
        S, I = self.S, self.I
        ident = self.ident
        sm_scale = 96.0 ** -0.5
        self.oT, self.oT_t = self.xT, self.xT_t
        with ExitStack() as st:
            cds = S.pds("mc")
            cnT = self.sb(st, [128, 3, S_LEN], BF16, "cnT")
            cqnT = cnT[:, 0:2, :]
            ckvnT = cnT[:, 2, :]
            kropeT = self.sb(st, [128, S_LEN], BF16, "kropeT")
            cq_t, ckv_t, krt_t = Tile("cqnT"), Tile("ckvnT"), Tile("kropeT")
            wuq = self.sb(st, [128, 2, 1568], BF16, "wuq")
            wq2 = self.sb(st, [128, 2, 16, 128], BF16, "wq2")
            wukv = self.sb(st, [128, 2048], BF16, "wukv")
            w_t = Tile("mlaw")
            wds = S.pds("mw")
            S.op("pool", lambda e: e.memset(wuq, 0.0), w=[w_t])
            S.dma("sp", wuq[:, :, 0:1536], self.W["mla_uq"][0], wds, r=[self.W["mla_uq"][1]], w=[w_t])
            S.dma("sp", wukv, self.W["mla_ukv"][0][:, 0, :], wds, r=[self.W["mla_ukv"][1]], w=[w_t])
            wuqv = wuq[:, :, 0:1536].rearrange("p k (h d) -> p k h d", h=16)
            S.op("pool", lambda e: e.memset(wq2, 0.0), w=[w_t])
            S.op("dve", lambda e: e.tensor_scalar(out=wq2[:, :, :, 64:80], in0=wuqv[:, :, :, 80:96], scalar1=-1.0, scalar2=None,
                                                  op0=ALU.mult), r=[w_t], w=[w_t])
            S.op("dve", lambda e: e.tensor_copy(out=wq2[:, :, :, 80:96], in_=wuqv[:, :, :, 64:80]), r=[w_t], w=[w_t])
            cosf = self.sb(st, [128, S_LEN], F32, "cosf")
            sinf = self.sb(st, [128, S_LEN], F32, "sinf")
            rp_t = Tile("ropef")
            S.dma("sp", cosf, I["rope_cos_fm"], cds, w=[rp_t])
            S.dma("sp", sinf, I["rope_sin_fm"], cds, w=[rp_t])
            cds_tiles = [rp_t]
            scp = [self.ps(st, [128, 512], F32, "scp") for _ in range(4)]
            scp_t = [Tile("scp") for _ in range(4)]
            rot = [0]

            def next_scp():
                i = rot[0] % 4
                rot[0] += 1
                return i

            with ExitStack() as st2:
                wa = self.sb(st2, [128, NCH, 416], BF16, "wa")
                wa_t = Tile("wa")
                S.dma("sp", wa, self.W["mla_a"][0], cds, r=[self.W["mla_a"][1]], w=[wa_t])
                gq = self.sb(st2, [128, 256], F32, "gq")
                gkv = self.sb(st2, [128, 128], F32, "gkv")
                g_t = Tile("g")
                S.dma("sp", gq, I["mla_g_q"][0].partition_broadcast(128), cds, w=[g_t])
                S.dma("sp", gkv, I["mla_g_kv"][0].partition_broadcast(128), cds, w=[g_t])
                cos_tm = self.sb(st2, [128, NT, 16], F32, "cos_tm")
                sin_tm = self.sb(st2, [128, NT, 16], F32, "sin_tm")
                S.dma("sp", cos_tm, I["rope_cos_tm"], cds, w=[g_t])
                S.dma("sp", sin_tm, I["rope_sin_tm"], cds, w=[g_t])
                S.group_done(cds, [rp_t, wa_t, g_t])
                KRraw = self.sb(st2, [128, NT, 32], F32, "KRraw")
                KR = self.sb(st2, [128, NT, 128], BF16, "KR")
                kr_t = Tile("KR")
                S.op("pool", lambda e: e.memset(KR, 0.0), w=[kr_t])
                junk = self.sb(st2, [128, 256], F32, "junk")
                junk2 = self.sb(st2, [128, 128], F32, "junk2")
                ssq = [self.sb(st2, [128, 8], F32, "ssq") for _ in range(2)]
                ssq_t = [Tile("ssq") for _ in range(2)]
                nb = [self.sb(st2, [128, 384], BF16, "nb") for _ in range(2)]
                nb_t = [Tile("nb") for _ in range(2)]
                tpa = [self.ps(st2, [128, 8, 128], BF16, "tpa") for _ in range(2)]
                tpa_t = [Tile("tpa") for _ in range(2)]
                ADBG = int(os.environ.get("MLA_A", "9"))
                for t in range(NT if ADBG >= 2 else 0):
                    b = t % 2
                    pi = next_scp()
                    aps = scp[pi]
                    for k in range(NCH):
                        S.op("pe", lambda e, k=k: e.matmul(aps[:, 0:416], self.xT[:, k, t * 128:(t + 1) * 128], wa[:, k, :],
                                                         start=(k == 0), stop=(k == NCH - 1)),
                             r=[self.xT_t[t], wa_t], w=[scp_t[pi]])
                    LDBG = int(os.environ.get("MLA_L", "9"))
                    q = ssq[b]
                    if LDBG < 2:
                        continue
                    S.op("act", lambda e: e.activation(out=junk[:, 0:256], in_=aps[:, 0:256], func=AF.Square, accum_out=q[:, 0:1]),
                         r=[scp_t[pi]], w=[ssq_t[b]])
                    S.op("act", lambda e: e.activation(out=junk2, in_=aps[:, 256:384], func=AF.Square, accum_out=q[:, 1:2]),
                         r=[scp_t[pi]], w=[ssq_t[b]])
                    if LDBG < 3:
                        continue
                    S.op("dve", lambda e: e.tensor_scalar(out=q[:, 2:3], in0=q[:, 0:1], scalar1=1.0 / 256.0, scalar2=RMS_EPS,
                                                          op0=ALU.mult, op1=ALU.add), r=[ssq_t[b]], w=[ssq_t[b]])
                    S.op("dve", lambda e: e.tensor_scalar(out=q[:, 3:4], in0=q[:, 1:2], scalar1=1.0 / 128.0, scalar2=RMS_EPS,
                                                          op0=ALU.mult, op1=ALU.add), r=[ssq_t[b]], w=[ssq_t[b]])
                    S.op("act", lambda e: e.activation(out=q[:, 4:6], in_=q[:, 2:4], func=AF.Ln), r=[ssq_t[b]], w=[ssq_t[b]])
                    S.op("act", lambda e: e.activation(out=q[:, 6:8], in_=q[:, 4:6], func=AF.Exp, scale=-0.5),
                         r=[ssq_t[b]], w=[ssq_t[b]])
                    if LDBG < 4:
                        continue
                    S.op("dve", lambda e: e.scalar_tensor_tensor(out=nb[b][:, 0:256], in0=aps[:, 0:256], scalar=q[:, 6:7], in1=gq,
                                                                 op0=ALU.mult, op1=ALU.mult),
                         r=[scp_t[pi], ssq_t[b], g_t], w=[nb_t[b]])
                    S.op("dve", lambda e: e.scalar_tensor_tensor(out=nb[b][:, 256:384], in0=aps[:, 256:384], scalar=q[:, 7:8], in1=gkv,
                                                                 op0=ALU.mult, op1=ALU.mult),
                         r=[scp_t[pi], ssq_t[b], g_t], w=[nb_t[b]])
                    S.op("act", lambda e: e.copy(out=KRraw[:, t, :], in_=aps[:, 384:416]), r=[scp_t[pi]], w=[kr_t])
                    if LDBG < 5:
                        continue
                    for i in range(3):
                        S.op("pe", lambda e, i=i: e.transpose(out=tpa[b][:, i, :], in_=nb[b][:, i * 128:(i + 1) * 128], identity=ident),
                             r=[nb_t[b], self.t_const], w=[tpa_t[b]])
                    S.op("act", lambda e: e.copy(out=cnT[:, :, t * 128:(t + 1) * 128], in_=tpa[b][:, 0:3, :]),
                         r=[tpa_t[b]], w=[cq_t, ckv_t])
                ra = self.sb(st2, [128, NT, 16], F32, "ra")
                rb = self.sb(st2, [128, NT, 16], F32, "rb")
                t1, t2 = KRraw[:, :, 0:16], KRraw[:, :, 16:32]
                if ADBG < 3:
                    S.barrier()
                    raise_skip = True
                else:
                    raise_skip = False
                if not raise_skip:
                    S.op("dve", lambda e: e.tensor_tensor(out=ra, in0=t1, in1=cos_tm, op=ALU.mult), r=[kr_t, g_t], w=[kr_t])
                    S.op("dve", lambda e: e.tensor_tensor(out=rb, in0=t2, in1=sin_tm, op=ALU.mult), r=[kr_t, g_t], w=[kr_t])
                    S.op("dve", lambda e: e.tensor_tensor(out=KR[:, :, 64:80], in0=ra, in1=rb, op=ALU.subtract), r=[kr_t], w=[kr_t])
                    S.op("dve", lambda e: e.tensor_tensor(out=ra, in0=t1, in1=sin_tm, op=ALU.mult), r=[kr_t, g_t], w=[kr_t])
                    S.op("dve", lambda e: e.tensor_tensor(out=rb, in0=t2, in1=cos_tm, op=ALU.mult), r=[kr_t, g_t], w=[kr_t])
                    S.op("dve", lambda e: e.tensor_tensor(out=KR[:, :, 80:96], in0=ra, in1=rb, op=ALU.add), r=[kr_t], w=[kr_t])
                    for g4 in range(4):
                        b = g4 % 2
                        for i in range(4):
                            t = g4 * 4 + i
                            S.op("pe", lambda e, i=i: e.transpose(out=tpa[b][:, i, :], in_=KR[:, t, :], identity=ident),
                                 r=[kr_t, self.t_const], w=[tpa_t[b]])
                        S.op("act", lambda e: e.copy(out=kropeT[64:96, g4 * 512:(g4 + 1) * 512],
                                                     in_=tpa[b][64:96, 0:4, :].rearrange("p a b -> p (a b)")),
                             r=[tpa_t[b]], w=[krt_t])
                S.barrier()

            qT = [self.sb(st, [128, S_LEN], BF16, "qT") for _ in range(2)]
            kT = [self.sb(st, [128, S_LEN], BF16, "kT") for _ in range(2)]
            vA = [self.sb(st, [128, NT, 65], BF16, "vA") for _ in range(2)]
            qT_t = [Tile("qT") for _ in range(2)]
            kT_t = [Tile("kT") for _ in range(2)]
            vA_t = [Tile("vA") for _ in range(2)]
            for b in range(2):
                S.op("pool", lambda e, b=b: e.memset(vA[b][:, :, 64:65], 1.0), w=[vA_t[b]])
                S.op("pool", lambda e, b=b: e.memset(qT[b], 0.0), w=[qT_t[b]])
                S.op("pool", lambda e, b=b: e.memset(kT[b], 0.0), w=[kT_t[b]])
            tm1 = [self.sb(st, [128, 512], F32, "tm1") for _ in range(2)]
            tm2 = [self.sb(st, [128, 512], F32, "tm2") for _ in range(2)]
            tm_t = [Tile("tm") for _ in range(2)]
            NPT = 4
            pT = [self.sb(st, [128, 512], BF16, "pT") for _ in range(NPT)]
            pT_t = [Tile("pT") for _ in range(NPT)]
            acc = [self.ps(st, [128, 4, 128], F32, "acc") for _ in range(2)]
            acc_t = [Tile("acc") for _ in range(2)]
            tps = [self.ps(st, [128, 1024], BF16, "tps")[:, 0:512] for _ in range(2)]
            tps_t = [Tile("tps") for _ in range(2)]
            obp = self.sb(st, [128, NT, 128], BF16, "obp")
            obp_t = [Tile("obp") for _ in range(4)]
            rz = self.sb(st, [128, 4], F32, "rz")
            rz_t = Tile("rz")
            cnt = 0
            tmk = 0
            qbk = 0
            DBG = int(os.environ.get("MLA_DBG", "9"))
            for h in range(16 if DBG >= 2 else 0):
                hb = h % 2
                c = h // 2
                for blk in range(4):
                    cols = slice(blk * 512, (blk + 1) * 512)
                    pa, pb = next_scp(), next_scp()
                    for k in range(2):
                        S.op("pe", lambda e, k=k: e.matmul(scp[pa], wuq[:, k, h * 96:h * 96 + 128], cqnT[:, k, cols],
                                                         start=(k == 0), stop=(k == 1)), r=[w_t, cq_t], w=[scp_t[pa]])
                    for k in range(2):
                        S.op("pe", lambda e, k=k: e.matmul(scp[pb], wq2[:, k, h, :], cqnT[:, k, cols],
                                                         start=(k == 0), stop=(k == 1)), r=[w_t, cq_t], w=[scp_t[pb]])
                    S.op("dve", lambda e: e.tensor_copy(out=qT[hb][0:64, cols], in_=scp[pa][0:64]), r=[scp_t[pa]], w=[qT_t[hb]])
                    tb = tmk % 2
                    tmk += 1
                    S.op("dve", lambda e: e.tensor_tensor(out=tm1[tb][64:96], in0=scp[pa][64:96], in1=cosf[64:96, cols], op=ALU.mult),
                         r=[scp_t[pa], rp_t], w=[tm_t[tb]])
                    S.op("dve", lambda e: e.tensor_tensor(out=tm2[tb][64:96], in0=scp[pb][64:96], in1=sinf[64:96, cols], op=ALU.mult),
                         r=[scp_t[pb], rp_t], w=[tm_t[tb]])
                    S.op("pool", lambda e: e.tensor_tensor(out=qT[hb][64:96, cols], in0=tm1[tb][64:96], in1=tm2[tb][64:96], op=ALU.add),
                         r=[tm_t[tb]], w=[qT_t[hb]])
                    pk = next_scp()
                    S.op("pe", lambda e: e.matmul(scp[pk][0:64], wukv[:, h * 128:h * 128 + 64], ckvnT[:, cols], start=True, stop=True),
                         r=[w_t, ckv_t], w=[scp_t[pk]])
                    S.op("act", lambda e: e.copy(out=kT[hb][0:64, cols], in_=scp[pk][0:64]), r=[scp_t[pk]], w=[kT_t[hb]])
                S.op("pool", lambda e: e.tensor_copy(out=kT[hb][64:96, :], in_=kropeT[64:96, :]), r=[krt_t], w=[kT_t[hb]])
                for half in range(2):
                    pv = next_scp()
                    for i in range(8):
                        t = half * 8 + i
                        S.op("pe", lambda e, i=i: e.matmul(scp[pv][:, i * 64:(i + 1) * 64], ckvnT[:, t * 128:(t + 1) * 128],
                                                         wukv[:, h * 128 + 64:h * 128 + 128], start=True, stop=True),
                             r=[w_t, ckv_t], w=[scp_t[pv]])
                    S.op("dve", lambda e: e.tensor_copy(out=vA[hb][:, half * 8:(half + 1) * 8, 0:64],
                                                        in_=scp[pv].rearrange("p (a d) -> p a d", a=8)),
                         r=[scp_t[pv]], w=[vA_t[hb]])
                steps = [(qb, kc) for qb in range(4) for kc in range(16)] if DBG >= 3 else []
                pend = None
                for i in range(len(steps) + 1):
                    cur = None
                    if i < len(steps):
                        qb, kc = steps[i]
                        pi = next_scp()
                        S.op("pe", lambda e: e.matmul(scp[pi], kT[hb][:, kc * 128:(kc + 1) * 128],
                                                      qT[hb][:, qb * 512:(qb + 1) * 512], start=True, stop=True),
                             r=[kT_t[hb], qT_t[hb]], w=[scp_t[pi]])
                        pj = cnt % NPT
                        cnt += 1
                        S.op("act", lambda e: e.activation(out=pT[pj], in_=scp[pi], func=AF.Exp, scale=sm_scale),
                             r=[scp_t[pi]], w=[pT_t[pj]])
                        cur = (qb, kc, pj)
                    if pend is not None and DBG >= 4:
                        pqb, pkc, pj = pend
                        ab = (qbk + pqb) % 2
                        for qt in range(4):
                            S.op("pe", lambda e: e.matmul(acc[ab][:, qt, 0:65], pT[pj][:, qt * 128:(qt + 1) * 128], vA[hb][:, pkc, :],
                                                          start=(pkc == 0 and qt == 0), stop=(pkc == 15 and qt == 3),
                                                          skip_group_check=True),
                                 r=[pT_t[pj], vA_t[hb]], w=[acc_t[ab]])
                        if pkc == 15:
                            S.op("dve", lambda e: e.reciprocal(out=rz, in_=acc[ab][:, :, 64]), r=[acc_t[ab]], w=[rz_t])
                            S.op("dve", lambda e: e.tensor_tensor(
                                out=obp[:, pqb * 4:(pqb + 1) * 4, hb * 64:(hb + 1) * 64], in0=acc[ab][:, :, 0:64],
                                in1=rz.unsqueeze(2).to_broadcast([128, 4, 64]), op=ALU.mult),
                                r=[acc_t[ab], rz_t], w=[obp_t[pqb]])
                            if hb == 1:
                                tb2 = pqb % 2
                                for qt in range(4):
                                    tq = pqb * 4 + qt
                                    S.op("pe", lambda e: e.transpose(out=tps[tb2][:, qt * 128:(qt + 1) * 128], in_=obp[:, tq, :],
                                                                     identity=ident),
                                         r=[obp_t[pqb], self.t_const], w=[tps_t[tb2]])
                                S.op("act", lambda e: e.copy(out=self.oT[:, c, pqb * 512:(pqb + 1) * 512], in_=tps[tb2]),
                                     r=[tps_t[tb2]], w=self.oT_t[pqb * 4:(pqb + 1) * 4])
                    pend = cur
                qbk += 4
            S.barrier()
        self.out_proj_ln("mla_out", 0, 3)


_CONSTS = None


def _run(prog, inputs, x_shards):
    global _CONSTS
    if _CONSTS is None:
        _CONSTS = _consts()
    col = np.arange(64)
    dc = np.clip(col[:, None] - col[None, :], -15, 15) + 15
    rpb = np.asarray(inputs["na_rpb"], dtype=np.float32)[0]
    rpbg = np.ascontiguousarray(np.transpose(rpb[:, :, dc], (1, 2, 0, 3)))
    in_maps = []
    for xs in x_shards:
        m = {"x": np.ascontiguousarray(xs, dtype=np.float32)}
        for k in INPUT_SHAPES:
            m[k] = np.ascontiguousarray(inputs[k], dtype=np.float32)
        for k in CONST_SPECS:
            m["c_" + k] = _CONSTS[k]
        m["na_rpbg"] = rpbg
        in_maps.append(m)
    res = run_bass_kernel_spmd(prog.nc, in_maps, core_ids=list(range(len(x_shards))))
    return [np.asarray(r["out"]) for r in res.results]


def kernel(**inputs):
    x = np.asarray(inputs["x"], dtype=np.float32)
    prog = Prog()
    shards = [x[i * SEQ_PER_CORE:(i + 1) * SEQ_PER_CORE] for i in range(NCORES)]
    outs = _run(prog, inputs, shards)
    return np.concatenate(outs, axis=0).astype(np.float32)
```

```python
import math
import os
from contextlib import ExitStack
import numpy as np
import ml_dtypes
import concourse.bass as bass
import concourse.mybir as mybir
from concourse.bass_utils import run_bass_kernel_spmd

F32 = mybir.dt.float32
BF16 = mybir.dt.bfloat16
AF = mybir.ActivationFunctionType
ALU = mybir.AluOpType

D = 1024
S_LEN = 2048
NT = 16
NCH = 8
DFF = 2816
NJ = 22
ALPHA = 8.0 ** 0.25
LN_EPS = 1e-5
RMS_EPS = 1e-6
NCORES = 8
SEQ_PER_CORE = 2
NEG = -30000.0


class Tile:
    __slots__ = ("name", "writer", "readers")

    def __init__(self, name):
        self.name = name
        self.writer = None
        self.readers = {}


class DSem:
    __slots__ = ("key", "total")

    def __init__(self, key):
        self.key = key
        self.total = 0


class Sched:
    LIMIT = 24000

    def __init__(self, nc):
        self.nc = nc
        self.E = dict(pe=nc.tensor, act=nc.scalar, dve=nc.vector, pool=nc.gpsimd, sp=nc.sync)
        self.sems = {}
        self.nsem = 0
        self.cur = {}
        self.waited = {e: {} for e in self.E}
        self.dsems = []
        self.nwaits = 0
        self.nops = 0
        for e in ("pe", "act", "dve", "pool"):
            self._new_eng_sem(e)

    def _alloc(self, name):
        k = self.nsem
        self.nsem += 1
        self.sems[k] = self.nc.alloc_semaphore(f"s{k}_{name}")
        return k

    def _new_eng_sem(self, e):
        self.cur[e] = [self._alloc(e), 0]

    def dsem(self, name="d"):
        d = DSem(self._alloc(name))
        self.dsems.append(d)
        return d

    def pool_reset(self):
        self.pool_i = 0

    def pds(self, name="p"):
        if not hasattr(self, "pool"):
            self.pool, self.pool_i = [], 0
        if self.pool_i >= len(self.pool):
            self.pool.append(self.dsem(name))
        d = self.pool[self.pool_i]
        self.pool_i += 1
        return d

    def _wait(self, eng, tok):
        key, val = tok[0], tok[1]
        if self.waited[eng].get(key, 0) >= val:
            return
        self.E[eng].wait_ge(self.sems[key], val)
        self.waited[eng][key] = val
        self.nwaits += 1

    def _deps(self, eng, r, w, is_dma):
        deps = []
        for t in r:
            wr = t.writer
            if wr is not None:
                if wr[2] == eng and not is_dma and eng == "pe":
                    continue
                deps.append(wr)
        for t in w:
            wr = t.writer
            if wr is not None and (is_dma or wr[2] != eng):
                deps.append(wr)
            for tok in t.readers.values():
                if is_dma or tok[2] != eng:
                    deps.append(tok)
        return deps

    def _commit(self, tok, r, w):
        for t in r:
            t.readers[tok[0]] = tok
        for t in w:
            t.writer = tok
            t.readers = {}

    def op(self, eng, fn, r=(), w=()):
        for tok in self._deps(eng, r, w, False):
            self._wait(eng, tok)
        cur = self.cur[eng]
        if cur[1] >= self.LIMIT:
            self._new_eng_sem(eng)
            cur = self.cur[eng]
        ins = fn(self.E[eng])
        cur[1] += 1
        ins.then_inc(self.sems[cur[0]], 1)
        tok = (cur[0], cur[1], eng)
        self._commit(tok, r, w)
        self.nops += 1
        return tok

    def dma(self, q, out, in_, ds, r=(), w=(), **kw):
        for tok in self._deps(q, r, w, True):
            self._wait(q, tok)
        if ds.total + 16 > self.LIMIT:
            ds.key = self._alloc("d")
            ds.total = 0
        ins = self.E[q].dma_start(out=out, in_=in_, **kw)
        ds.total += 16
        ins.then_inc(self.sems[ds.key], 16)
        tok = (ds.key, ds.total, "dma")
        self._commit(tok, r, w)
        self.nops += 1
        return tok

    def group_done(self, ds, tiles):
        tok = (ds.key, ds.total, "dma")
        for t in tiles:
            t.writer = tok

    def barrier(self):
        toks = [(c[0], c[1], e) for e, c in self.cur.items() if c[1] > 0]
        toks += [(d.key, d.total, "dma") for d in self.dsems if d.total > 0]
        for e in self.E:
            for tok in toks:
                if tok[2] == e and e == "pe":
                    continue
                self._wait(e, tok)
        self.pool_reset()


def _consts():
    c = {}
    c["ident"] = np.eye(128, dtype=np.float32).astype(ml_dtypes.bfloat16)
    p = np.arange(128)[:, None]
    cc = np.arange(3968)[None, :]
    c["alibi"] = np.abs(p - cc + 1920).astype(np.float32)
    inv_freq = (1.0 / (10000.0 ** (np.arange(0, 32, 2, dtype=np.float32) / np.float32(32)))).astype(np.float32)
    ang = (np.arange(S_LEN, dtype=np.float32)[:, None] * inv_freq[None, :]).astype(np.float32)
    cos, sin = np.cos(ang).astype(np.float32), np.sin(ang).astype(np.float32)
    c["rope_cos_tm"] = np.ascontiguousarray(cos.reshape(NT, 128, 16).transpose(1, 0, 2))
    c["rope_sin_tm"] = np.ascontiguousarray(sin.reshape(NT, 128, 16).transpose(1, 0, 2))
    cf = np.zeros((128, S_LEN), np.float32)
    sf = np.zeros((128, S_LEN), np.float32)
    for i in range(32):
        cf[64 + i] = cos[:, i % 16]
        sf[64 + i] = sin[:, i % 16]
    c["rope_cos_fm"] = cf
    c["rope_sin_fm"] = sf
    col = np.arange(64)
    cs = np.clip(col - 8, 0, 48)
    valid = (col[None, :] >= cs[:, None]) & (col[None, :] < cs[:, None] + 16)
    madd = np.where(valid.T, 0.0, NEG).astype(np.float32)
    c["na_mask"] = np.concatenate([madd, madd], axis=0)
    return c


CONST_SPECS = {
    "ident": ([128, 128], BF16),
    "alibi": ([128, 3968], F32),
    "rope_cos_tm": ([128, NT, 16], F32),
    "rope_sin_tm": ([128, NT, 16], F32),
    "rope_cos_fm": ([128, S_LEN], F32),
    "rope_sin_fm": ([128, S_LEN], F32),
    "na_mask": ([128, 64], F32),
}

INPUT_SHAPES = {
    "conv_w_in": [1, 1024, 3072], "conv_w": [1, 3, 1024], "conv_w_out": [1, 1024, 1024],
    "diff_w_qkv": [1, 1024, 3072], "diff_lambda": [1, 4, 64], "diff_subln_g": [1, 128], "diff_w_out": [1, 1024, 1024],
    "na_w_qkv": [1, 1024, 3072], "na_rpb": [1, 16, 15, 31], "na_w_out": [1, 1024, 1024],
    "mla_w_a": [1, 1024, 416], "mla_g_q": [1, 256], "mla_g_kv": [1, 128], "mla_w_uq": [1, 256, 1536],
    "mla_w_ukv": [1, 128, 2048], "mla_w_out": [1, 1024, 1024],
    "ln1_g": [4, 1024], "ln1_b": [4, 1024], "ffn_w_gu": [4, 1024, 5632], "ffn_w_down": [4, 2816, 1024],
    "ln2_g": [4, 1024], "ln2_b": [4, 1024],
}
DERIVED_SHAPES = {"na_rpbg": [15, 64, 16, 64]}


class Prog:
    def __init__(self, layers=(0, 1, 2, 3), nseq=SEQ_PER_CORE, do_ffn=True):
        self.layers = tuple(layers)
        self.nseq = nseq
        self.do_ffn = do_ffn
        nc = self.nc = bass.Bass("TRN2", target_bir_lowering=False)
        self.S = Sched(nc)
        self.I = {}
        self.I["x"] = nc.dram_tensor("x", [nseq, S_LEN, D], F32, kind="ExternalInput").ap()
        for k, shp in INPUT_SHAPES.items():
            self.I[k] = nc.dram_tensor(k, shp, F32, kind="ExternalInput").ap()
        for k, (shp, dt) in CONST_SPECS.items():
            self.I[k] = nc.dram_tensor("c_" + k, shp, dt, kind="ExternalInput").ap()
        for k, shp in DERIVED_SHAPES.items():
            self.I[k] = nc.dram_tensor(k, shp, F32, kind="ExternalInput").ap()
        self.out = nc.dram_tensor("out", [nseq, S_LEN, D], F32, kind="ExternalOutput").ap()
        self.uid = 0
        self.build()

    def name(self, p):
        self.uid += 1
        return f"{p}{self.uid}"

    def sb(self, st, shape, dt, name="sb"):
        return st.enter_context(self.nc.sbuf_tensor(self.name(name), shape, dt)).ap()

    def ps(self, st, shape, dt, name="ps"):
        return st.enter_context(self.nc.psum_tensor(self.name(name), shape, dt)).ap()

    def scratch(self, shape, dt=BF16, name="scr"):
        return self.nc.dram_tensor(self.name(name), shape, dt, kind="Internal").ap()

    def conv_lhsT(self, w2d, K, N, name):
        S = self.S
        kc, nj = K // 128, N // 128
        scr = self.scratch([nj, 128, kc, 128], name=name)
        ds = S.dsem(name)
        t = Tile(name)
        for j in range(nj):
            src = w2d[:, j * 128:(j + 1) * 128].rearrange("(kc p) n -> p kc n", p=128)
            S.dma("pool", scr[j], src, ds, w=[t])
        t.writer = (ds.key, ds.total, "dma")
        return scr, t

    def conv_rhs(self, w2d, K, N, name):
        S = self.S
        kc = K // 128
        scr = self.scratch([128, kc, N], name=name)
        ds = S.dsem(name)
        t = Tile(name)
        for k in range(kc):
            S.dma("pool", scr[:, k, :], w2d[k * 128:(k + 1) * 128, :], ds, w=[t])
        t.writer = (ds.key, ds.total, "dma")
        return scr, t

    def build(self):
        nc, S, I = self.nc, self.S, self.I
        with ExitStack() as gst:
            cds = S.dsem("const")
            self.t_const = Tile("const")
            self.ident = self.sb(gst, [128, 128], BF16, "ident")
            S.dma("sp", self.ident, I["ident"], cds, w=[self.t_const])
            self.W = {}
            for i, L in enumerate(self.layers):
                self.convert_layer(L)
            self.x = self.sb(gst, [128, NT, D], F32, "x")
            self.xT = self.sb(gst, [128, NCH, S_LEN], BF16, "xT")
            self.x_t = [Tile(f"x{t}") for t in range(NT)]
            self.xT_t = [Tile(f"xT{t}") for t in range(NT)]
            self.oT_t = [Tile(f"oT{t}") for t in range(NT)]
            self.out_ds = [S.dsem("out") for _ in range(4)]
            self.xin_ds = [S.dsem("xin") for _ in range(4)]
            for s in range(self.nseq):
                self.load_x(s)
                for L in self.layers:
                    [self.layer_conv, self.layer_diff, self.layer_na, self.layer_mla][L](s)
                    if self.do_ffn:
                        self.ffn(L, s, last=(L == self.layers[-1]))
                if not self.do_ffn:
                    self.store_x(s)
                S.barrier()
            S.barrier()

    def convert_layer(self, L):
        I, W = self.I, self.W
        if L == 0:
            W["conv_in"] = self.conv_lhsT(I["conv_w_in"][0], 1024, 3072, "cwin")
            W["conv_out"] = self.conv_rhs(I["conv_w_out"][0], 1024, 1024, "cwout")
        elif L == 1:
            W["diff_qkv"] = self.conv_lhsT(I["diff_w_qkv"][0], 1024, 3072, "dqkv")
            W["diff_out"] = self.conv_rhs(I["diff_w_out"][0], 1024, 1024, "dwout")
        elif L == 2:
            W["na_qkv"] = self.conv_lhsT(I["na_w_qkv"][0], 1024, 3072, "nqkv")
            W["na_out"] = self.conv_rhs(I["na_w_out"][0], 1024, 1024, "nwout")
        elif L == 3:
            W["mla_a"] = self.conv_rhs(I["mla_w_a"][0], 1024, 416, "mwa")
            W["mla_uq"] = self.conv_rhs(I["mla_w_uq"][0], 256, 1536, "muq")
            W["mla_ukv"] = self.conv_rhs(I["mla_w_ukv"][0], 128, 2048, "mukv")
            W["mla_out"] = self.conv_rhs(I["mla_w_out"][0], 1024, 1024, "mwout")
        if self.do_ffn:
            S = self.S
            scr = self.scratch([NJ, 128, NCH, 256], name=f"wgu{L}")
            ds = S.dsem("wgu")
            t = Tile("wgu")
            w = I["ffn_w_gu"][L]
            for j in range(NJ):
                for half in range(2):
                    src = w[:, half * DFF + j * 128: half * DFF + (j + 1) * 128].rearrange("(kc p) n -> p kc n", p=128)
                    S.dma("pool", scr[j, :, :, half * 128:(half + 1) * 128], src, ds, w=[t])
            t.writer = (ds.key, ds.total, "dma")
            W[f"gu{L}"] = (scr, t)
            W[f"down{L}"] = self.conv_rhs(I["ffn_w_down"][L], DFF, 1024, f"wdn{L}")

    def load_lnp(self, st, L, which):
        S, I = self.S, self.I
        self.lnp = self.sb(st, [128, 2, D], F32, "lnp")
        self.t_lnp = Tile("lnp")
        ds = S.pds("lnp")
        for i, k in enumerate([f"ln{which}_g", f"ln{which}_b"]):
            S.dma("sp", self.lnp[:, i, :], I[k][L].partition_broadcast(128), ds, w=[self.t_lnp])

    def load_x(self, s):
        S, I = self.S, self.I
        xs = I["x"][s].rearrange("(t p) d -> p t d", p=128)
        for t in range(NT):
            S.dma("sp", self.x[:, t, :], xs[:, t, :], self.xin_ds[0], w=[self.x_t[t]])
        S.group_done(self.xin_ds[0], self.x_t)
        with ExitStack() as st:
            xb = [self.sb(st, [128, D], BF16, "xb") for _ in range(2)]
            xb_t = [Tile("xb") for _ in range(2)]
            tp = [self.ps(st, [128, NCH, 128], BF16, "tp") for _ in range(2)]
            tp_t = [Tile("tp") for _ in range(2)]
            for t in range(NT):
                b = t % 2
                self.to_featmajor(t, xb[b], xb_t[b], tp[b], tp_t[b], "act" if t % 2 else "dve")
            S.barrier()

    def to_featmajor(self, t, xb, xb_t, tp, tp_t, eng):
        S = self.S
        if eng == "act":
            S.op("act", lambda e: e.copy(out=xb, in_=self.x[:, t, :]), r=[self.x_t[t]], w=[xb_t])
        else:
            S.op("dve", lambda e: e.tensor_copy(out=xb, in_=self.x[:, t, :]), r=[self.x_t[t]], w=[xb_t])
        for c in range(NCH):
            S.op("pe", lambda e, c=c: e.transpose(out=tp[:, c, :], in_=xb[:, c * 128:(c + 1) * 128], identity=self.ident),
                 r=[xb_t, self.t_const], w=[tp_t])
        dst = self.xT[:, :, t * 128:(t + 1) * 128]
        if eng == "act":
            S.op("dve", lambda e: e.tensor_copy(out=dst, in_=tp), r=[tp_t], w=[self.xT_t[t]])
        else:
            S.op("act", lambda e: e.copy(out=dst, in_=tp), r=[tp_t], w=[self.xT_t[t]])

    def store_x(self, s):
        S = self.S
        os_ = self.out[s].rearrange("(t p) d -> p t d", p=128)
        for t in range(NT):
            S.dma("sp", os_[:, t, :], self.x[:, t, :], self.out_ds[t % 4], r=[self.x_t[t]])

    def ln_epilogue(self, t, y_ps, y_t, gi, L, W, store=None):
        S = self.S
        k = W["k"]
        W["k"] += 1
        b = k % 3
        z, z_t = W["z"][b], W["z_t"][b]
        st6, st6_t = W["st"][b], W["st_t"][b]
        xt = self.x[:, t, :]
        S.op("dve", lambda e: e.scalar_tensor_tensor(out=z, in0=xt, scalar=ALPHA, in1=y_ps, op0=ALU.mult, op1=ALU.add),
             r=[self.x_t[t]] + y_t, w=[z_t])
        for h in range(2):
            S.op("dve", lambda e, h=h: e.bn_stats(out=st6[:, h * 6:(h + 1) * 6], in_=z[:, h * 512:(h + 1) * 512]),
                 r=[z_t], w=[st6_t])
        S.op("dve", lambda e: e.bn_aggr(out=st6[:, 12:14], in_=st6[:, 0:12]), r=[st6_t], w=[st6_t])
        S.op("dve", lambda e: e.tensor_scalar(out=st6[:, 13:14], in0=st6[:, 13:14], scalar1=LN_EPS, scalar2=None,
                                              op0=ALU.add), r=[st6_t], w=[st6_t])
        S.op("act", lambda e: e.activation(out=st6[:, 14:15], in_=st6[:, 13:14], func=AF.Ln), r=[st6_t], w=[st6_t])
        S.op("act", lambda e: e.activation(out=st6[:, 14:15], in_=st6[:, 14:15], func=AF.Exp, scale=-0.5),
             r=[st6_t], w=[st6_t])
        S.op("dve", lambda e: e.scalar_tensor_tensor(out=st6[:, 15:16], in0=st6[:, 12:13], scalar=-1.0, in1=st6[:, 14:15],
                                                     op0=ALU.mult, op1=ALU.mult), r=[st6_t], w=[st6_t])
        S.op("act", lambda e: e.activation(out=z, in_=z, func=AF.Identity, bias=st6[:, 15:16], scale=st6[:, 14:15]),
             r=[z_t, st6_t], w=[z_t])
        g = self.lnp[:, 0, :]
        bb = self.lnp[:, 1, :]
        S.op("pool", lambda e: e.tensor_tensor(out=z, in0=z, in1=g, op=ALU.mult), r=[z_t, self.t_lnp], w=[z_t])
        S.op("dve", lambda e: e.tensor_tensor(out=xt, in0=z, in1=bb, op=ALU.add), r=[z_t, self.t_lnp], w=[self.x_t[t]])
        if store is not None:
            s, dsl = store
            os_ = self.out[s].rearrange("(t p) d -> p t d", p=128)
            S.dma("sp", os_[:, t, :], xt, dsl[t % 4], r=[self.x_t[t]])
            return None
        xb, xb_t = W["xb"][b], W["xb_t"][b]
        S.op("act", lambda e: e.copy(out=xb, in_=xt), r=[self.x_t[t]], w=[xb_t])

        def part_b():
            kb = W["kb"]
            W["kb"] += 1
            tp, tp_t = W["tp"][kb % 2], W["tp_t"][kb % 2]
            for c in range(NCH):
                S.op("pe", lambda e, c=c: e.transpose(out=tp[:, c, :], in_=xb[:, c * 128:(c + 1) * 128], identity=self.ident),
                     r=[xb_t, self.t_const], w=[tp_t])
            dst = self.xT[:, :, t * 128:(t + 1) * 128]
            if kb % 2:
                S.op("dve", lambda e: e.tensor_copy(out=dst, in_=tp), r=[tp_t], w=[self.xT_t[t]])
            else:
                S.op("act", lambda e: e.copy(out=dst, in_=tp), r=[tp_t], w=[self.xT_t[t]])
        return part_b

    def ln_scratch(self, st):
        W = {"k": 0, "kb": 0, "pending": []}
        W["z"] = [self.sb(st, [128, D], F32, "z") for _ in range(3)]
        W["z_t"] = [Tile("z") for _ in range(3)]
        W["st"] = [self.sb(st, [128, 16], F32, "st") for _ in range(3)]
        W["st_t"] = [Tile("st") for _ in range(3)]
        W["xb"] = [self.sb(st, [128, D], BF16, "xb") for _ in range(3)]
        W["xb_t"] = [Tile("xb") for _ in range(3)]
        W["tp"] = [self.ps(st, [128, NCH, 128], BF16, "tp") for _ in range(2)]
        W["tp_t"] = [Tile("tp") for _ in range(2)]
        return W

    def ln_push(self, W, pb):
        if pb is not None:
            W["pending"].append(pb)
        while len(W["pending"]) > 2:
            W["pending"].pop(0)()

    def ln_pop(self, W, n=1):
        for _ in range(n):
            if W["pending"]:
                W["pending"].pop(0)()

    def out_proj_ln(self, wkey, gi, L):
        S = self.S
        scr, wt = self.W[wkey]
        with ExitStack() as st:
            self.load_lnp(st, L, 1)
            w_sb = self.sb(st, [128, NCH, D], BF16, "wout")
            w_t = Tile("wout")
            ds = S.pds("wout")
            for c in range(0, NCH, 2):
                S.dma("sp", w_sb[:, c:c + 2, :], scr[:, c:c + 2, :], ds, r=[wt], w=[w_t])
            LW = self.ln_scratch(st)
            yps = [self.ps(st, [128, D], F32, "y") for _ in range(2)]
            y_t = [[Tile("y0"), Tile("y1")] for _ in range(2)]
            for t in range(NT):
                b = t % 2
                for h in range(2):
                    for c in range(NCH):
                        S.op("pe", lambda e, c=c, h=h: e.matmul(yps[b][:, h * 512:(h + 1) * 512],
                                                              self.oT[:, c, t * 128:(t + 1) * 128],
                                                              w_sb[:, c, h * 512:(h + 1) * 512],
                                                              start=(c == 0), stop=(c == NCH - 1)),
                             r=[self.oT_t[t], w_t], w=[y_t[b][h]])
                self.ln_push(LW, self.ln_epilogue(t, yps[b], y_t[b], gi, L, LW))
            self.ln_pop(LW, 2)
            S.barrier()

    def ffn(self, L, s, last):
        S = self.S
        gscr, gt = self.W[f"gu{L}"]
        dscr, dt_ = self.W[f"down{L}"]
        with ExitStack() as st:
            self.load_lnp(st, L, 2)
            wd = self.sb(st, [128, NJ, D], BF16, "wd")
            wd_t = Tile("wd")
            ds = S.pds("wd")
            for j in range(0, NJ, 2):
                S.dma("sp", wd[:, j:j + 2, :], dscr[:, j:j + 2, :], ds, r=[dt_], w=[wd_t])
            NSLOT = 3
            ring = [self.sb(st, [128, NCH, 256], BF16, "wgu") for _ in range(NSLOT)]
            ring_t = [Tile("wgu") for _ in range(NSLOT)]
            ring_ds = [S.pds("wgu") for _ in range(NSLOT)]
            hT = self.sb(st, [128, NJ, 512], BF16, "hT")
            hT_t = [Tile("hT") for _ in range(4)]
            sg = [self.sb(st, [128, 512], F32, "sg") for _ in range(2)]
            sg_t = [Tile("sg") for _ in range(2)]
            gps = self.ps(st, [128, 512], F32, "g")
            ups = self.ps(st, [128, 512], F32, "u")
            g_t, u_t = Tile("g"), Tile("u")
            LW = self.ln_scratch(st)
            yps = [self.ps(st, [128, D], F32, "y") for _ in range(2)]
            y_t = [[Tile("y0"), Tile("y1")] for _ in range(2)]
            nload = 0
            total = 4 * NJ

            def issue(i):
                j = i % NJ
                sl = i % NSLOT
                S.dma("sp", ring[sl], gscr[j], ring_ds[sl], r=[gt], w=[ring_t[sl]])

            for i in range(min(NSLOT - 1, total)):
                issue(i)
                nload += 1
            it = 0
            for blk in range(4):
                xts = self.xT_t[blk * 4:(blk + 1) * 4]
                for j in range(NJ):
                    if nload < total:
                        issue(nload)
                        nload += 1
                    sl = it % NSLOT
                    it += 1
                    for c in range(NCH):
                        S.op("pe", lambda e, c=c: e.matmul(gps, ring[sl][:, c, 0:128], self.xT[:, c, blk * 512:(blk + 1) * 512],
                                                         start=(c == 0), stop=(c == NCH - 1)),
                             r=[ring_t[sl]] + xts, w=[g_t])
                    for c in range(NCH):
                        S.op("pe", lambda e, c=c: e.matmul(ups, ring[sl][:, c, 128:256], self.xT[:, c, blk * 512:(blk + 1) * 512],
                                                         start=(c == 0), stop=(c == NCH - 1)),
                             r=[ring_t[sl]] + xts, w=[u_t])
                    if j in (2, 5):
                        self.ln_pop(LW, 1)
                    b = j % 2
                    S.op("act", lambda e: e.activation(out=sg[b], in_=gps, func=AF.Silu), r=[g_t], w=[sg_t[b]])
                    S.op("dve", lambda e: e.tensor_tensor(out=hT[:, j, :], in0=sg[b], in1=ups, op=ALU.mult),
                         r=[sg_t[b], u_t], w=hT_t)
                for tt in range(4):
                    t = blk * 4 + tt
                    b = t % 2
                    for h in range(2):
                        for j in range(NJ):
                            S.op("pe", lambda e, j=j, h=h: e.matmul(yps[b][:, h * 512:(h + 1) * 512],
                                                                  hT[:, j, tt * 128:(tt + 1) * 128],
                                                                  wd[:, j, h * 512:(h + 1) * 512],
                                                                  start=(j == 0), stop=(j == NJ - 1)),
                                 r=[hT_t[tt], wd_t], w=[y_t[b][h]])
                    self.ln_push(LW, self.ln_epilogue(t, yps[b], y_t[b], 2, L, LW, store=(s, self.out_ds) if last else None))
            self.ln_pop(LW, 2)
            S.barrier()

    def layer_conv(self, s):
        S, I = self.S, self.I
        scr, wt = self.W["conv_in"]
        with ExitStack() as ost:
          self.oT = self.sb(ost, [128, NCH, S_LEN], BF16, "oT")
          with ExitStack() as st:
            cw = self.sb(st, [128, 3, NCH], F32, "cw")
            cw_t = Tile("cw")
            ds = S.pds("cw")
            S.dma("sp", cw, I["conv_w"][0].rearrange("t (c p) -> p t c", p=128), ds, w=[cw_t],
                  allow_slow_non_contiguous=True)
            NSLOT = 6
            ring = [self.sb(st, [128, NCH, 128], BF16, "win") for _ in range(NSLOT)]
            ring_t = [Tile("win") for _ in range(NSLOT)]
            ring_ds = [S.pds("win") for _ in range(NSLOT)]
            u = [self.sb(st, [128, S_LEN + 2], F32, "u") for _ in range(2)]
            u_t = [Tile("u") for _ in range(2)]
            bg = [self.sb(st, [128, S_LEN], F32, "bg") for _ in range(2)]
            bg_t = [Tile("bg") for _ in range(2)]
            y = self.sb(st, [128, S_LEN], F32, "y")
            y_t = Tile("y")
            cgs = [self.sb(st, [128, 512], F32, "cgs") for _ in range(2)]
            cgs_t = [Tile("cgs") for _ in range(2)]
            pss = [[self.ps(st, [128, 512], F32, "cps") for _ in range(3)] for _ in range(2)]
            pss_t = [[Tile("cps") for _ in range(3)] for _ in range(2)]
            for b in range(2):
                S.op("pool", lambda e, b=b: e.memset(u[b][:, 0:1], 0.0), w=[u_t[b]])
                S.op("pool", lambda e, b=b: e.memset(u[b][:, S_LEN + 1:S_LEN + 2], 0.0), w=[u_t[b]])
            order = [(c, kind) for c in range(NCH) for kind in range(3)]

            def issue(i):
                c, kind = order[i]
                sl = i % NSLOT
                S.dma("sp", ring[sl], scr[kind * 8 + c], ring_ds[sl], r=[wt], w=[ring_t[sl]])

            nload = 0
            for i in range(NSLOT - 3):
                issue(i)
                nload += 1
            kk = 0
            for c in range(NCH):
                for _ in range(3):
                    if nload < len(order):
                        issue(nload)
                        nload += 1
                ub = c % 2
                for blk in range(4):
                    pb = kk % 2
                    kk += 1
                    xts = self.xT_t[blk * 4:(blk + 1) * 4]
                    for kind in range(3):
                        sl = (c * 3 + kind) % NSLOT
                        for k in range(NCH):
                            S.op("pe", lambda e, k=k, kind=kind, sl=sl: e.matmul(
                                pss[pb][kind], ring[sl][:, k, :], self.xT[:, k, blk * 512:(blk + 1) * 512],
                                start=(k == 0), stop=(k == NCH - 1)),
                                r=[ring_t[sl]] + xts, w=[pss_t[pb][kind]])
                    S.op("act", lambda e: e.copy(out=bg[ub][:, blk * 512:(blk + 1) * 512], in_=pss[pb][0]),
                         r=[pss_t[pb][0]], w=[bg_t[ub]])
                    S.op("act", lambda e: e.copy(out=cgs[pb], in_=pss[pb][1]), r=[pss_t[pb][1]], w=[cgs_t[pb]])
                    S.op("dve", lambda e: e.tensor_tensor(out=u[ub][:, 1 + blk * 512:1 + (blk + 1) * 512], in0=cgs[pb],
                                                          in1=pss[pb][2], op=ALU.mult),
                         r=[cgs_t[pb], pss_t[pb][2]], w=[u_t[ub]])
                uu = u[ub]
                S.op("act", lambda e: e.activation(out=y, in_=uu[:, 1:S_LEN + 1], func=AF.Copy, scale=cw[:, 1, c:c + 1]),
                     r=[u_t[ub], cw_t], w=[y_t])
                S.op("dve", lambda e: e.scalar_tensor_tensor(out=y, in0=uu[:, 0:S_LEN], scalar=cw[:, 0, c:c + 1], in1=y,
                                                             op0=ALU.mult, op1=ALU.add), r=[u_t[ub], cw_t, y_t], w=[y_t])
                S.op("dve", lambda e: e.scalar_tensor_tensor(out=y, in0=uu[:, 2:S_LEN + 2], scalar=cw[:, 2, c:c + 1], in1=y,
                                                             op0=ALU.mult, op1=ALU.add), r=[u_t[ub], cw_t, y_t], w=[y_t])
                S.op("pool", lambda e: e.tensor_tensor(out=self.oT[:, c, :], in0=bg[ub], in1=y, op=ALU.mult),
                     r=[bg_t[ub], y_t], w=self.oT_t)
            S.barrier()
          self.out_proj_ln("conv_out", 0, 0)

    def layer_diff(self, s):
        S, I = self.S, self.I
        scr, wt = self.W["diff_qkv"]
        lam_init = 0.8 - 0.6 * math.exp(-0.3 * 1)
        with ExitStack() as ost:
          self.oT = self.sb(ost, [128, NCH, S_LEN], BF16, "oT")
          with ExitStack() as st:
            cds = S.pds("dc")
            alibi = self.sb(st, [128, 3968], F32, "alibi")
            al_t = Tile("alibi")
            S.dma("sp", alibi, I["alibi"], cds, w=[al_t])
            lam_sb = self.sb(st, [128, 4, 64], F32, "lam")
            gsub = self.sb(st, [128, 128], F32, "gsub")
            sm = self.sb(st, [128, 8], F32, "sm")
            prm_t = Tile("prm")
            S.dma("sp", lam_sb, I["diff_lambda"][0].partition_broadcast(128), cds, w=[prm_t])
            S.dma("sp", gsub, I["diff_subln_g"][0].partition_broadcast(128), cds, w=[prm_t])
            S.group_done(cds, [al_t, prm_t])
            lp = self.sb(st, [128, 2, 64], F32, "lp")
            S.op("dve", lambda e: e.tensor_tensor(out=lp, in0=lam_sb[:, 0:4:2, :], in1=lam_sb[:, 1:4:2, :], op=ALU.mult),
                 r=[prm_t], w=[prm_t])
            S.op("dve", lambda e: e.reduce_sum(out=sm[:, 0:2], in_=lp, axis=mybir.AxisListType.X), r=[prm_t], w=[prm_t])
            S.op("act", lambda e: e.activation(out=sm[:, 2:4], in_=sm[:, 0:2], func=AF.Exp), r=[prm_t], w=[prm_t])
            S.op("dve", lambda e: e.tensor_tensor(out=sm[:, 4:5], in0=sm[:, 3:4], in1=sm[:, 2:3], op=ALU.subtract),
                 r=[prm_t], w=[prm_t])
            S.op("dve", lambda e: e.tensor_scalar(out=sm[:, 5:6], in0=sm[:, 4:5], scalar1=-lam_init, scalar2=None, op0=ALU.add),
                 r=[prm_t], w=[prm_t])
            S.op("dve", lambda e: e.tensor_scalar(out=gsub, in0=gsub, scalar1=1.0 - lam_init, scalar2=None, op0=ALU.mult),
                 r=[prm_t], w=[prm_t])
            neglam = sm[:, 5:6]

            NSLOT = 6
            ring = [self.sb(st, [128, NCH, 128], BF16, "wqkv") for _ in range(NSLOT)]
            ring_t = [Tile("wqkv") for _ in range(NSLOT)]
            ring_ds = [S.pds("wqkv") for _ in range(NSLOT)]
            qT = self.sb(st, [128, S_LEN], BF16, "qT")
            kT = self.sb(st, [128, S_LEN], BF16, "kT")
            vA = self.sb(st, [128, NT, 129], BF16, "vA")
            qT_t, kT_t, vA_t = Tile("qT"), Tile("kT"), Tile("vA")
            S.op("pool", lambda e: e.memset(vA[:, :, 128:129], 1.0), w=[vA_t])
            NSB = 3
            sbs = [self.sb(st, [128, 512], F32, "scs") for _ in range(NSB)]
            sbs_t = [Tile("scs") for _ in range(NSB)]
            NPT = 4
            pT = [self.sb(st, [128, 512], BF16, "pT") for _ in range(NPT)]
            pT_t = [Tile("pT") for _ in range(NPT)]
            scp = [self.ps(st, [128, 512], F32, "scp") for _ in range(4)]
            scp_t = [Tile("scp") for _ in range(4)]
            acc = [self.ps(st, [128, 2, 256], F32, "acc") for _ in range(4)]
            acc_t = [Tile("acc") for _ in range(4)]
            ep = self.sb(st, [128, 16], F32, "ep")
            ep_t = Tile("ep")
            t1 = self.sb(st, [128, 128], F32, "t1")
            o32 = self.sb(st, [128, 128], F32, "o32")
            junk = self.sb(st, [128, 128], F32, "junk")
            ob = [self.sb(st, [128, 128], BF16, "ob") for _ in range(2)]
            ob_t = [Tile("ob") for _ in range(2)]
            tpb = [self.ps(st, [128, 128], BF16, "tpb") for _ in range(0)]
            ework_t = Tile("ework")

            def issue(i):
                h, kind = divmod(i, 3)
                sl = i % NSLOT
                S.dma("sp", ring[sl], scr[kind * 8 + h], ring_ds[sl], r=[wt], w=[ring_t[sl]])

            for i in range(3):
                issue(i)
            rot = [0]

            def next_scp():
                i = rot[0] % 4
                rot[0] += 1
                return i

            for h in range(8):
                if h + 1 < 8:
                    for kind in range(3):
                        issue((h + 1) * 3 + kind)
                slq, slk, slv = (h * 3) % NSLOT, (h * 3 + 1) % NSLOT, (h * 3 + 2) % NSLOT
                slope = 2.0 ** (-(h + 1))
                for blk in range(4):
                    xts = self.xT_t[blk * 4:(blk + 1) * 4]
                    for (sl, dst, dst_t, sc) in ((slq, qT, qT_t, 0.125), (slk, kT, kT_t, 1.0)):
                        pi = next_scp()
                        for k in range(NCH):
                            S.op("pe", lambda e, k=k: e.matmul(scp[pi], ring[sl][:, k, :], self.xT[:, k, blk * 512:(blk + 1) * 512],
                                                             start=(k == 0), stop=(k == NCH - 1)),
                                 r=[ring_t[sl]] + xts, w=[scp_t[pi]])
                        S.op("act", lambda e: e.mul(out=dst[:, blk * 512:(blk + 1) * 512], in_=scp[pi], mul=sc),
                             r=[scp_t[pi]], w=[dst_t])
                    pi = next_scp()
                    for tt in range(4):
                        t = blk * 4 + tt
                        for k in range(NCH):
                            S.op("pe", lambda e, k=k: e.matmul(scp[pi][:, tt * 128:(tt + 1) * 128], self.xT[:, k, t * 128:(t + 1) * 128],
                                                             ring[slv][:, k, :], start=(k == 0), stop=(k == NCH - 1)),
                                 r=[ring_t[slv], self.xT_t[t]], w=[scp_t[pi]])
                    S.op("dve", lambda e: e.tensor_copy(out=vA[:, blk * 4:(blk + 1) * 4, 0:128],
                                                        in_=scp[pi].rearrange("p (a b) -> p a b", a=4)),
                         r=[scp_t[pi]], w=[vA_t])
                steps = [(qb, kc) for qb in range(4) for kc in range(16)]
                pend = None
                cnt = 0
                for i in range(len(steps) + 1):
                    cur = None
                    if i < len(steps):
                        qb, kc = steps[i]
                        c0 = 512 * qb - 128 * kc + 1920
                        cur = []
                        for m in range(2):
                            pi = next_scp()
                            S.op("pe", lambda e: e.matmul(scp[pi], kT[m * 64:(m + 1) * 64, kc * 128:(kc + 1) * 128],
                                                          qT[m * 64:(m + 1) * 64, qb * 512:(qb + 1) * 512], start=True, stop=True),
                                 r=[kT_t, qT_t], w=[scp_t[pi]])
                            si = cnt % NSB
                            pj = cnt % NPT
                            cnt += 1
                            S.op("dve", lambda e: e.scalar_tensor_tensor(out=sbs[si], in0=alibi[:, c0:c0 + 512], scalar=-slope,
                                                                         in1=scp[pi], op0=ALU.mult, op1=ALU.add),
                                 r=[al_t, scp_t[pi]], w=[sbs_t[si]])
                            S.op("act", lambda e: e.activation(out=pT[pj], in_=sbs[si], func=AF.Exp), r=[sbs_t[si]], w=[pT_t[pj]])
                            cur.append(pj)
                        cur = (qb, kc, cur)
                    if pend is not None:
                        pqb, pkc, pjs = pend
                        for m in range(2):
                            pj = pjs[m]
                            for qt in range(4):
                                S.op("pe", lambda e: e.matmul(acc[qt][:, m, 0:129], pT[pj][:, qt * 128:(qt + 1) * 128], vA[:, pkc, :],
                                                              start=(pkc == 0 and m == 0), stop=(pkc == 15 and m == 1),
                                                              skip_group_check=True),
                                     r=[pT_t[pj], vA_t], w=[acc_t[qt]])
                        if pkc == 15:
                            for qt in range(4):
                                tq = pqb * 4 + qt
                                a0 = acc[qt][:, 0, 0:128]
                                a1 = acc[qt][:, 1, 0:128]
                                S.op("dve", lambda e: e.reciprocal(out=ep[:, 0:2], in_=acc[qt][:, :, 128]),
                                     r=[acc_t[qt]], w=[ep_t])
                                S.op("dve", lambda e: e.tensor_tensor(out=ep[:, 2:3], in0=ep[:, 1:2], in1=neglam, op=ALU.mult),
                                     r=[ep_t, prm_t], w=[ep_t])
                                S.op("dve", lambda e: e.tensor_scalar(out=t1, in0=a1, scalar1=ep[:, 2:3], scalar2=None, op0=ALU.mult),
                                     r=[ep_t, acc_t[qt]], w=[ework_t])
                                S.op("dve", lambda e: e.scalar_tensor_tensor(out=o32, in0=a0, scalar=ep[:, 0:1], in1=t1,
                                                                             op0=ALU.mult, op1=ALU.add),
                                     r=[ep_t, acc_t[qt], ework_t], w=[ework_t])
                                S.op("act", lambda e: e.activation(out=junk, in_=o32, func=AF.Square, accum_out=ep[:, 4:5]),
                                     r=[ework_t], w=[ep_t])
                                S.op("dve", lambda e: e.tensor_scalar(out=ep[:, 5:6], in0=ep[:, 4:5], scalar1=1.0 / 128.0, scalar2=RMS_EPS,
                                                                      op0=ALU.mult, op1=ALU.add), r=[ep_t], w=[ep_t])
                                S.op("act", lambda e: e.activation(out=ep[:, 6:7], in_=ep[:, 5:6], func=AF.Ln), r=[ep_t], w=[ep_t])
                                S.op("act", lambda e: e.activation(out=ep[:, 7:8], in_=ep[:, 6:7], func=AF.Exp, scale=-0.5),
                                     r=[ep_t], w=[ep_t])
                                bi = tq % 2
                                S.op("dve", lambda e: e.scalar_tensor_tensor(out=ob[bi], in0=o32, scalar=ep[:, 7:8], in1=gsub,
                                                                             op0=ALU.mult, op1=ALU.mult),
                                     r=[ework_t, ep_t, prm_t], w=[ob_t[bi]])
                                pi = next_scp()
                                tview = scp[pi].bitcast(BF16)[:, 0:128]
                                S.op("pe", lambda e: e.transpose(out=tview, in_=ob[bi], identity=self.ident),
                                     r=[ob_t[bi], self.t_const], w=[scp_t[pi]])
                                S.op("act", lambda e: e.copy(out=self.oT[:, h, tq * 128:(tq + 1) * 128], in_=tview),
                                     r=[scp_t[pi]], w=[self.oT_t[tq]])
                    pend = cur
            S.barrier()
          self.out_proj_ln("diff_out", 0, 1)

    def layer_na(self, s):
        S, I = self.S, self.I
        scr, wt = self.W["na_qkv"]
        rpbg = I["na_rpbg"]
        with ExitStack() as ost:
          self.oT = self.sb(ost, [128, NCH, S_LEN], BF16, "oT")
          with ExitStack() as st:
            cds = S.pds("nc")
            M2 = self.sb(st, [128, 14, 16, 64], BF16, "M2")
            M2_t = Tile("M2")
            mask = self.sb(st, [128, 64], F32, "mask")
            mask_t = Tile("mask")
            S.dma("sp", mask, I["na_mask"], cds, w=[mask_t])
            NSLOT = 6
            ring = [self.sb(st, [128, NCH, 128], BF16, "wqkv") for _ in range(NSLOT)]
            ring_t = [Tile("wqkv") for _ in range(NSLOT)]
            ring_ds = [S.pds("wqkv") for _ in range(NSLOT)]
            qbd = self.sb(st, [128, 32, 128], BF16, "qbd")
            kT = self.sb(st, [128, S_LEN], BF16, "kT")
            vE = self.sb(st, [128, NT, 2, 65], BF16, "vE")
            vO = self.sb(st, [128, NT - 1, 2, 65], BF16, "vO")
            qT_t, kT_t, vE_t, vO_t = Tile("qT"), Tile("kT"), Tile("vE"), Tile("vO")
            S.op("pool", lambda e: e.memset(qbd, 0.0), w=[qT_t])
            S.op("pool", lambda e: e.memset(vE[:, :, :, 64:65], 1.0), w=[vE_t])
            S.op("pool", lambda e: e.memset(vO[:, :, :, 64:65], 1.0), w=[vO_t])
            NPT = 3
            pT = [self.sb(st, [128, 4, 2, 64], BF16, "pT") for _ in range(NPT)]
            pT_t = [Tile("pT") for _ in range(NPT)]
            ob = [self.sb(st, [64, 8, 128], BF16, "ob") for _ in range(2)]
            ob_t = [Tile("ob") for _ in range(2)]
            rz = self.sb(st, [64, 4], F32, "rz")
            rz_t = Tile("rz")
            scp = [self.ps(st, [128, 512], F32, "scp") for _ in range(4)]
            scp_t = [Tile("scp") for _ in range(4)]
            ops_ = [self.ps(st, [128, 4, 128], F32, "ops")[0:64, 0:2, :] for _ in range(2)]
            ops_t = [Tile("ops") for _ in range(2)]
            tps = [self.ps(st, [128, 1024], BF16, "tps")[:, 0:512] for _ in range(2)]
            tps_t = [Tile("tps") for _ in range(2)]
            with ExitStack() as st2:
                stage = [self.sb(st2, [128, 2, 1024], F32, "stage") for _ in range(1)]
                stage_t = [Tile("stage") for _ in range(1)]
                sds = [S.pds("stage") for _ in range(1)]
                for i in range(7):
                    d0 = 2 * i
                    b = 0
                    S.dma("sp", stage[b][0:64], rpbg[d0:d0 + 2].rearrange("d t h c -> t d (h c)"), sds[b], w=[stage_t[b]])
                    S.dma("sp", stage[b][64:128], rpbg[d0 + 1:d0 + 3].rearrange("d t h c -> t d (h c)"), sds[b], w=[stage_t[b]])
                    S.op("dve", lambda e: e.tensor_tensor(
                        out=M2[:, d0:d0 + 2, :, :].rearrange("p d h c -> p (d h) c"),
                        in0=stage[b].rearrange("p d (h c) -> p (d h) c", c=64),
                        in1=mask.unsqueeze(1).to_broadcast([128, 32, 64]), op=ALU.add),
                        r=[stage_t[b], mask_t], w=[M2_t])
                S.barrier()

            def issue(i):
                c, kind = divmod(i, 3)
                sl = i % NSLOT
                S.dma("sp", ring[sl], scr[kind * 8 + c], ring_ds[sl], r=[wt], w=[ring_t[sl]])

            for i in range(3):
                issue(i)
            rot = [0]

            def next_scp():
                i = rot[0] % 4
                rot[0] += 1
                return i

            ident = self.ident
            M2v = M2.rearrange("p d h c -> p d (h c)")
            for c in range(NCH):
                if c + 1 < NCH:
                    for kind in range(3):
                        issue((c + 1) * 3 + kind)
                slq, slk, slv = (c * 3) % NSLOT, (c * 3 + 1) % NSLOT, (c * 3 + 2) % NSLOT
                for blk in range(4):
                    xts = self.xT_t[blk * 4:(blk + 1) * 4]
                    for isq, sl in ((True, slq), (False, slk)):
                        pi = next_scp()
                        for k in range(NCH):
                            S.op("pe", lambda e, k=k: e.matmul(scp[pi], ring[sl][:, k, :], self.xT[:, k, blk * 512:(blk + 1) * 512],
                                                             start=(k == 0), stop=(k == NCH - 1)),
                                 r=[ring_t[sl]] + xts, w=[scp_t[pi]])
                        if isq:
                            for hh in range(2):
                                S.op("act", lambda e, hh=hh: e.mul(
                                    out=qbd[hh * 64:(hh + 1) * 64, blk * 8:(blk + 1) * 8, hh * 64:(hh + 1) * 64],
                                    in_=scp[pi][hh * 64:(hh + 1) * 64, :].rearrange("p (r c) -> p r c", r=8), mul=0.125),
                                    r=[scp_t[pi]], w=[qT_t])
                        else:
                            S.op("act", lambda e: e.copy(out=kT[:, blk * 512:(blk + 1) * 512], in_=scp[pi]),
                                 r=[scp_t[pi]], w=[kT_t])
                    for (vbuf, vbuf_t, off, ntile) in ((vE, vE_t, 0, 4), (vO, vO_t, 64, 4 if blk < 3 else 3)):
                        pi = next_scp()
                        for tt in range(ntile):
                            t = blk * 4 + tt
                            tok0 = t * 128 + off
                            tl = sorted(set([tok0 // 128, (tok0 + 127) // 128]))
                            for k in range(NCH):
                                S.op("pe", lambda e, k=k: e.matmul(scp[pi][:, tt * 128:(tt + 1) * 128], self.xT[:, k, tok0:tok0 + 128],
                                                                 ring[slv][:, k, :], start=(k == 0), stop=(k == NCH - 1)),
                                     r=[ring_t[slv]] + [self.xT_t[i] for i in tl], w=[scp_t[pi]])
                        S.op("dve", lambda e: e.tensor_copy(
                            out=vbuf[:, blk * 4:blk * 4 + ntile, :, 0:64],
                            in_=scp[pi][:, 0:ntile * 128].rearrange("p (a h d) -> p a h d", a=ntile, h=2)),
                            r=[scp_t[pi]], w=[vbuf_t])
                def stage_a(r):
                    rs = min(max(r - 4, 0), 24)
                    d0b = rs - r + 7
                    pj = r % NPT
                    pi = next_scp()
                    scv = scp[pi].rearrange("p (j c) -> p j c", j=4)
                    S.op("pe", lambda e: e.matmul(scv, ident, M2v[:, d0b:d0b + 7:2, 2 * c * 64:2 * c * 64 + 128], start=True,
                                                  stop=False, skip_group_check=True),
                         r=[M2_t, self.t_const], w=[scp_t[pi]])
                    for j in range(4):
                        ks = (rs + 2 * j) * 64
                        S.op("pe", lambda e: e.matmul(scv[:, j, :], kT[:, ks:ks + 128], qbd[:, r, :], start=False,
                                                      stop=(j == 3), skip_group_check=True),
                             r=[kT_t, qT_t], w=[scp_t[pi]])
                    S.op("act", lambda e: e.activation(out=pT[pj].rearrange("p j h c -> p (j h c)"), in_=scp[pi], func=AF.Exp),
                         r=[scp_t[pi]], w=[pT_t[pj]])

                def stage_b(r):
                    rs = min(max(r - 4, 0), 24)
                    pj = r % NPT
                    oi = r % 2
                    for j in range(4):
                        kr = rs + 2 * j
                        if kr % 2 == 0:
                            vb, vb_t, vt = vE, vE_t, kr // 2
                        else:
                            vb, vb_t, vt = vO, vO_t, (kr - 1) // 2
                        for hh in range(2):
                            S.op("pe", lambda e: e.matmul(ops_[oi][:, hh, 0:65], pT[pj][:, j, hh, :], vb[:, vt, hh, :],
                                                          start=(j == 0 and hh == 0), stop=(j == 3 and hh == 1), skip_group_check=True),
                                 r=[pT_t[pj], vb_t], w=[ops_t[oi]])
                    g8, r8 = divmod(r, 8)
                    bi = g8 % 2
                    S.op("dve", lambda e: e.reciprocal(out=rz[:, 2 * oi:2 * oi + 2], in_=ops_[oi][:, :, 64]), r=[ops_t[oi]], w=[rz_t])
                    S.op("dve", lambda e: e.tensor_tensor(out=ob[bi][:, r8, :].rearrange("p (h d) -> p h d", h=2),
                                                          in0=ops_[oi][:, :, 0:64],
                                                          in1=rz[:, 2 * oi:2 * oi + 2].unsqueeze(2).to_broadcast([64, 2, 64]), op=ALU.mult),
                         r=[ops_t[oi], rz_t], w=[ob_t[bi]])

                def stage_c(r):
                    g8, r8 = divmod(r, 8)
                    bi = g8 % 2
                    S.op("pe", lambda e: e.transpose(out=tps[bi][:, r8 * 64:(r8 + 1) * 64], in_=ob[bi][:, r8, :],
                                                     identity=ident[0:64, 0:64]),
                         r=[ob_t[bi], self.t_const], w=[tps_t[bi]])
                    if r8 == 7:
                        S.op("act", lambda e: e.copy(out=self.oT[:, c, g8 * 512:(g8 + 1) * 512], in_=tps[bi]),
                             r=[tps_t[bi]], w=self.oT_t[g8 * 4:(g8 + 1) * 4])

                for i in range(32 + 2):
                    if i < 32:
                        stage_a(i)
                    if 1 <= i <= 32:
                        stage_b(i - 1)
                    if i >= 2:
                        stage_c(i - 2)
            S.barrier()
          self.out_proj_ln("na_out", 0, 2)

    def layer_mla(self, s):
        S, I = self.S, self.I
        ident = self.ident
        sm_scale = 96.0 ** -0.5
        self.oT, self.oT_t = self.xT, self.xT_t
        with ExitStack() as st:
            cds = S.pds("mc")
            cnT = self.sb(st, [128, 3, S_LEN], BF16, "cnT")
            cqnT = cnT[:, 0:2, :]
            ckvnT = cnT[:, 2, :]
            kropeT = self.sb(st, [128, S_LEN], BF16, "kropeT")
            cq_t, ckv_t, krt_t = Tile("cqnT"), Tile("ckvnT"), Tile("kropeT")
            wuq = self.sb(st, [128, 2, 1568], BF16, "wuq")
            wq2 = self.sb(st, [128, 2, 16, 128], BF16, "wq2")
            wukv = self.sb(st, [128, 2048], BF16, "wukv")
            w_t = Tile("mlaw")
            wds = S.pds("mw")
            S.op("pool", lambda e: e.memset(wuq, 0.0), w=[w_t])
            S.dma("sp", wuq[:, :, 0:1536], self.W["mla_uq"][0], wds, r=[self.W["mla_uq"][1]], w=[w_t])
            S.dma("sp", wukv, self.W["mla_ukv"][0][:, 0, :], wds, r=[self.W["mla_ukv"][1]], w=[w_t])
            wuqv = wuq[:, :, 0:1536].rearrange("p k (h d) -> p k h d", h=16)
            S.op("pool", lambda e: e.memset(wq2, 0.0), w=[w_t])
            S.op("dve", lambda e: e.tensor_scalar(out=wq2[:, :, :, 64:80], in0=wuqv[:, :, :, 80:96], scalar1=-1.0, scalar2=None,
                                                  op0=ALU.mult), r=[w_t], w=[w_t])
            S.op("dve", lambda e: e.tensor_copy(out=wq2[:, :, :, 80:96], in_=wuqv[:, :, :, 64:80]), r=[w_t], w=[w_t])
            cosf = self.sb(st, [128, S_LEN], F32, "cosf")
            sinf = self.sb(st, [128, S_LEN], F32, "sinf")
            rp_t = Tile("ropef")
            S.dma("sp", cosf, I["rope_cos_fm"], cds, w=[rp_t])
            S.dma("sp", sinf, I["rope_sin_fm"], cds, w=[rp_t])
            cds_tiles = [rp_t]
            scp = [self.ps(st, [128, 512], F32, "scp") for _ in range(4)]
            scp_t = [Tile("scp") for _ in range(4)]
            rot = [0]

            def next_scp():
                i = rot[0] % 4
                rot[0] += 1
                return i

            with ExitStack() as st2:
                wa = self.sb(st2, [128, NCH, 416], BF16, "wa")
                wa_t = Tile("wa")
                S.dma("sp", wa, self.W["mla_a"][0], cds, r=[self.W["mla_a"][1]], w=[wa_t])
                gq = self.sb(st2, [128, 256], F32, "gq")
                gkv = self.sb(st2, [128, 128], F32, "gkv")
                g_t = Tile("g")
                S.dma("sp", gq, I["mla_g_q"][0].partition_broadcast(128), cds, w=[g_t])
                S.dma("sp", gkv, I["mla_g_kv"][0].partition_broadcast(128), cds, w=[g_t])
                cos_tm = self.sb(st2, [128, NT, 16], F32, "cos_tm")
                sin_tm = self.sb(st2, [128, NT, 16], F32, "sin_tm")
                S.dma("sp", cos_tm, I["rope_cos_tm"], cds, w=[g_t])
                S.dma("sp", sin_tm, I["rope_sin_tm"], cds, w=[g_t])
                S.group_done(cds, [rp_t, wa_t, g_t])
                KRraw = self.sb(st2, [128, NT, 32], F32, "KRraw")
                KR = self.sb(st2, [128, NT, 128], BF16, "KR")
                kr_t = Tile("KR")
                S.op("pool", lambda e: e.memset(KR, 0.0), w=[kr_t])
                junk = self.sb(st2, [128, 256], F32, "junk")
                junk2 = self.sb(st2, [128, 128], F32, "junk2")
                ssq = [self.sb(st2, [128, 8], F32, "ssq") for _ in range(2)]
                ssq_t = [Tile("ssq") for _ in range(2)]
                nb = [self.sb(st2, [128, 384], BF16, "nb") for _ in range(2)]
                nb_t = [Tile("nb") for _ in range(2)]
                tpa = [self.ps(st2, [128, 8, 128], BF16, "tpa") for _ in range(2)]
                tpa_t = [Tile("tpa") for _ in range(2)]
                ADBG = int(os.environ.get("MLA_A", "9"))
                for t in range(NT if ADBG >= 2 else 0):
                    b = t % 2
                    pi = next_scp()
                    aps = scp[pi]
                    for k in range(NCH):
                        S.op("pe", lambda e, k=k: e.matmul(aps[:, 0:416], self.xT[:, k, t * 128:(t + 1) * 128], wa[:, k, :],
                                                         start=(k == 0), stop=(k == NCH - 1)),
                             r=[self.xT_t[t], wa_t], w=[scp_t[pi]])
                    LDBG = int(os.environ.get("MLA_L", "9"))
                    q = ssq[b]
                    if LDBG < 2:
                        continue
                    S.op("act", lambda e: e.activation(out=junk[:, 0:256], in_=aps[:, 0:256], func=AF.Square, accum_out=q[:, 0:1]),
                         r=[scp_t[pi]], w=[ssq_t[b]])
                    S.op("act", lambda e: e.activation(out=junk2, in_=aps[:, 256:384], func=AF.Square, accum_out=q[:, 1:2]),
                         r=[scp_t[pi]], w=[ssq_t[b]])
                    if LDBG < 3:
                        continue
                    S.op("dve", lambda e: e.tensor_scalar(out=q[:, 2:3], in0=q[:, 0:1], scalar1=1.0 / 256.0, scalar2=RMS_EPS,
                                                          op0=ALU.mult, op1=ALU.add), r=[ssq_t[b]], w=[ssq_t[b]])
                    S.op("dve", lambda e: e.tensor_scalar(out=q[:, 3:4], in0=q[:, 1:2], scalar1=1.0 / 128.0, scalar2=RMS_EPS,
                                                          op0=ALU.mult, op1=ALU.add), r=[ssq_t[b]], w=[ssq_t[b]])
                    S.op("act", lambda e: e.activation(out=q[:, 4:6], in_=q[:, 2:4], func=AF.Ln), r=[ssq_t[b]], w=[ssq_t[b]])
                    S.op("act", lambda e: e.activation(out=q[:, 6:8], in_=q[:, 4:6], func=AF.Exp, scale=-0.5),
                         r=[ssq_t[b]], w=[ssq_t[b]])
                    if LDBG < 4:
                        continue
                    S.op("dve", lambda e: e.scalar_tensor_tensor(out=nb[b][:, 0:256], in0=aps[:, 0:256], scalar=q[:, 6:7], in1=gq,
                                                                 op0=ALU.mult, op1=ALU.mult),
                         r=[scp_t[pi], ssq_t[b], g_t], w=[nb_t[b]])
                    S.op("dve", lambda e: e.scalar_tensor_tensor(out=nb[b][:, 256:384], in0=aps[:, 256:384], scalar=q[:, 7:8], in1=gkv,
                                                                 op0=ALU.mult, op1=ALU.mult),
                         r=[scp_t[pi], ssq_t[b], g_t], w=[nb_t[b]])
                    S.op("act", lambda e: e.copy(out=KRraw[:, t, :], in_=aps[:, 384:416]), r=[scp_t[pi]], w=[kr_t])
                    if LDBG < 5:
                        continue
                    for i in range(3):
                        S.op("pe", lambda e, i=i: e.transpose(out=tpa[b][:, i, :], in_=nb[b][:, i * 128:(i + 1) * 128], identity=ident),
                             r=[nb_t[b], self.t_const], w=[tpa_t[b]])
                    S.op("act", lambda e: e.copy(out=cnT[:, :, t * 128:(t + 1) * 128], in_=tpa[b][:, 0:3, :]),
                         r=[tpa_t[b]], w=[cq_t, ckv_t])
                ra = self.sb(st2, [128, NT, 16], F32, "ra")
                rb = self.sb(st2, [128, NT, 16], F32, "rb")
                t1, t2 = KRraw[:, :, 0:16], KRraw[:, :, 16:32]
                if ADBG < 3:
                    S.barrier()
                    raise_skip = True
                else:
                    raise_skip = False
                if not raise_skip:
                    S.op("dve", lambda e: e.tensor_tensor(out=ra, in0=t1, in1=cos_tm, op=ALU.mult), r=[kr_t, g_t], w=[kr_t])
                    S.op("dve", lambda e: e.tensor_tensor(out=rb, in0=t2, in1=sin_tm, op=ALU.mult), r=[kr_t, g_t], w=[kr_t])
                    S.op("dve", lambda e: e.tensor_tensor(out=KR[:, :, 64:80], in0=ra, in1=rb, op=ALU.subtract), r=[kr_t], w=[kr_t])
                    S.op("dve", lambda e: e.tensor_tensor(out=ra, in0=t1, in1=sin_tm, op=ALU.mult), r=[kr_t, g_t], w=[kr_t])
                    S.op("dve", lambda e: e.tensor_tensor(out=rb, in0=t2, in1=cos_tm, op=ALU.mult), r=[kr_t, g_t], w=[kr_t])
                    S.op("dve", lambda e: e.tensor_tensor(out=KR[:, :, 80:96], in0=ra, in1=rb, op=ALU.add), r=[kr_t], w=[kr_t])
                    for g4 in range(4):
                        b = g4 % 2
                        for i in range(4):
                            t = g4 * 4 + i
                            S.op("pe", lambda e, i=i: e.transpose(out=tpa[b][:, i, :], in_=KR[:, t, :], identity=ident),
                                 r=[kr_t, self.t_const], w=[tpa_t[b]])
                        S.op("act", lambda e: e.copy(out=kropeT[64:96, g4 * 512:(g4 + 1) * 512],
                                                     in_=tpa[b][64:96, 0:4, :].rearrange("p a b -> p (a b)")),
                             r=[tpa_t[b]], w=[krt_t])
                S.barrier()

            qT = [self.sb(st, [128, S_LEN], BF16, "qT") for _ in range(2)]
            kT = [self.sb(st, [128, S_LEN], BF16, "kT") for _ in range(2)]
            vA = [self.sb(st, [128, NT, 128], BF16, "vP") for _ in range(2)]
            ones128 = self.sb(st, [128, 128], BF16, "ones128")
            ones_t = Tile("ones")
            S.op("pool", lambda e: e.memset(ones128, 1.0), w=[ones_t])
            qT_t = [Tile("qT") for _ in range(2)]
            kT_t = [Tile("kT") for _ in range(2)]
            vA_t = [Tile("vA") for _ in range(2)]
            for b in range(2):
                S.op("pool", lambda e, b=b: e.memset(vA[b], 0.0), w=[vA_t[b]])
                S.op("pool", lambda e, b=b: e.memset(qT[b], 0.0), w=[qT_t[b]])
                S.op("pool", lambda e, b=b: e.memset(kT[b], 0.0), w=[kT_t[b]])
            tm1 = [self.sb(st, [128, 512], F32, "tm1") for _ in range(2)]
            tm2 = [self.sb(st, [128, 512], F32, "tm2") for _ in range(2)]
            tm_t = [Tile("tm") for _ in range(2)]
            NPT = 4
            pT = [self.sb(st, [128, 512], BF16, "pT") for _ in range(NPT)]
            pT_t = [Tile("pT") for _ in range(NPT)]
            accO = [self.ps(st, [128, 512], F32, "accO") for _ in range(2)]
            accZ = [self.ps(st, [128, 512], F32, "accZ") for _ in range(2)]
            accO_t = [Tile("accO") for _ in range(2)]
            accZ_t = [Tile("accZ") for _ in range(2)]
            rzf = self.sb(st, [128, 512], F32, "rzf")
            rz_t = Tile("rz")
            cnt = 0
            tmk = 0
            qbk = 0
            for h in range(16):
                hb = h % 2
                c = h // 2
                for blk in range(4):
                    cols = slice(blk * 512, (blk + 1) * 512)
                    pa, pb = next_scp(), next_scp()
                    for k in range(2):
                        S.op("pe", lambda e, k=k: e.matmul(scp[pa], wuq[:, k, h * 96:h * 96 + 128], cqnT[:, k, cols],
                                                         start=(k == 0), stop=(k == 1)), r=[w_t, cq_t], w=[scp_t[pa]])
                    for k in range(2):
                        S.op("pe", lambda e, k=k: e.matmul(scp[pb], wq2[:, k, h, :], cqnT[:, k, cols],
                                                         start=(k == 0), stop=(k == 1)), r=[w_t, cq_t], w=[scp_t[pb]])
                    S.op("dve", lambda e: e.tensor_copy(out=qT[hb][0:64, cols], in_=scp[pa][0:64]), r=[scp_t[pa]], w=[qT_t[hb]])
                    tb = tmk % 2
                    tmk += 1
                    S.op("dve", lambda e: e.tensor_tensor(out=tm1[tb][64:96], in0=scp[pa][64:96], in1=cosf[64:96, cols], op=ALU.mult),
                         r=[scp_t[pa], rp_t], w=[tm_t[tb]])
                    S.op("dve", lambda e: e.tensor_tensor(out=tm2[tb][64:96], in0=scp[pb][64:96], in1=sinf[64:96, cols], op=ALU.mult),
                         r=[scp_t[pb], rp_t], w=[tm_t[tb]])
                    S.op("pool", lambda e: e.tensor_tensor(out=qT[hb][64:96, cols], in0=tm1[tb][64:96], in1=tm2[tb][64:96], op=ALU.add),
                         r=[tm_t[tb]], w=[qT_t[hb]])
                    pk = next_scp()
                    S.op("pe", lambda e: e.matmul(scp[pk][0:64], wukv[:, h * 128:h * 128 + 64], ckvnT[:, cols], start=True, stop=True),
                         r=[w_t, ckv_t], w=[scp_t[pk]])
                    S.op("act", lambda e: e.copy(out=kT[hb][0:64, cols], in_=scp[pk][0:64]), r=[scp_t[pk]], w=[kT_t[hb]])
                S.op("pool", lambda e: e.tensor_copy(out=kT[hb][64:96, :], in_=kropeT[64:96, :]), r=[krt_t], w=[kT_t[hb]])
                for half in range(2):
                    pv = next_scp()
                    for i in range(8):
                        t = half * 8 + i
                        S.op("pe", lambda e, i=i: e.matmul(scp[pv][:, i * 64:(i + 1) * 64], ckvnT[:, t * 128:(t + 1) * 128],
                                                         wukv[:, h * 128 + 64:h * 128 + 128], start=True, stop=True),
                             r=[w_t, ckv_t], w=[scp_t[pv]])
                    S.op("dve", lambda e: e.tensor_copy(out=vA[hb][:, half * 8:(half + 1) * 8, hb * 64:(hb + 1) * 64],
                                                        in_=scp[pv].rearrange("p (a d) -> p a d", a=8)),
                         r=[scp_t[pv]], w=[vA_t[hb]])
                steps = [(qb, kc) for qb in range(4) for kc in range(16)]
                LAG = 3
                fifo = []
                for i in range(len(steps) + LAG):
                    if i < len(steps):
                        qb, kc = steps[i]
                        pi = next_scp()
                        S.op("pe", lambda e: e.matmul(scp[pi], kT[hb][:, kc * 128:(kc + 1) * 128],
                                                      qT[hb][:, qb * 512:(qb + 1) * 512], start=True, stop=True),
                             r=[kT_t[hb], qT_t[hb]], w=[scp_t[pi]])
                        pj = cnt % NPT
                        cnt += 1
                        S.op("act", lambda e: e.activation(out=pT[pj], in_=scp[pi], func=AF.Exp, scale=sm_scale),
                             r=[scp_t[pi]], w=[pT_t[pj]])
                        fifo.append((qb, kc, pj))
                    if i >= LAG:
                        pqb, pkc, pj = fifo.pop(0)
                        ab = (qbk + pqb) % 2
                        S.op("pe", lambda e: e.matmul(accO[ab], vA[hb][:, pkc, :], pT[pj], start=(pkc == 0), stop=(pkc == 15)),
                             r=[pT_t[pj], vA_t[hb]], w=[accO_t[ab]])
                        S.op("pe", lambda e: e.matmul(accZ[ab], ones128, pT[pj], start=(pkc == 0), stop=(pkc == 15)),
                             r=[pT_t[pj], ones_t], w=[accZ_t[ab]])
                        if pkc == 15 and os.environ.get("MLA_NOEP", "0") != "1":
                            rows = slice(hb * 64, (hb + 1) * 64)
                            S.op("dve", lambda e: e.reciprocal(out=rzf, in_=accZ[ab]), r=[accZ_t[ab]], w=[rz_t])
                            S.op("dve", lambda e: e.tensor_tensor(out=self.oT[rows, c, pqb * 512:(pqb + 1) * 512], in0=accO[ab][rows],
                                                                  in1=rzf[rows], op=ALU.mult),
                                 r=[accO_t[ab], rz_t], w=self.oT_t[pqb * 4:(pqb + 1) * 4])
                qbk += 4
            S.barrier()
        self.out_proj_ln("mla_out", 0, 3)


_CONSTS = None


def _run(prog, inputs, x_shards):
    global _CONSTS
    if _CONSTS is None:
        _CONSTS = _consts()
    col = np.arange(64)
    dc = np.clip(col[:, None] - col[None, :], -15, 15) + 15
    rpb = np.asarray(inputs["na_rpb"], dtype=np.float32)[0]
    rpbg = np.ascontiguousarray(np.transpose(rpb[:, :, dc], (1, 2, 0, 3)))
    in_maps = []
    for xs in x_shards:
        m = {"x": np.ascontiguousarray(xs, dtype=np.float32)}
        for k in INPUT_SHAPES:
            m[k] = np.ascontiguousarray(inputs[k], dtype=np.float32)
        for k in CONST_SPECS:
            m["c_" + k] = _CONSTS[k]
        m["na_rpbg"] = rpbg
        in_maps.append(m)
    res = run_bass_kernel_spmd(prog.nc, in_maps, core_ids=list(range(len(x_shards))))
    return [np.asarray(r["out"]) for r in res.results]


def kernel(**inputs):
    x = np.asarray(inputs["x"], dtype=np.float32)
    prog = Prog()
    shards = [x[i * SEQ_PER_CORE:(i + 1) * SEQ_PER_CORE] for i in range(NCORES)]
    outs = _run(prog, inputs, shards)
    return np.concatenate(outs, axis=0).astype(np.float32)
```

```python
import math
import os
from contextlib import ExitStack
import numpy as np
import ml_dtypes
import concourse.bass as bass
import concourse.mybir as mybir
from concourse.bass_utils import run_bass_kernel_spmd

F32 = mybir.dt.float32
BF16 = mybir.dt.bfloat16
AF = mybir.ActivationFunctionType
ALU = mybir.AluOpType

D = 1024
S_LEN = 2048
NT = 16
NCH = 8
DFF = 2816
NJ = 22
ALPHA = 8.0 ** 0.25
LN_EPS = 1e-5
RMS_EPS = 1e-6
NCORES = 8
SEQ_PER_CORE = 2
NEG = -30000.0


class Tile:
    __slots__ = ("name", "writer", "readers")

    def __init__(self, name):
        self.name = name
        self.writer = None
        self.readers = {}


class DSem:
    __slots__ = ("key", "total")

    def __init__(self, key):
        self.key = key
        self.total = 0


class Sched:
    LIMIT = 24000

    def __init__(self, nc):
        self.nc = nc
        self.E = dict(pe=nc.tensor, act=nc.scalar, dve=nc.vector, pool=nc.gpsimd, sp=nc.sync)
        self.sems = {}
        self.nsem = 0
        self.cur = {}
        self.waited = {e: {} for e in self.E}
        self.dsems = []
        self.nwaits = 0
        self.nops = 0
        for e in ("pe", "act", "dve", "pool"):
            self._new_eng_sem(e)

    def _alloc(self, name):
        k = self.nsem
        self.nsem += 1
        self.sems[k] = self.nc.alloc_semaphore(f"s{k}_{name}")
        return k

    def _new_eng_sem(self, e):
        self.cur[e] = [self._alloc(e), 0]

    def dsem(self, name="d"):
        d = DSem(self._alloc(name))
        self.dsems.append(d)
        return d

    def pool_reset(self):
        self.pool_i = 0

    def pds(self, name="p"):
        if not hasattr(self, "pool"):
            self.pool, self.pool_i = [], 0
        if self.pool_i >= len(self.pool):
            self.pool.append(self.dsem(name))
        d = self.pool[self.pool_i]
        self.pool_i += 1
        return d

    def _wait(self, eng, tok):
        key, val = tok[0], tok[1]
        if self.waited[eng].get(key, 0) >= val:
            return
        self.E[eng].wait_ge(self.sems[key], val)
        self.waited[eng][key] = val
        self.nwaits += 1

    def _deps(self, eng, r, w, is_dma):
        deps = []
        for t in r:
            wr = t.writer
            if wr is not None:
                if wr[2] == eng and not is_dma and eng == "pe":
                    continue
                deps.append(wr)
        for t in w:
            wr = t.writer
            if wr is not None and (is_dma or wr[2] != eng):
                deps.append(wr)
            for tok in t.readers.values():
                if is_dma or tok[2] != eng:
                    deps.append(tok)
        return deps

    def _commit(self, tok, r, w):
        for t in r:
            t.readers[tok[0]] = tok
        for t in w:
            t.writer = tok
            t.readers = {}

    def op(self, eng, fn, r=(), w=()):
        for tok in self._deps(eng, r, w, False):
            self._wait(eng, tok)
        cur = self.cur[eng]
        if cur[1] >= self.LIMIT:
            self._new_eng_sem(eng)
            cur = self.cur[eng]
        ins = fn(self.E[eng])
        cur[1] += 1
        ins.then_inc(self.sems[cur[0]], 1)
        tok = (cur[0], cur[1], eng)
        self._commit(tok, r, w)
        self.nops += 1
        return tok

    def dma(self, q, out, in_, ds, r=(), w=(), **kw):
        for tok in self._deps(q, r, w, True):
            self._wait(q, tok)
        if ds.total + 16 > self.LIMIT:
            ds.key = self._alloc("d")
            ds.total = 0
        ins = self.E[q].dma_start(out=out, in_=in_, **kw)
        ds.total += 16
        ins.then_inc(self.sems[ds.key], 16)
        tok = (ds.key, ds.total, "dma")
        self._commit(tok, r, w)
        self.nops += 1
        return tok

    def group_done(self, ds, tiles):
        tok = (ds.key, ds.total, "dma")
        for t in tiles:
            t.writer = tok

    def barrier(self):
        toks = [(c[0], c[1], e) for e, c in self.cur.items() if c[1] > 0]
        toks += [(d.key, d.total, "dma") for d in self.dsems if d.total > 0]
        for e in self.E:
            for tok in toks:
                if tok[2] == e and e == "pe":
                    continue
                self._wait(e, tok)
        self.pool_reset()


def _consts():
    c = {}
    c["ident"] = np.eye(128, dtype=np.float32).astype(ml_dtypes.bfloat16)
    p = np.arange(128)[:, None]
    cc = np.arange(3968)[None, :]
    c["alibi"] = np.abs(p - cc + 1920).astype(np.float32)
    inv_freq = (1.0 / (10000.0 ** (np.arange(0, 32, 2, dtype=np.float32) / np.float32(32)))).astype(np.float32)
    ang = (np.arange(S_LEN, dtype=np.float32)[:, None] * inv_freq[None, :]).astype(np.float32)
    cos, sin = np.cos(ang).astype(np.float32), np.sin(ang).astype(np.float32)
    c["rope_cos_tm"] = np.ascontiguousarray(cos.reshape(NT, 128, 16).transpose(1, 0, 2))
    c["rope_sin_tm"] = np.ascontiguousarray(sin.reshape(NT, 128, 16).transpose(1, 0, 2))
    cf = np.zeros((128, S_LEN), np.float32)
    sf = np.zeros((128, S_LEN), np.float32)
    for i in range(32):
        cf[64 + i] = cos[:, i % 16]
        sf[64 + i] = sin[:, i % 16]
    c["rope_cos_fm"] = cf
    c["rope_sin_fm"] = sf
    col = np.arange(64)
    cs = np.clip(col - 8, 0, 48)
    valid = (col[None, :] >= cs[:, None]) & (col[None, :] < cs[:, None] + 16)
    madd = np.where(valid.T, 0.0, NEG).astype(np.float32)
    c["na_mask"] = np.concatenate([madd, madd], axis=0)
    return c


CONST_SPECS = {
    "ident": ([128, 128], BF16),
    "alibi": ([128, 3968], F32),
    "rope_cos_tm": ([128, NT, 16], F32),
    "rope_sin_tm": ([128, NT, 16], F32),
    "rope_cos_fm": ([128, S_LEN], F32),
    "rope_sin_fm": ([128, S_LEN], F32),
    "na_mask": ([128, 64], F32),
}

INPUT_SHAPES = {
    "conv_w_in": [1, 1024, 3072], "conv_w": [1, 3, 1024], "conv_w_out": [1, 1024, 1024],
    "diff_w_qkv": [1, 1024, 3072], "diff_lambda": [1, 4, 64], "diff_subln_g": [1, 128], "diff_w_out": [1, 1024, 1024],
    "na_w_qkv": [1, 1024, 3072], "na_rpb": [1, 16, 15, 31], "na_w_out": [1, 1024, 1024],
    "mla_w_a": [1, 1024, 416], "mla_g_q": [1, 256], "mla_g_kv": [1, 128], "mla_w_uq": [1, 256, 1536],
    "mla_w_ukv": [1, 128, 2048], "mla_w_out": [1, 1024, 1024],
    "ln1_g": [4, 1024], "ln1_b": [4, 1024], "ffn_w_gu": [4, 1024, 5632], "ffn_w_down": [4, 2816, 1024],
    "ln2_g": [4, 1024], "ln2_b": [4, 1024],
}
DERIVED_SHAPES = {"na_rpbg": [15, 64, 16, 64]}


class Prog:
    def __init__(self, layers=(0, 1, 2, 3), nseq=SEQ_PER_CORE, do_ffn=True):
        self.layers = tuple(layers)
        self.nseq = nseq
        self.do_ffn = do_ffn
        nc = self.nc = bass.Bass("TRN2", target_bir_lowering=False)
        self.S = Sched(nc)
        self.I = {}
        self.I["x"] = nc.dram_tensor("x", [nseq, S_LEN, D], F32, kind="ExternalInput").ap()
        for k, shp in INPUT_SHAPES.items():
            self.I[k] = nc.dram_tensor(k, shp, F32, kind="ExternalInput").ap()
        for k, (shp, dt) in CONST_SPECS.items():
            self.I[k] = nc.dram_tensor("c_" + k, shp, dt, kind="ExternalInput").ap()
        for k, shp in DERIVED_SHAPES.items():
            self.I[k] = nc.dram_tensor(k, shp, F32, kind="ExternalInput").ap()
        self.out = nc.dram_tensor("out", [nseq, S_LEN, D], F32, kind="ExternalOutput").ap()
        self.uid = 0
        self.build()

    def name(self, p):
        self.uid += 1
        return f"{p}{self.uid}"

    def sb(self, st, shape, dt, name="sb"):
        return st.enter_context(self.nc.sbuf_tensor(self.name(name), shape, dt)).ap()

    def ps(self, st, shape, dt, name="ps"):
        return st.enter_context(self.nc.psum_tensor(self.name(name), shape, dt)).ap()

    def scratch(self, shape, dt=BF16, name="scr"):
        return self.nc.dram_tensor(self.name(name), shape, dt, kind="Internal").ap()

    def conv_lhsT(self, w2d, K, N, name):
        S = self.S
        kc, nj = K // 128, N // 128
        scr = self.scratch([nj, 128, kc, 128], name=name)
        ds = S.dsem(name)
        t = Tile(name)
        for j in range(nj):
            src = w2d[:, j * 128:(j + 1) * 128].rearrange("(kc p) n -> p kc n", p=128)
            S.dma("pool", scr[j], src, ds, w=[t])
        t.writer = (ds.key, ds.total, "dma")
        return scr, t

    def conv_rhs(self, w2d, K, N, name):
        S = self.S
        kc = K // 128
        scr = self.scratch([128, kc, N], name=name)
        ds = S.dsem(name)
        t = Tile(name)
        for k in range(kc):
            S.dma("pool", scr[:, k, :], w2d[k * 128:(k + 1) * 128, :], ds, w=[t])
        t.writer = (ds.key, ds.total, "dma")
        return scr, t

    def build(self):
        nc, S, I = self.nc, self.S, self.I
        with ExitStack() as gst:
            cds = S.dsem("const")
            self.t_const = Tile("const")
            self.ident = self.sb(gst, [128, 128], BF16, "ident")
            S.dma("sp", self.ident, I["ident"], cds, w=[self.t_const])
            self.W = {}
            for i, L in enumerate(self.layers):
                self.convert_layer(L)
            self.x = self.sb(gst, [128, NT, D], F32, "x")
            self.xT = self.sb(gst, [128, NCH, S_LEN], BF16, "xT")
            self.x_t = [Tile(f"x{t}") for t in range(NT)]
            self.xT_t = [Tile(f"xT{t}") for t in range(NT)]
            self.oT_t = [Tile(f"oT{t}") for t in range(NT)]
            self.out_ds = [S.dsem("out") for _ in range(4)]
            self.xin_ds = [S.dsem("xin") for _ in range(4)]
            for s in range(self.nseq):
                self.load_x(s)
                for L in self.layers:
                    [self.layer_conv, self.layer_diff, self.layer_na, self.layer_mla][L](s)
                    if self.do_ffn:
                        self.ffn(L, s, last=(L == self.layers[-1]))
                if not self.do_ffn:
                    self.store_x(s)
                S.barrier()
            S.barrier()

    def convert_layer(self, L):
        I, W = self.I, self.W
        if L == 0:
            W["conv_in"] = self.conv_lhsT(I["conv_w_in"][0], 1024, 3072, "cwin")
            W["conv_out"] = self.conv_rhs(I["conv_w_out"][0], 1024, 1024, "cwout")
        elif L == 1:
            W["diff_qkv"] = self.conv_lhsT(I["diff_w_qkv"][0], 1024, 3072, "dqkv")
            W["diff_out"] = self.conv_rhs(I["diff_w_out"][0], 1024, 1024, "dwout")
        elif L == 2:
            W["na_qkv"] = self.conv_lhsT(I["na_w_qkv"][0], 1024, 3072, "nqkv")
            W["na_out"] = self.conv_rhs(I["na_w_out"][0], 1024, 1024, "nwout")
        elif L == 3:
            W["mla_a"] = self.conv_rhs(I["mla_w_a"][0], 1024, 416, "mwa")
            W["mla_uq"] = self.conv_rhs(I["mla_w_uq"][0], 256, 1536, "muq")
            W["mla_ukv"] = self.conv_rhs(I["mla_w_ukv"][0], 128, 2048, "mukv")
            W["mla_out"] = self.conv_rhs(I["mla_w_out"][0], 1024, 1024, "mwout")
        if self.do_ffn:
            S = self.S
            scr = self.scratch([NJ, 128, NCH, 256], name=f"wgu{L}")
            ds = S.dsem("wgu")
            t = Tile("wgu")
            w = I["ffn_w_gu"][L]
            for j in range(NJ):
                for half in range(2):
                    src = w[:, half * DFF + j * 128: half * DFF + (j + 1) * 128].rearrange("(kc p) n -> p kc n", p=128)
                    S.dma("pool", scr[j, :, :, half * 128:(half + 1) * 128], src, ds, w=[t])
            t.writer = (ds.key, ds.total, "dma")
            W[f"gu{L}"] = (scr, t)
            W[f"down{L}"] = self.conv_rhs(I["ffn_w_down"][L], DFF, 1024, f"wdn{L}")

    def load_lnp(self, st, L, which):
        S, I = self.S, self.I
        self.lnp = self.sb(st, [128, 2, D], F32, "lnp")
        self.t_lnp = Tile("lnp")
        ds = S.pds("lnp")
        for i, k in enumerate([f"ln{which}_g", f"ln{which}_b"]):
            S.dma("sp", self.lnp[:, i, :], I[k][L].partition_broadcast(128), ds, w=[self.t_lnp])

    def load_x(self, s):
        S, I = self.S, self.I
        xs = I["x"][s].rearrange("(t p) d -> p t d", p=128)
        for t in range(NT):
            S.dma("sp", self.x[:, t, :], xs[:, t, :], self.xin_ds[0], w=[self.x_t[t]])
        S.group_done(self.xin_ds[0], self.x_t)
        with ExitStack() as st:
            xb = [self.sb(st, [128, D], BF16, "xb") for _ in range(2)]
            xb_t = [Tile("xb") for _ in range(2)]
            tp = [self.ps(st, [128, NCH, 128], BF16, "tp") for _ in range(2)]
            tp_t = [Tile("tp") for _ in range(2)]
            for t in range(NT):
                b = t % 2
                self.to_featmajor(t, xb[b], xb_t[b], tp[b], tp_t[b], "act" if t % 2 else "dve")
            S.barrier()

    def to_featmajor(self, t, xb, xb_t, tp, tp_t, eng):
        S = self.S
        if eng == "act":
            S.op("act", lambda e: e.copy(out=xb, in_=self.x[:, t, :]), r=[self.x_t[t]], w=[xb_t])
        else:
            S.op("dve", lambda e: e.tensor_copy(out=xb, in_=self.x[:, t, :]), r=[self.x_t[t]], w=[xb_t])
        for c in range(NCH):
            S.op("pe", lambda e, c=c: e.transpose(out=tp[:, c, :], in_=xb[:, c * 128:(c + 1) * 128], identity=self.ident),
                 r=[xb_t, self.t_const], w=[tp_t])
        dst = self.xT[:, :, t * 128:(t + 1) * 128]
        if eng == "act":
            S.op("dve", lambda e: e.tensor_copy(out=dst, in_=tp), r=[tp_t], w=[self.xT_t[t]])
        else:
            S.op("act", lambda e: e.copy(out=dst, in_=tp), r=[tp_t], w=[self.xT_t[t]])

    def store_x(self, s):
        S = self.S
        os_ = self.out[s].rearrange("(t p) d -> p t d", p=128)
        for t in range(NT):
            S.dma("sp", os_[:, t, :], self.x[:, t, :], self.out_ds[t % 4], r=[self.x_t[t]])

    def ln_epilogue(self, t, y_ps, y_t, gi, L, W, store=None):
        S = self.S
        k = W["k"]
        W["k"] += 1
        b = k % 3
        z, z_t = W["z"][b], W["z_t"][b]
        st6, st6_t = W["st"][b], W["st_t"][b]
        xt = self.x[:, t, :]
        S.op("dve", lambda e: e.scalar_tensor_tensor(out=z, in0=xt, scalar=ALPHA, in1=y_ps, op0=ALU.mult, op1=ALU.add),
             r=[self.x_t[t]] + y_t, w=[z_t])
        for h in range(2):
            S.op("dve", lambda e, h=h: e.bn_stats(out=st6[:, h * 6:(h + 1) * 6], in_=z[:, h * 512:(h + 1) * 512]),
                 r=[z_t], w=[st6_t])
        S.op("dve", lambda e: e.bn_aggr(out=st6[:, 12:14], in_=st6[:, 0:12]), r=[st6_t], w=[st6_t])
        S.op("dve", lambda e: e.tensor_scalar(out=st6[:, 13:14], in0=st6[:, 13:14], scalar1=LN_EPS, scalar2=None,
                                              op0=ALU.add), r=[st6_t], w=[st6_t])
        S.op("act", lambda e: e.activation(out=st6[:, 14:15], in_=st6[:, 13:14], func=AF.Ln), r=[st6_t], w=[st6_t])
        S.op("act", lambda e: e.activation(out=st6[:, 14:15], in_=st6[:, 14:15], func=AF.Exp, scale=-0.5),
             r=[st6_t], w=[st6_t])
        S.op("dve", lambda e: e.scalar_tensor_tensor(out=st6[:, 15:16], in0=st6[:, 12:13], scalar=-1.0, in1=st6[:, 14:15],
                                                     op0=ALU.mult, op1=ALU.mult), r=[st6_t], w=[st6_t])
        S.op("act", lambda e: e.activation(out=z, in_=z, func=AF.Identity, bias=st6[:, 15:16], scale=st6[:, 14:15]),
             r=[z_t, st6_t], w=[z_t])
        g = self.lnp[:, 0, :]
        bb = self.lnp[:, 1, :]
        S.op("pool", lambda e: e.tensor_tensor(out=z, in0=z, in1=g, op=ALU.mult), r=[z_t, self.t_lnp], w=[z_t])
        S.op("dve", lambda e: e.tensor_tensor(out=xt, in0=z, in1=bb, op=ALU.add), r=[z_t, self.t_lnp], w=[self.x_t[t]])
        if store is not None:
            s, dsl = store
            os_ = self.out[s].rearrange("(t p) d -> p t d", p=128)
            S.dma("sp", os_[:, t, :], xt, dsl[t % 4], r=[self.x_t[t]])
            return None
        xb, xb_t = W["xb"][b], W["xb_t"][b]
        S.op("act", lambda e: e.copy(out=xb, in_=xt), r=[self.x_t[t]], w=[xb_t])

        def part_b():
            kb = W["kb"]
            W["kb"] += 1
            tp, tp_t = W["tp"][kb % 2], W["tp_t"][kb % 2]
            for c in range(NCH):
                S.op("pe", lambda e, c=c: e.transpose(out=tp[:, c, :], in_=xb[:, c * 128:(c + 1) * 128], identity=self.ident),
                     r=[xb_t, self.t_const], w=[tp_t])
            dst = self.xT[:, :, t * 128:(t + 1) * 128]
            if kb % 2:
                S.op("dve", lambda e: e.tensor_copy(out=dst, in_=tp), r=[tp_t], w=[self.xT_t[t]])
            else:
                S.op("act", lambda e: e.copy(out=dst, in_=tp), r=[tp_t], w=[self.xT_t[t]])
        return part_b

    def ln_scratch(self, st):
        W = {"k": 0, "kb": 0, "pending": []}
        W["z"] = [self.sb(st, [128, D], F32, "z") for _ in range(3)]
        W["z_t"] = [Tile("z") for _ in range(3)]
        W["st"] = [self.sb(st, [128, 16], F32, "st") for _ in range(3)]
        W["st_t"] = [Tile("st") for _ in range(3)]
        W["xb"] = [self.sb(st, [128, D], BF16, "xb") for _ in range(3)]
        W["xb_t"] = [Tile("xb") for _ in range(3)]
        W["tp"] = [self.ps(st, [128, NCH, 128], BF16, "tp") for _ in range(2)]
        W["tp_t"] = [Tile("tp") for _ in range(2)]
        return W

    def ln_push(self, W, pb):
        if pb is not None:
            W["pending"].append(pb)
        while len(W["pending"]) > 2:
            W["pending"].pop(0)()

    def ln_pop(self, W, n=1):
        for _ in range(n):
            if W["pending"]:
                W["pending"].pop(0)()

    def out_proj_ln(self, wkey, gi, L):
        S = self.S
        scr, wt = self.W[wkey]
        with ExitStack() as st:
            self.load_lnp(st, L, 1)
            w_sb = self.sb(st, [128, NCH, D], BF16, "wout")
            w_t = Tile("wout")
            ds = S.pds("wout")
            for c in range(0, NCH, 2):
                S.dma("sp", w_sb[:, c:c + 2, :], scr[:, c:c + 2, :], ds, r=[wt], w=[w_t])
            LW = self.ln_scratch(st)
            yps = [self.ps(st, [128, D], F32, "y") for _ in range(2)]
            y_t = [[Tile("y0"), Tile("y1")] for _ in range(2)]
            for t in range(NT):
                b = t % 2
                for h in range(2):
                    for c in range(NCH):
                        S.op("pe", lambda e, c=c, h=h: e.matmul(yps[b][:, h * 512:(h + 1) * 512],
                                                              self.oT[:, c, t * 128:(t + 1) * 128],
                                                              w_sb[:, c, h * 512:(h + 1) * 512],
                                                              start=(c == 0), stop=(c == NCH - 1)),
                             r=[self.oT_t[t], w_t], w=[y_t[b][h]])
                self.ln_push(LW, self.ln_epilogue(t, yps[b], y_t[b], gi, L, LW))
            self.ln_pop(LW, 2)
            S.barrier()

    def ffn(self, L, s, last):
        S = self.S
        gscr, gt = self.W[f"gu{L}"]
        dscr, dt_ = self.W[f"down{L}"]
        with ExitStack() as st:
            self.load_lnp(st, L, 2)
            wd = self.sb(st, [128, NJ, D], BF16, "wd")
            wd_t = Tile("wd")
            ds = S.pds("wd")
            for j in range(0, NJ, 2):
                S.dma("sp", wd[:, j:j + 2, :], dscr[:, j:j + 2, :], ds, r=[dt_], w=[wd_t])
            NSLOT = 3
            ring = [self.sb(st, [128, NCH, 256], BF16, "wgu") for _ in range(NSLOT)]
            ring_t = [Tile("wgu") for _ in range(NSLOT)]
            ring_ds = [S.pds("wgu") for _ in range(NSLOT)]
            hT = self.sb(st, [128, NJ, 512], BF16, "hT")
            hT_t = [Tile("hT") for _ in range(4)]
            sg = [self.sb(st, [128, 512], F32, "sg") for _ in range(2)]
            sg_t = [Tile("sg") for _ in range(2)]
            gps = self.ps(st, [128, 512], F32, "g")
            ups = self.ps(st, [128, 512], F32, "u")
            g_t, u_t = Tile("g"), Tile("u")
            LW = self.ln_scratch(st)
            yps = [self.ps(st, [128, D], F32, "y") for _ in range(2)]
            y_t = [[Tile("y0"), Tile("y1")] for _ in range(2)]
            nload = 0
            total = 4 * NJ

            def issue(i):
                j = i % NJ
                sl = i % NSLOT
                S.dma("sp", ring[sl], gscr[j], ring_ds[sl], r=[gt], w=[ring_t[sl]])

            for i in range(min(NSLOT - 1, total)):
                issue(i)
                nload += 1
            it = 0
            for blk in range(4):
                xts = self.xT_t[blk * 4:(blk + 1) * 4]
                for j in range(NJ):
                    if nload < total:
                        issue(nload)
                        nload += 1
                    sl = it % NSLOT
                    it += 1
                    for c in range(NCH):
                        S.op("pe", lambda e, c=c: e.matmul(gps, ring[sl][:, c, 0:128], self.xT[:, c, blk * 512:(blk + 1) * 512],
                                                         start=(c == 0), stop=(c == NCH - 1)),
                             r=[ring_t[sl]] + xts, w=[g_t])
                    for c in range(NCH):
                        S.op("pe", lambda e, c=c: e.matmul(ups, ring[sl][:, c, 128:256], self.xT[:, c, blk * 512:(blk + 1) * 512],
                                                         start=(c == 0), stop=(c == NCH - 1)),
                             r=[ring_t[sl]] + xts, w=[u_t])
                    if j in (2, 5):
                        self.ln_pop(LW, 1)
                    b = j % 2
                    S.op("act", lambda e: e.activation(out=sg[b], in_=gps, func=AF.Silu), r=[g_t], w=[sg_t[b]])
                    S.op("dve", lambda e: e.tensor_tensor(out=hT[:, j, :], in0=sg[b], in1=ups, op=ALU.mult),
                         r=[sg_t[b], u_t], w=hT_t)
                for tt in range(4):
                    t = blk * 4 + tt
                    b = t % 2
                    for h in range(2):
                        for j in range(NJ):
                            S.op("pe", lambda e, j=j, h=h: e.matmul(yps[b][:, h * 512:(h + 1) * 512],
                                                                  hT[:, j, tt * 128:(tt + 1) * 128],
                                                                  wd[:, j, h * 512:(h + 1) * 512],
                                                                  start=(j == 0), stop=(j == NJ - 1)),
                                 r=[hT_t[tt], wd_t], w=[y_t[b][h]])
                    self.ln_push(LW, self.ln_epilogue(t, yps[b], y_t[b], 2, L, LW, store=(s, self.out_ds) if last else None))
            self.ln_pop(LW, 2)
            S.barrier()

    def layer_conv(self, s):
        S, I = self.S, self.I
        scr, wt = self.W["conv_in"]
        with ExitStack() as ost:
          self.oT = self.sb(ost, [128, NCH, S_LEN], BF16, "oT")
          with ExitStack() as st:
            cw = self.sb(st, [128, 3, NCH], F32, "cw")
            cw_t = Tile("cw")
            ds = S.pds("cw")
            S.dma("sp", cw, I["conv_w"][0].rearrange("t (c p) -> p t c", p=128), ds, w=[cw_t],
                  allow_slow_non_contiguous=True)
            NSLOT = 6
            ring = [self.sb(st, [128, NCH, 128], BF16, "win") for _ in range(NSLOT)]
            ring_t = [Tile("win") for _ in range(NSLOT)]
            ring_ds = [S.pds("win") for _ in range(NSLOT)]
            u = [self.sb(st, [128, S_LEN + 2], F32, "u") for _ in range(2)]
            u_t = [Tile("u") for _ in range(2)]
            bg = [self.sb(st, [128, S_LEN], F32, "bg") for _ in range(2)]
            bg_t = [Tile("bg") for _ in range(2)]
            y = self.sb(st, [128, S_LEN], F32, "y")
            y_t = Tile("y")
            cgs = [self.sb(st, [128, 512], F32, "cgs") for _ in range(2)]
            cgs_t = [Tile("cgs") for _ in range(2)]
            pss = [[self.ps(st, [128, 512], F32, "cps") for _ in range(3)] for _ in range(2)]
            pss_t = [[Tile("cps") for _ in range(3)] for _ in range(2)]
            for b in range(2):
                S.op("pool", lambda e, b=b: e.memset(u[b][:, 0:1], 0.0), w=[u_t[b]])
                S.op("pool", lambda e, b=b: e.memset(u[b][:, S_LEN + 1:S_LEN + 2], 0.0), w=[u_t[b]])
            order = [(c, kind) for c in range(NCH) for kind in range(3)]

            def issue(i):
                c, kind = order[i]
                sl = i % NSLOT
                S.dma("sp", ring[sl], scr[kind * 8 + c], ring_ds[sl], r=[wt], w=[ring_t[sl]])

            nload = 0
            for i in range(NSLOT - 3):
                issue(i)
                nload += 1
            kk = 0
            for c in range(NCH):
                for _ in range(3):
                    if nload < len(order):
                        issue(nload)
                        nload += 1
                ub = c % 2
                for blk in range(4):
                    pb = kk % 2
                    kk += 1
                    xts = self.xT_t[blk * 4:(blk + 1) * 4]
                    for kind in range(3):
                        sl = (c * 3 + kind) % NSLOT
                        for k in range(NCH):
                            S.op("pe", lambda e, k=k, kind=kind, sl=sl: e.matmul(
                                pss[pb][kind], ring[sl][:, k, :], self.xT[:, k, blk * 512:(blk + 1) * 512],
                                start=(k == 0), stop=(k == NCH - 1)),
                                r=[ring_t[sl]] + xts, w=[pss_t[pb][kind]])
                    S.op("act", lambda e: e.copy(out=bg[ub][:, blk * 512:(blk + 1) * 512], in_=pss[pb][0]),
                         r=[pss_t[pb][0]], w=[bg_t[ub]])
                    S.op("act", lambda e: e.copy(out=cgs[pb], in_=pss[pb][1]), r=[pss_t[pb][1]], w=[cgs_t[pb]])
                    S.op("dve", lambda e: e.tensor_tensor(out=u[ub][:, 1 + blk * 512:1 + (blk + 1) * 512], in0=cgs[pb],
                                                          in1=pss[pb][2], op=ALU.mult),
                         r=[cgs_t[pb], pss_t[pb][2]], w=[u_t[ub]])
                uu = u[ub]
                S.op("act", lambda e: e.activation(out=y, in_=uu[:, 1:S_LEN + 1], func=AF.Copy, scale=cw[:, 1, c:c + 1]),
                     r=[u_t[ub], cw_t], w=[y_t])
                S.op("dve", lambda e: e.scalar_tensor_tensor(out=y, in0=uu[:, 0:S_LEN], scalar=cw[:, 0, c:c + 1], in1=y,
                                                             op0=ALU.mult, op1=ALU.add), r=[u_t[ub], cw_t, y_t], w=[y_t])
                S.op("dve", lambda e: e.scalar_tensor_tensor(out=y, in0=uu[:, 2:S_LEN + 2], scalar=cw[:, 2, c:c + 1], in1=y,
                                                             op0=ALU.mult, op1=ALU.add), r=[u_t[ub], cw_t, y_t], w=[y_t])
                S.op("pool", lambda e: e.tensor_tensor(out=self.oT[:, c, :], in0=bg[ub], in1=y, op=ALU.mult),
                     r=[bg_t[ub], y_t], w=self.oT_t)
            S.barrier()
          self.out_proj_ln("conv_out", 0, 0)

    def layer_diff(self, s):
        S, I = self.S, self.I
        scr, wt = self.W["diff_qkv"]
        lam_init = 0.8 - 0.6 * math.exp(-0.3 * 1)
        with ExitStack() as ost:
          self.oT = self.sb(ost, [128, NCH, S_LEN], BF16, "oT")
          with ExitStack() as st:
            cds = S.pds("dc")
            alibi = self.sb(st, [128, 3968], F32, "alibi")
            al_t = Tile("alibi")
            S.dma("sp", alibi, I["alibi"], cds, w=[al_t])
            lam_sb = self.sb(st, [128, 4, 64], F32, "lam")
            gsub = self.sb(st, [128, 1], F32, "gsub")
            sm = self.sb(st, [128, 8], F32, "sm")
            prm_t = Tile("prm")
            S.dma("sp", lam_sb, I["diff_lambda"][0].partition_broadcast(128), cds, w=[prm_t])
            S.dma("sp", gsub, I["diff_subln_g"][0].rearrange("(p o) -> p o", o=1), cds, w=[prm_t])
            S.group_done(cds, [al_t, prm_t])
            lp = self.sb(st, [128, 2, 64], F32, "lp")
            S.op("dve", lambda e: e.tensor_tensor(out=lp, in0=lam_sb[:, 0:4:2, :], in1=lam_sb[:, 1:4:2, :], op=ALU.mult),
                 r=[prm_t], w=[prm_t])
            S.op("dve", lambda e: e.reduce_sum(out=sm[:, 0:2], in_=lp, axis=mybir.AxisListType.X), r=[prm_t], w=[prm_t])
            S.op("act", lambda e: e.activation(out=sm[:, 2:4], in_=sm[:, 0:2], func=AF.Exp), r=[prm_t], w=[prm_t])
            S.op("dve", lambda e: e.tensor_tensor(out=sm[:, 4:5], in0=sm[:, 3:4], in1=sm[:, 2:3], op=ALU.subtract),
                 r=[prm_t], w=[prm_t])
            S.op("dve", lambda e: e.tensor_scalar(out=sm[:, 5:6], in0=sm[:, 4:5], scalar1=-lam_init, scalar2=None, op0=ALU.add),
                 r=[prm_t], w=[prm_t])
            S.op("dve", lambda e: e.tensor_scalar(out=gsub, in0=gsub, scalar1=1.0 - lam_init, scalar2=None, op0=ALU.mult),
                 r=[prm_t], w=[prm_t])
            neglam = sm[:, 5:6]
            gs_col = gsub[:, 0:1]

            NSLOT = 6
            ring = [self.sb(st, [128, NCH, 128], BF16, "wqkv") for _ in range(NSLOT)]
            ring_t = [Tile("wqkv") for _ in range(NSLOT)]
            ring_ds = [S.pds("wqkv") for _ in range(NSLOT)]
            qT = self.sb(st, [128, S_LEN], BF16, "qT")
            kT = self.sb(st, [128, S_LEN], BF16, "kT")
            vA = self.sb(st, [128, NT, 129], BF16, "vA")
            qT_t, kT_t, vA_t = Tile("qT"), Tile("kT"), Tile("vA")
            S.op("pool", lambda e: e.memset(vA[:, :, 128:129], 1.0), w=[vA_t])
            NSB = 4
            sbs = [self.sb(st, [128, 512], F32, "scs") for _ in range(NSB)]
            sbs_t = [Tile("scs") for _ in range(NSB)]
            NPT = 8
            pT = [self.sb(st, [128, 512], BF16, "pT") for _ in range(NPT)]
            pT_t = [Tile("pT") for _ in range(NPT)]
            scp = [self.ps(st, [128, 512], F32, "scp") for _ in range(4)]
            scp_t = [Tile("scp") for _ in range(4)]
            accO = [self.ps(st, [128, 512], F32, "accO") for _ in range(2)]
            accZ = [self.ps(st, [128, 512], F32, "accZ") for _ in range(2)]
            accO_t = [Tile("accO") for _ in range(2)]
            accZ_t = [Tile("accZ") for _ in range(2)]
            ones128 = self.sb(st, [128, 128], BF16, "ones128")
            ones_t = Tile("ones")
            S.op("pool", lambda e: e.memset(ones128, 1.0), w=[ones_t])
            z0s = self.sb(st, [128, 512], F32, "z0s")
            z1s = self.sb(st, [128, 512], F32, "z1s")
            t0 = self.sb(st, [128, 512], F32, "t0")
            t1 = self.sb(st, [128, 512], F32, "t1")
            sqb = self.sb(st, [128, 512], BF16, "sqb")
            ew_t, ew2_t, ew3_t, sq_t = Tile("ew"), Tile("ew2"), Tile("ew3"), Tile("sq")

            def issue(i):
                h, kind = divmod(i, 3)
                sl = i % NSLOT
                S.dma("sp", ring[sl], scr[kind * 8 + h], ring_ds[sl], r=[wt], w=[ring_t[sl]])

            for i in range(3):
                issue(i)
            rot = [0]

            def next_scp():
                i = rot[0] % 4
                rot[0] += 1
                return i

            for h in range(8):
                if h + 1 < 8:
                    for kind in range(3):
                        issue((h + 1) * 3 + kind)
                slq, slk, slv = (h * 3) % NSLOT, (h * 3 + 1) % NSLOT, (h * 3 + 2) % NSLOT
                slope = 2.0 ** (-(h + 1))
                for blk in range(4):
                    xts = self.xT_t[blk * 4:(blk + 1) * 4]
                    for (sl, dst, dst_t, sc) in ((slq, qT, qT_t, 0.125), (slk, kT, kT_t, 1.0)):
                        pi = next_scp()
                        for k in range(NCH):
                            S.op("pe", lambda e, k=k: e.matmul(scp[pi], ring[sl][:, k, :], self.xT[:, k, blk * 512:(blk + 1) * 512],
                                                             start=(k == 0), stop=(k == NCH - 1)),
                                 r=[ring_t[sl]] + xts, w=[scp_t[pi]])
                        S.op("act", lambda e: e.mul(out=dst[:, blk * 512:(blk + 1) * 512], in_=scp[pi], mul=sc),
                             r=[scp_t[pi]], w=[dst_t])
                    pi = next_scp()
                    for tt in range(4):
                        t = blk * 4 + tt
                        for k in range(NCH):
                            S.op("pe", lambda e, k=k: e.matmul(scp[pi][:, tt * 128:(tt + 1) * 128], self.xT[:, k, t * 128:(t + 1) * 128],
                                                             ring[slv][:, k, :], start=(k == 0), stop=(k == NCH - 1)),
                                 r=[ring_t[slv], self.xT_t[t]], w=[scp_t[pi]])
                    S.op("dve", lambda e: e.tensor_copy(out=vA[:, blk * 4:(blk + 1) * 4, 0:128],
                                                        in_=scp[pi].rearrange("p (a b) -> p a b", a=4)),
                         r=[scp_t[pi]], w=[vA_t])
                steps = [(qb, kc) for qb in range(4) for kc in range(16)]
                LAG = 3
                fifo = []
                cnt = 0
                for i in range(len(steps) + LAG):
                    if i < len(steps):
                        qb, kc = steps[i]
                        c0 = 512 * qb - 128 * kc + 1920
                        cur = []
                        for m in range(2):
                            pi = next_scp()
                            S.op("pe", lambda e: e.matmul(scp[pi], kT[m * 64:(m + 1) * 64, kc * 128:(kc + 1) * 128],
                                                          qT[m * 64:(m + 1) * 64, qb * 512:(qb + 1) * 512], start=True, stop=True),
                                 r=[kT_t, qT_t], w=[scp_t[pi]])
                            si = cnt % NSB
                            pj = cnt % NPT
                            cnt += 1
                            S.op("dve", lambda e: e.scalar_tensor_tensor(out=sbs[si], in0=alibi[:, c0:c0 + 512], scalar=-slope,
                                                                         in1=scp[pi], op0=ALU.mult, op1=ALU.add),
                                 r=[al_t, scp_t[pi]], w=[sbs_t[si]])
                            S.op("act", lambda e: e.activation(out=pT[pj], in_=sbs[si], func=AF.Exp), r=[sbs_t[si]], w=[pT_t[pj]])
                            cur.append(pj)
                        fifo.append((qb, kc, cur))
                    if i >= LAG:
                        pqb, pkc, pjs = fifo.pop(0)
                        for m in range(2):
                            pj = pjs[m]
                            S.op("pe", lambda e: e.matmul(accO[m], vA[:, pkc, 0:128], pT[pj], start=(pkc == 0), stop=(pkc == 15)),
                                 r=[pT_t[pj], vA_t], w=[accO_t[m]])
                            S.op("pe", lambda e: e.matmul(accZ[m], ones128, pT[pj], start=(pkc == 0), stop=(pkc == 15)),
                                 r=[pT_t[pj], ones_t], w=[accZ_t[m]])
                        if pkc == 15:
                            cols = slice(pqb * 512, (pqb + 1) * 512)
                            S.op("act", lambda e: e.copy(out=z0s, in_=accZ[0]), r=[accZ_t[0]], w=[ew_t])
                            S.op("act", lambda e: e.copy(out=z1s, in_=accZ[1]), r=[accZ_t[1]], w=[ew_t])
                            S.op("dve", lambda e: e.tensor_tensor(out=t0, in0=accO[0], in1=z1s, op=ALU.mult),
                                 r=[accO_t[0], ew_t], w=[ew2_t])
                            S.op("dve", lambda e: e.tensor_tensor(out=t1, in0=accO[1], in1=z0s, op=ALU.mult),
                                 r=[accO_t[1], ew_t], w=[ew3_t])
                            S.op("dve", lambda e: e.scalar_tensor_tensor(out=t0, in0=t1, scalar=neglam, in1=t0,
                                                                         op0=ALU.mult, op1=ALU.add),
                                 r=[ew2_t, ew3_t, prm_t], w=[ew2_t])
                            S.op("pool", lambda e: e.tensor_tensor(out=z0s, in0=z0s, in1=z1s, op=ALU.mult), r=[ew_t], w=[ew_t])
                            S.op("pool", lambda e: e.tensor_tensor(out=z0s, in0=z0s, in1=z0s, op=ALU.mult), r=[ew_t], w=[ew_t])
                            S.op("act", lambda e: e.activation(out=sqb, in_=t0, func=AF.Square), r=[ew2_t], w=[sq_t])
                            pi = next_scp()
                            S.op("pe", lambda e: e.matmul(scp[pi], ones128, sqb, start=True, stop=True),
                                 r=[sq_t, ones_t], w=[scp_t[pi]])
                            S.op("dve", lambda e: e.scalar_tensor_tensor(out=t1, in0=z0s, scalar=RMS_EPS * 128.0, in1=scp[pi],
                                                                         op0=ALU.mult, op1=ALU.add),
                                 r=[ew_t, scp_t[pi]], w=[ew3_t])
                            S.op("act", lambda e: e.activation(out=t1, in_=t1, func=AF.Ln, scale=1.0 / 128.0), r=[ew3_t], w=[ew3_t])
                            S.op("act", lambda e: e.activation(out=t1, in_=t1, func=AF.Exp, scale=-0.5), r=[ew3_t], w=[ew3_t])
                            S.op("dve", lambda e: e.scalar_tensor_tensor(out=self.oT[:, h, cols], in0=t0, scalar=gs_col, in1=t1,
                                                                         op0=ALU.mult, op1=ALU.mult),
                                 r=[ew2_t, ew3_t, prm_t], w=self.oT_t[pqb * 4:(pqb + 1) * 4])
            S.barrier()
          self.out_proj_ln("diff_out", 0, 1)

    def layer_na(self, s):
        S, I = self.S, self.I
        scr, wt = self.W["na_qkv"]
        rpbg = I["na_rpbg"]
        with ExitStack() as ost:
          self.oT = self.sb(ost, [128, NCH, S_LEN], BF16, "oT")
          with ExitStack() as st:
            cds = S.pds("nc")
            M2 = self.sb(st, [128, 14, 16, 64], BF16, "M2")
            M2_t = Tile("M2")
            mask = self.sb(st, [128, 64], F32, "mask")
            mask_t = Tile("mask")
            S.dma("sp", mask, I["na_mask"], cds, w=[mask_t])
            NSLOT = 6
            ring = [self.sb(st, [128, NCH, 128], BF16, "wqkv") for _ in range(NSLOT)]
            ring_t = [Tile("wqkv") for _ in range(NSLOT)]
            ring_ds = [S.pds("wqkv") for _ in range(NSLOT)]
            qbd = self.sb(st, [128, 32, 128], BF16, "qbd")
            kT = self.sb(st, [128, S_LEN], BF16, "kT")
            vE = self.sb(st, [128, NT, 2, 65], BF16, "vE")
            vO = self.sb(st, [128, NT - 1, 2, 65], BF16, "vO")
            qT_t, kT_t, vE_t, vO_t = Tile("qT"), Tile("kT"), Tile("vE"), Tile("vO")
            S.op("pool", lambda e: e.memset(qbd, 0.0), w=[qT_t])
            S.op("pool", lambda e: e.memset(vE[:, :, :, 64:65], 1.0), w=[vE_t])
            S.op("pool", lambda e: e.memset(vO[:, :, :, 64:65], 1.0), w=[vO_t])
            NPT = 3
            pT = [self.sb(st, [128, 4, 2, 64], BF16, "pT") for _ in range(NPT)]
            pT_t = [Tile("pT") for _ in range(NPT)]
            ob = [self.sb(st, [64, 8, 128], BF16, "ob") for _ in range(2)]
            ob_t = [Tile("ob") for _ in range(2)]
            rz = self.sb(st, [64, 4], F32, "rz")
            rz_t = Tile("rz")
            scp = [self.ps(st, [128, 512], F32, "scp") for _ in range(4)]
            scp_t = [Tile("scp") for _ in range(4)]
            ops_ = [self.ps(st, [128, 4, 128], F32, "ops")[0:64, 0:2, :] for _ in range(2)]
            ops_t = [Tile("ops") for _ in range(2)]
            tps = [self.ps(st, [128, 1024], BF16, "tps")[:, 0:512] for _ in range(2)]
            tps_t = [Tile("tps") for _ in range(2)]
            with ExitStack() as st2:
                stage = [self.sb(st2, [128, 2, 1024], F32, "stage") for _ in range(1)]
                stage_t = [Tile("stage") for _ in range(1)]
                sds = [S.pds("stage") for _ in range(1)]
                for i in range(7):
                    d0 = 2 * i
                    b = 0
                    S.dma("sp", stage[b][0:64], rpbg[d0:d0 + 2].rearrange("d t h c -> t d (h c)"), sds[b], w=[stage_t[b]])
                    S.dma("sp", stage[b][64:128], rpbg[d0 + 1:d0 + 3].rearrange("d t h c -> t d (h c)"), sds[b], w=[stage_t[b]])
                    S.op("dve", lambda e: e.tensor_tensor(
                        out=M2[:, d0:d0 + 2, :, :].rearrange("p d h c -> p (d h) c"),
                        in0=stage[b].rearrange("p d (h c) -> p (d h) c", c=64),
                        in1=mask.unsqueeze(1).to_broadcast([128, 32, 64]), op=ALU.add),
                        r=[stage_t[b], mask_t], w=[M2_t])
                S.barrier()

            def issue(i):
                c, kind = divmod(i, 3)
                sl = i % NSLOT
                S.dma("sp", ring[sl], scr[kind * 8 + c], ring_ds[sl], r=[wt], w=[ring_t[sl]])

            for i in range(3):
                issue(i)
            rot = [0]

            def next_scp():
                i = rot[0] % 4
                rot[0] += 1
                return i

            ident = self.ident
            M2v = M2.rearrange("p d h c -> p d (h c)")
            for c in range(NCH):
                if c + 1 < NCH:
                    for kind in range(3):
                        issue((c + 1) * 3 + kind)
                slq, slk, slv = (c * 3) % NSLOT, (c * 3 + 1) % NSLOT, (c * 3 + 2) % NSLOT
                for blk in range(4):
                    xts = self.xT_t[blk * 4:(blk + 1) * 4]
                    for isq, sl in ((True, slq), (False, slk)):
                        pi = next_scp()
                        for k in range(NCH):
                            S.op("pe", lambda e, k=k: e.matmul(scp[pi], ring[sl][:, k, :], self.xT[:, k, blk * 512:(blk + 1) * 512],
                                                             start=(k == 0), stop=(k == NCH - 1)),
                                 r=[ring_t[sl]] + xts, w=[scp_t[pi]])
                        if isq:
                            for hh in range(2):
                                S.op("act", lambda e, hh=hh: e.mul(
                                    out=qbd[hh * 64:(hh + 1) * 64, blk * 8:(blk + 1) * 8, hh * 64:(hh + 1) * 64],
                                    in_=scp[pi][hh * 64:(hh + 1) * 64, :].rearrange("p (r c) -> p r c", r=8), mul=0.125),
                                    r=[scp_t[pi]], w=[qT_t])
                        else:
                            S.op("act", lambda e: e.copy(out=kT[:, blk * 512:(blk + 1) * 512], in_=scp[pi]),
                                 r=[scp_t[pi]], w=[kT_t])
                    for (vbuf, vbuf_t, off, ntile) in ((vE, vE_t, 0, 4), (vO, vO_t, 64, 4 if blk < 3 else 3)):
                        pi = next_scp()
                        for tt in range(ntile):
                            t = blk * 4 + tt
                            tok0 = t * 128 + off
                            tl = sorted(set([tok0 // 128, (tok0 + 127) // 128]))
                            for k in range(NCH):
                                S.op("pe", lambda e, k=k: e.matmul(scp[pi][:, tt * 128:(tt + 1) * 128], self.xT[:, k, tok0:tok0 + 128],
                                                                 ring[slv][:, k, :], start=(k == 0), stop=(k == NCH - 1)),
                                     r=[ring_t[slv]] + [self.xT_t[i] for i in tl], w=[scp_t[pi]])
                        S.op("dve", lambda e: e.tensor_copy(
                            out=vbuf[:, blk * 4:blk * 4 + ntile, :, 0:64],
                            in_=scp[pi][:, 0:ntile * 128].rearrange("p (a h d) -> p a h d", a=ntile, h=2)),
                            r=[scp_t[pi]], w=[vbuf_t])
                def stage_a(r):
                    rs = min(max(r - 4, 0), 24)
                    d0b = rs - r + 7
                    pj = r % NPT
                    pi = next_scp()
                    scv = scp[pi].rearrange("p (j c) -> p j c", j=4)
                    S.op("pe", lambda e: e.matmul(scv, ident, M2v[:, d0b:d0b + 7:2, 2 * c * 64:2 * c * 64 + 128], start=True,
                                                  stop=False, skip_group_check=True),
                         r=[M2_t, self.t_const], w=[scp_t[pi]])
                    for j in range(4):
                        ks = (rs + 2 * j) * 64
                        S.op("pe", lambda e: e.matmul(scv[:, j, :], kT[:, ks:ks + 128], qbd[:, r, :], start=False,
                                                      stop=(j == 3), skip_group_check=True),
                             r=[kT_t, qT_t], w=[scp_t[pi]])
                    S.op("act", lambda e: e.activation(out=pT[pj].rearrange("p j h c -> p (j h c)"), in_=scp[pi], func=AF.Exp),
                         r=[scp_t[pi]], w=[pT_t[pj]])

                def stage_b(r):
                    rs = min(max(r - 4, 0), 24)
                    pj = r % NPT
                    oi = r % 2
                    for j in range(4):
                        kr = rs + 2 * j
                        if kr % 2 == 0:
                            vb, vb_t, vt = vE, vE_t, kr // 2
                        else:
                            vb, vb_t, vt = vO, vO_t, (kr - 1) // 2
                        for hh in range(2):
                            S.op("pe", lambda e: e.matmul(ops_[oi][:, hh, 0:65], pT[pj][:, j, hh, :], vb[:, vt, hh, :],
                                                          start=(j == 0 and hh == 0), stop=(j == 3 and hh == 1), skip_group_check=True),
                                 r=[pT_t[pj], vb_t], w=[ops_t[oi]])
                    g8, r8 = divmod(r, 8)
                    bi = g8 % 2
                    S.op("dve", lambda e: e.reciprocal(out=rz[:, 2 * oi:2 * oi + 2], in_=ops_[oi][:, :, 64]), r=[ops_t[oi]], w=[rz_t])
                    S.op("dve", lambda e: e.tensor_tensor(out=ob[bi][:, r8, :].rearrange("p (h d) -> p h d", h=2),
                                                          in0=ops_[oi][:, :, 0:64],
                                                          in1=rz[:, 2 * oi:2 * oi + 2].unsqueeze(2).to_broadcast([64, 2, 64]), op=ALU.mult),
                         r=[ops_t[oi], rz_t], w=[ob_t[bi]])

                def stage_c(r):
                    g8, r8 = divmod(r, 8)
                    bi = g8 % 2
                    S.op("pe", lambda e: e.transpose(out=tps[bi][:, r8 * 64:(r8 + 1) * 64], in_=ob[bi][:, r8, :],
                                                     identity=ident[0:64, 0:64]),
                         r=[ob_t[bi], self.t_const], w=[tps_t[bi]])
                    if r8 == 7:
                        S.op("act", lambda e: e.copy(out=self.oT[:, c, g8 * 512:(g8 + 1) * 512], in_=tps[bi]),
                             r=[tps_t[bi]], w=self.oT_t[g8 * 4:(g8 + 1) * 4])

                for i in range(32 + 2):
                    if i < 32:
                        stage_a(i)
                    if 1 <= i <= 32:
                        stage_b(i - 1)
                    if i >= 2:
                        stage_c(i - 2)
            S.barrier()
          self.out_proj_ln("na_out", 0, 2)

    def layer_mla(self, s):
        S, I = self.S, self.I
        ident = self.ident
        sm_scale = 96.0 ** -0.5
        self.oT, self.oT_t = self.xT, self.xT_t
        with ExitStack() as st:
            cds = S.pds("mc")
            cnT = self.sb(st, [128, 3, S_LEN], BF16, "cnT")
            cqnT = cnT[:, 0:2, :]
            ckvnT = cnT[:, 2, :]
            kropeT = self.sb(st, [128, S_LEN], BF16, "kropeT")
            cq_t, ckv_t, krt_t = Tile("cqnT"), Tile("ckvnT"), Tile("kropeT")
            wuq = self.sb(st, [128, 2, 1568], BF16, "wuq")
            wq2 = self.sb(st, [128, 2, 16, 128], BF16, "wq2")
            wukv = self.sb(st, [128, 2048], BF16, "wukv")
            w_t = Tile("mlaw")
            wds = S.pds("mw")
            S.op("pool", lambda e: e.memset(wuq, 0.0), w=[w_t])
            S.dma("sp", wuq[:, :, 0:1536], self.W["mla_uq"][0], wds, r=[self.W["mla_uq"][1]], w=[w_t])
            S.dma("sp", wukv, self.W["mla_ukv"][0][:, 0, :], wds, r=[self.W["mla_ukv"][1]], w=[w_t])
            wuqv = wuq[:, :, 0:1536].rearrange("p k (h d) -> p k h d", h=16)
            S.op("pool", lambda e: e.memset(wq2, 0.0), w=[w_t])
            S.op("dve", lambda e: e.tensor_scalar(out=wq2[:, :, :, 64:80], in0=wuqv[:, :, :, 80:96], scalar1=-1.0, scalar2=None,
                                                  op0=ALU.mult), r=[w_t], w=[w_t])
            S.op("dve", lambda e: e.tensor_copy(out=wq2[:, :, :, 80:96], in_=wuqv[:, :, :, 64:80]), r=[w_t], w=[w_t])
            cosf = self.sb(st, [128, S_LEN], F32, "cosf")
            sinf = self.sb(st, [128, S_LEN], F32, "sinf")
            rp_t = Tile("ropef")
            S.dma("sp", cosf, I["rope_cos_fm"], cds, w=[rp_t])
            S.dma("sp", sinf, I["rope_sin_fm"], cds, w=[rp_t])
            cds_tiles = [rp_t]
            scp = [self.ps(st, [128, 512], F32, "scp") for _ in range(4)]
            scp_t = [Tile("scp") for _ in range(4)]
            rot = [0]

            def next_scp():
                i = rot[0] % 4
                rot[0] += 1
                return i

            with ExitStack() as st2:
                wa = self.sb(st2, [128, NCH, 416], BF16, "wa")
                wa_t = Tile("wa")
                S.dma("sp", wa, self.W["mla_a"][0], cds, r=[self.W["mla_a"][1]], w=[wa_t])
                gq = self.sb(st2, [128, 256], F32, "gq")
                gkv = self.sb(st2, [128, 128], F32, "gkv")
                g_t = Tile("g")
                S.dma("sp", gq, I["mla_g_q"][0].partition_broadcast(128), cds, w=[g_t])
                S.dma("sp", gkv, I["mla_g_kv"][0].partition_broadcast(128), cds, w=[g_t])
                cos_tm = self.sb(st2, [128, NT, 16], F32, "cos_tm")
                sin_tm = self.sb(st2, [128, NT, 16], F32, "sin_tm")
                S.dma("sp", cos_tm, I["rope_cos_tm"], cds, w=[g_t])
                S.dma("sp", sin_tm, I["rope_sin_tm"], cds, w=[g_t])
                S.group_done(cds, [rp_t, wa_t, g_t])
                KRraw = self.sb(st2, [128, NT, 32], F32, "KRraw")
                KR = self.sb(st2, [128, NT, 128], BF16, "KR")
                kr_t = Tile("KR")
                S.op("pool", lambda e: e.memset(KR, 0.0), w=[kr_t])
                junk = self.sb(st2, [128, 256], F32, "junk")
                junk2 = self.sb(st2, [128, 128], F32, "junk2")
                ssq = [self.sb(st2, [128, 8], F32, "ssq") for _ in range(2)]
                ssq_t = [Tile("ssq") for _ in range(2)]
                nb = [self.sb(st2, [128, 384], BF16, "nb") for _ in range(2)]
                nb_t = [Tile("nb") for _ in range(2)]
                tpa = [self.ps(st2, [128, 8, 128], BF16, "tpa") for _ in range(2)]
                tpa_t = [Tile("tpa") for _ in range(2)]
                ADBG = int(os.environ.get("MLA_A", "9"))
                for t in range(NT if ADBG >= 2 else 0):
                    b = t % 2
                    pi = next_scp()
                    aps = scp[pi]
                    for k in range(NCH):
                        S.op("pe", lambda e, k=k: e.matmul(aps[:, 0:416], self.xT[:, k, t * 128:(t + 1) * 128], wa[:, k, :],
                                                         start=(k == 0), stop=(k == NCH - 1)),
                             r=[self.xT_t[t], wa_t], w=[scp_t[pi]])
                    LDBG = int(os.environ.get("MLA_L", "9"))
                    q = ssq[b]
                    if LDBG < 2:
                        continue
                    S.op("act", lambda e: e.activation(out=junk[:, 0:256], in_=aps[:, 0:256], func=AF.Square, accum_out=q[:, 0:1]),
                         r=[scp_t[pi]], w=[ssq_t[b]])
                    S.op("act", lambda e: e.activation(out=junk2, in_=aps[:, 256:384], func=AF.Square, accum_out=q[:, 1:2]),
                         r=[scp_t[pi]], w=[ssq_t[b]])
                    if LDBG < 3:
                        continue
                    S.op("dve", lambda e: e.tensor_scalar(out=q[:, 2:3], in0=q[:, 0:1], scalar1=1.0 / 256.0, scalar2=RMS_EPS,
                                                          op0=ALU.mult, op1=ALU.add), r=[ssq_t[b]], w=[ssq_t[b]])
                    S.op("dve", lambda e: e.tensor_scalar(out=q[:, 3:4], in0=q[:, 1:2], scalar1=1.0 / 128.0, scalar2=RMS_EPS,
                                                          op0=ALU.mult, op1=ALU.add), r=[ssq_t[b]], w=[ssq_t[b]])
                    S.op("act", lambda e: e.activation(out=q[:, 4:6], in_=q[:, 2:4], func=AF.Ln), r=[ssq_t[b]], w=[ssq_t[b]])
                    S.op("act", lambda e: e.activation(out=q[:, 6:8], in_=q[:, 4:6], func=AF.Exp, scale=-0.5),
                         r=[ssq_t[b]], w=[ssq_t[b]])
                    if LDBG < 4:
                        continue
                    S.op("dve", lambda e: e.scalar_tensor_tensor(out=nb[b][:, 0:256], in0=aps[:, 0:256], scalar=q[:, 6:7], in1=gq,
                                                                 op0=ALU.mult, op1=ALU.mult),
                         r=[scp_t[pi], ssq_t[b], g_t], w=[nb_t[b]])
                    S.op("dve", lambda e: e.scalar_tensor_tensor(out=nb[b][:, 256:384], in0=aps[:, 256:384], scalar=q[:, 7:8], in1=gkv,
                                                                 op0=ALU.mult, op1=ALU.mult),
                         r=[scp_t[pi], ssq_t[b], g_t], w=[nb_t[b]])
                    S.op("act", lambda e: e.copy(out=KRraw[:, t, :], in_=aps[:, 384:416]), r=[scp_t[pi]], w=[kr_t])
                    if LDBG < 5:
                        continue
                    for i in range(3):
                        S.op("pe", lambda e, i=i: e.transpose(out=tpa[b][:, i, :], in_=nb[b][:, i * 128:(i + 1) * 128], identity=ident),
                             r=[nb_t[b], self.t_const], w=[tpa_t[b]])
                    S.op("act", lambda e: e.copy(out=cnT[:, :, t * 128:(t + 1) * 128], in_=tpa[b][:, 0:3, :]),
                         r=[tpa_t[b]], w=[cq_t, ckv_t])
                ra = self.sb(st2, [128, NT, 16], F32, "ra")
                rb = self.sb(st2, [128, NT, 16], F32, "rb")
                t1, t2 = KRraw[:, :, 0:16], KRraw[:, :, 16:32]
                if ADBG < 3:
                    S.barrier()
                    raise_skip = True
                else:
                    raise_skip = False
                if not raise_skip:
                    S.op("dve", lambda e: e.tensor_tensor(out=ra, in0=t1, in1=cos_tm, op=ALU.mult), r=[kr_t, g_t], w=[kr_t])
                    S.op("dve", lambda e: e.tensor_tensor(out=rb, in0=t2, in1=sin_tm, op=ALU.mult), r=[kr_t, g_t], w=[kr_t])
                    S.op("dve", lambda e: e.tensor_tensor(out=KR[:, :, 64:80], in0=ra, in1=rb, op=ALU.subtract), r=[kr_t], w=[kr_t])
                    S.op("dve", lambda e: e.tensor_tensor(out=ra, in0=t1, in1=sin_tm, op=ALU.mult), r=[kr_t, g_t], w=[kr_t])
                    S.op("dve", lambda e: e.tensor_tensor(out=rb, in0=t2, in1=cos_tm, op=ALU.mult), r=[kr_t, g_t], w=[kr_t])
                    S.op("dve", lambda e: e.tensor_tensor(out=KR[:, :, 80:96], in0=ra, in1=rb, op=ALU.add), r=[kr_t], w=[kr_t])
                    for g4 in range(4):
                        b = g4 % 2
                        for i in range(4):
                            t = g4 * 4 + i
                            S.op("pe", lambda e, i=i: e.transpose(out=tpa[b][:, i, :], in_=KR[:, t, :], identity=ident),
                                 r=[kr_t, self.t_const], w=[tpa_t[b]])
                        S.op("act", lambda e: e.copy(out=kropeT[64:96, g4 * 512:(g4 + 1) * 512],
                                                     in_=tpa[b][64:96, 0:4, :].rearrange("p a b -> p (a b)")),
                             r=[tpa_t[b]], w=[krt_t])
                S.barrier()

            qT = [self.sb(st, [128, S_LEN], BF16, "qT") for _ in range(2)]
            kT = [self.sb(st, [128, S_LEN], BF16, "kT") for _ in range(2)]
            vA = [self.sb(st, [128, NT, 128], BF16, "vP") for _ in range(2)]
            ones128 = self.sb(st, [128, 128], BF16, "ones128")
            ones_t = Tile("ones")
            S.op("pool", lambda e: e.memset(ones128, 1.0), w=[ones_t])
            qT_t = [Tile("qT") for _ in range(2)]
            kT_t = [Tile("kT") for _ in range(2)]
            vA_t = [Tile("vA") for _ in range(2)]
            for b in range(2):
                S.op("pool", lambda e, b=b: e.memset(vA[b], 0.0), w=[vA_t[b]])
                S.op("pool", lambda e, b=b: e.memset(qT[b], 0.0), w=[qT_t[b]])
                S.op("pool", lambda e, b=b: e.memset(kT[b], 0.0), w=[kT_t[b]])
            tm1 = [self.sb(st, [128, 512], F32, "tm1") for _ in range(2)]
            tm2 = [self.sb(st, [128, 512], F32, "tm2") for _ in range(2)]
            tm_t = [Tile("tm") for _ in range(2)]
            NPT = 4
            pT = [self.sb(st, [128, 512], BF16, "pT") for _ in range(NPT)]
            pT_t = [Tile("pT") for _ in range(NPT)]
            accO = [self.ps(st, [128, 512], F32, "accO") for _ in range(2)]
            accZ = [self.ps(st, [128, 512], F32, "accZ") for _ in range(2)]
            accO_t = [Tile("accO") for _ in range(2)]
            accZ_t = [Tile("accZ") for _ in range(2)]
            rzf = self.sb(st, [128, 512], F32, "rzf")
            rz_t = Tile("rz")
            cnt = 0
            tmk = 0
            qbk = 0
            for h in range(16):
                hb = h % 2
                c = h // 2
                for blk in range(4):
                    cols = slice(blk * 512, (blk + 1) * 512)
                    pa, pb = next_scp(), next_scp()
                    for k in range(2):
                        S.op("pe", lambda e, k=k: e.matmul(scp[pa], wuq[:, k, h * 96:h * 96 + 128], cqnT[:, k, cols],
                                                         start=(k == 0), stop=(k == 1)), r=[w_t, cq_t], w=[scp_t[pa]])
                    for k in range(2):
                        S.op("pe", lambda e, k=k: e.matmul(scp[pb], wq2[:, k, h, :], cqnT[:, k, cols],
                                                         start=(k == 0), stop=(k == 1)), r=[w_t, cq_t], w=[scp_t[pb]])
                    S.op("dve", lambda e: e.tensor_copy(out=qT[hb][0:64, cols], in_=scp[pa][0:64]), r=[scp_t[pa]], w=[qT_t[hb]])
                    tb = tmk % 2
                    tmk += 1
                    S.op("dve", lambda e: e.tensor_tensor(out=tm1[tb][64:96], in0=scp[pa][64:96], in1=cosf[64:96, cols], op=ALU.mult),
                         r=[scp_t[pa], rp_t], w=[tm_t[tb]])
                    S.op("dve", lambda e: e.tensor_tensor(out=tm2[tb][64:96], in0=scp[pb][64:96], in1=sinf[64:96, cols], op=ALU.mult),
                         r=[scp_t[pb], rp_t], w=[tm_t[tb]])
                    S.op("pool", lambda e: e.tensor_tensor(out=qT[hb][64:96, cols], in0=tm1[tb][64:96], in1=tm2[tb][64:96], op=ALU.add),
                         r=[tm_t[tb]], w=[qT_t[hb]])
                    pk = next_scp()
                    S.op("pe", lambda e: e.matmul(scp[pk][0:64], wukv[:, h * 128:h * 128 + 64], ckvnT[:, cols], start=True, stop=True),
                         r=[w_t, ckv_t], w=[scp_t[pk]])
                    S.op("act", lambda e: e.copy(out=kT[hb][0:64, cols], in_=scp[pk][0:64]), r=[scp_t[pk]], w=[kT_t[hb]])
                S.op("pool", lambda e: e.tensor_copy(out=kT[hb][64:96, :], in_=kropeT[64:96, :]), r=[krt_t], w=[kT_t[hb]])
                for half in range(2):
                    pv = next_scp()
                    for i in range(8):
                        t = half * 8 + i
                        S.op("pe", lambda e, i=i: e.matmul(scp[pv][:, i * 64:(i + 1) * 64], ckvnT[:, t * 128:(t + 1) * 128],
                                                         wukv[:, h * 128 + 64:h * 128 + 128], start=True, stop=True),
                             r=[w_t, ckv_t], w=[scp_t[pv]])
                    S.op("dve", lambda e: e.tensor_copy(out=vA[hb][:, half * 8:(half + 1) * 8, hb * 64:(hb + 1) * 64],
                                                        in_=scp[pv].rearrange("p (a d) -> p a d", a=8)),
                         r=[scp_t[pv]], w=[vA_t[hb]])
                steps = [(qb, kc) for qb in range(4) for kc in range(16)]
                LAG = 3
                fifo = []
                for i in range(len(steps) + LAG):
                    if i < len(steps):
                        qb, kc = steps[i]
                        pi = next_scp()
                        S.op("pe", lambda e: e.matmul(scp[pi], kT[hb][:, kc * 128:(kc + 1) * 128],
                                                      qT[hb][:, qb * 512:(qb + 1) * 512], start=True, stop=True),
                             r=[kT_t[hb], qT_t[hb]], w=[scp_t[pi]])
                        pj = cnt % NPT
                        cnt += 1
                        S.op("act", lambda e: e.activation(out=pT[pj], in_=scp[pi], func=AF.Exp, scale=sm_scale),
                             r=[scp_t[pi]], w=[pT_t[pj]])
                        fifo.append((qb, kc, pj))
                    if i >= LAG:
                        pqb, pkc, pj = fifo.pop(0)
                        ab = (qbk + pqb) % 2
                        S.op("pe", lambda e: e.matmul(accO[ab], vA[hb][:, pkc, :], pT[pj], start=(pkc == 0), stop=(pkc == 15)),
                             r=[pT_t[pj], vA_t[hb]], w=[accO_t[ab]])
                        S.op("pe", lambda e: e.matmul(accZ[ab], ones128, pT[pj], start=(pkc == 0), stop=(pkc == 15)),
                             r=[pT_t[pj], ones_t], w=[accZ_t[ab]])
                        if pkc == 15 and os.environ.get("MLA_NOEP", "0") != "1":
                            rows = slice(hb * 64, (hb + 1) * 64)
                            S.op("dve", lambda e: e.reciprocal(out=rzf, in_=accZ[ab]), r=[accZ_t[ab]], w=[rz_t])
                            S.op("dve", lambda e: e.tensor_tensor(out=self.oT[rows, c, pqb * 512:(pqb + 1) * 512], in0=accO[ab][rows],
                                                                  in1=rzf[rows], op=ALU.mult),
                                 r=[accO_t[ab], rz_t], w=self.oT_t[pqb * 4:(pqb + 1) * 4])
                qbk += 4
            S.barrier()
        self.out_proj_ln("mla_out", 0, 3)


_CONSTS = None


def _run(prog, inputs, x_shards):
    global _CONSTS
    if _CONSTS is None:
        _CONSTS = _consts()
    col = np.arange(64)
    dc = np.clip(col[:, None] - col[None, :], -15, 15) + 15
    rpb = np.asarray(inputs["na_rpb"], dtype=np.float32)[0]
    rpbg = np.ascontiguousarray(np.transpose(rpb[:, :, dc], (1, 2, 0, 3)))
    in_maps = []
    for xs in x_shards:
        m = {"x": np.ascontiguousarray(xs, dtype=np.float32)}
        for k in INPUT_SHAPES:
            m[k] = np.ascontiguousarray(inputs[k], dtype=np.float32)
        for k in CONST_SPECS:
            m["c_" + k] = _CONSTS[k]
        m["na_rpbg"] = rpbg
        in_maps.append(m)
    res = run_bass_kernel_spmd(prog.nc, in_maps, core_ids=list(range(len(x_shards))))
    return [np.asarray(r["out"]) for r in res.results]


def kernel(**inputs):
    x = np.asarray(inputs["x"], dtype=np.float32)
    prog = Prog()
    shards = [x[i * SEQ_PER_CORE:(i + 1) * SEQ_PER_CORE] for i in range(NCORES)]
    outs = _run(prog, inputs, shards)
    return np.concatenate(outs, axis=0).astype(np.float32)
```

```python
import math
import os
from contextlib import ExitStack
import numpy as np
import ml_dtypes
import concourse.bass as bass
import concourse.mybir as mybir
from concourse.bass_utils import run_bass_kernel_spmd

F32 = mybir.dt.float32
BF16 = mybir.dt.bfloat16
AF = mybir.ActivationFunctionType
ALU = mybir.AluOpType

D = 1024
S_LEN = 2048
NT = 16
NCH = 8
DFF = 2816
NJ = 22
ALPHA = 8.0 ** 0.25
LN_EPS = 1e-5
RMS_EPS = 1e-6
NCORES = 8
SEQ_PER_CORE = 2
NEG = -30000.0


class Tile:
    __slots__ = ("name", "writer", "readers")

    def __init__(self, name):
        self.name = name
        self.writer = None
        self.readers = {}


class DSem:
    __slots__ = ("key", "total", "in_barrier")

    def __init__(self, key):
        self.key = key
        self.total = 0
        self.in_barrier = True


class Sched:
    LIMIT = 24000

    def __init__(self, nc):
        self.nc = nc
        self.E = dict(pe=nc.tensor, act=nc.scalar, dve=nc.vector, pool=nc.gpsimd, sp=nc.sync)
        self.sems = {}
        self.nsem = 0
        self.cur = {}
        self.waited = {e: {} for e in self.E}
        self.dsems = []
        self.nwaits = 0
        self.nops = 0
        for e in ("pe", "act", "dve", "pool"):
            self._new_eng_sem(e)

    def _alloc(self, name):
        k = self.nsem
        self.nsem += 1
        self.sems[k] = self.nc.alloc_semaphore(f"s{k}_{name}")
        return k

    def _new_eng_sem(self, e):
        self.cur[e] = [self._alloc(e), 0]

    def dsem(self, name="d"):
        d = DSem(self._alloc(name))
        self.dsems.append(d)
        return d

    def pool_reset(self):
        self.pool_i = 0

    def pds(self, name="p"):
        if not hasattr(self, "pool"):
            self.pool, self.pool_i = [], 0
        if self.pool_i >= len(self.pool):
            self.pool.append(self.dsem(name))
        d = self.pool[self.pool_i]
        self.pool_i += 1
        return d

    def _wait(self, eng, tok):
        key, val = tok[0], tok[1]
        if self.waited[eng].get(key, 0) >= val:
            return
        self.E[eng].wait_ge(self.sems[key], val)
        self.waited[eng][key] = val
        self.nwaits += 1

    def _deps(self, eng, r, w, is_dma):
        deps = []
        for t in r:
            wr = t.writer
            if wr is not None:
                if wr[2] == eng and not is_dma and eng == "pe":
                    continue
                deps.append(wr)
        for t in w:
            wr = t.writer
            if wr is not None and (is_dma or wr[2] != eng):
                deps.append(wr)
            for tok in t.readers.values():
                if is_dma or tok[2] != eng:
                    deps.append(tok)
        return deps

    def _commit(self, tok, r, w):
        for t in r:
            t.readers[tok[0]] = tok
        for t in w:
            t.writer = tok
            t.readers = {}

    def op(self, eng, fn, r=(), w=()):
        for tok in self._deps(eng, r, w, False):
            self._wait(eng, tok)
        cur = self.cur[eng]
        if cur[1] >= self.LIMIT:
            self._new_eng_sem(eng)
            cur = self.cur[eng]
        ins = fn(self.E[eng])
        cur[1] += 1
        ins.then_inc(self.sems[cur[0]], 1)
        tok = (cur[0], cur[1], eng)
        self._commit(tok, r, w)
        self.nops += 1
        return tok

    def dma(self, q, out, in_, ds, r=(), w=(), **kw):
        for tok in self._deps(q, r, w, True):
            self._wait(q, tok)
        if ds.total + 16 > self.LIMIT:
            ds.key = self._alloc("d")
            ds.total = 0
        ins = self.E[q].dma_start(out=out, in_=in_, **kw)
        ds.total += 16
        ins.then_inc(self.sems[ds.key], 16)
        tok = (ds.key, ds.total, "dma")
        self._commit(tok, r, w)
        self.nops += 1
        return tok

    def group_done(self, ds, tiles):
        tok = (ds.key, ds.total, "dma")
        for t in tiles:
            t.writer = tok

    def barrier(self):
        toks = [(c[0], c[1], e) for e, c in self.cur.items() if c[1] > 0]
        toks += [(d.key, d.total, "dma") for d in self.dsems if d.total > 0 and d.in_barrier]
        for e in self.E:
            for tok in toks:
                if tok[2] == e and e == "pe":
                    continue
                self._wait(e, tok)
        self.pool_reset()


def _consts():
    c = {}
    c["ident"] = np.eye(128, dtype=np.float32).astype(ml_dtypes.bfloat16)
    p = np.arange(128)[:, None]
    cc = np.arange(3968)[None, :]
    c["alibi"] = np.abs(p - cc + 1920).astype(np.float32)
    inv_freq = (1.0 / (10000.0 ** (np.arange(0, 32, 2, dtype=np.float32) / np.float32(32)))).astype(np.float32)
    ang = (np.arange(S_LEN, dtype=np.float32)[:, None] * inv_freq[None, :]).astype(np.float32)
    cos, sin = np.cos(ang).astype(np.float32), np.sin(ang).astype(np.float32)
    c["rope_cos_tm"] = np.ascontiguousarray(cos.reshape(NT, 128, 16).transpose(1, 0, 2))
    c["rope_sin_tm"] = np.ascontiguousarray(sin.reshape(NT, 128, 16).transpose(1, 0, 2))
    cf = np.zeros((128, S_LEN), np.float32)
    sf = np.zeros((128, S_LEN), np.float32)
    for i in range(32):
        cf[64 + i] = cos[:, i % 16]
        sf[64 + i] = sin[:, i % 16]
    c["rope_cos_fm"] = cf
    c["rope_sin_fm"] = sf
    col = np.arange(64)
    cs = np.clip(col - 8, 0, 48)
    valid = (col[None, :] >= cs[:, None]) & (col[None, :] < cs[:, None] + 16)
    madd = np.where(valid.T, 0.0, NEG).astype(np.float32)
    c["na_mask"] = np.concatenate([madd, madd], axis=0)
    return c


CONST_SPECS = {
    "ident": ([128, 128], BF16),
    "alibi": ([128, 3968], F32),
    "rope_cos_tm": ([128, NT, 16], F32),
    "rope_sin_tm": ([128, NT, 16], F32),
    "rope_cos_fm": ([128, S_LEN], F32),
    "rope_sin_fm": ([128, S_LEN], F32),
    "na_mask": ([128, 64], F32),
}

INPUT_SHAPES = {
    "conv_w_in": [1, 1024, 3072], "conv_w": [1, 3, 1024], "conv_w_out": [1, 1024, 1024],
    "diff_w_qkv": [1, 1024, 3072], "diff_lambda": [1, 4, 64], "diff_subln_g": [1, 128], "diff_w_out": [1, 1024, 1024],
    "na_w_qkv": [1, 1024, 3072], "na_rpb": [1, 16, 15, 31], "na_w_out": [1, 1024, 1024],
    "mla_w_a": [1, 1024, 416], "mla_g_q": [1, 256], "mla_g_kv": [1, 128], "mla_w_uq": [1, 256, 1536],
    "mla_w_ukv": [1, 128, 2048], "mla_w_out": [1, 1024, 1024],
    "ln1_g": [4, 1024], "ln1_b": [4, 1024], "ffn_w_gu": [4, 1024, 5632], "ffn_w_down": [4, 2816, 1024],
    "ln2_g": [4, 1024], "ln2_b": [4, 1024],
}
DERIVED_SHAPES = {"na_rpbg": [15, 64, 16, 64]}


class Prog:
    def __init__(self, layers=(0, 1, 2, 3), nseq=SEQ_PER_CORE, do_ffn=True):
        self.layers = tuple(layers)
        self.nseq = nseq
        self.do_ffn = do_ffn
        nc = self.nc = bass.Bass("TRN2", target_bir_lowering=False)
        self.S = Sched(nc)
        self.I = {}
        self.I["x"] = nc.dram_tensor("x", [nseq, S_LEN, D], F32, kind="ExternalInput").ap()
        for k, shp in INPUT_SHAPES.items():
            self.I[k] = nc.dram_tensor(k, shp, F32, kind="ExternalInput").ap()
        for k, (shp, dt) in CONST_SPECS.items():
            self.I[k] = nc.dram_tensor("c_" + k, shp, dt, kind="ExternalInput").ap()
        for k, shp in DERIVED_SHAPES.items():
            self.I[k] = nc.dram_tensor(k, shp, F32, kind="ExternalInput").ap()
        self.out = nc.dram_tensor("out", [nseq, S_LEN, D], F32, kind="ExternalOutput").ap()
        self.uid = 0
        self.build()

    def name(self, p):
        self.uid += 1
        return f"{p}{self.uid}"

    def sb(self, st, shape, dt, name="sb"):
        return st.enter_context(self.nc.sbuf_tensor(self.name(name), shape, dt)).ap()

    def ps(self, st, shape, dt, name="ps"):
        return st.enter_context(self.nc.psum_tensor(self.name(name), shape, dt)).ap()

    def scratch(self, shape, dt=BF16, name="scr"):
        return self.nc.dram_tensor(self.name(name), shape, dt, kind="Internal").ap()

    def conv_lhsT(self, w2d, K, N, name):
        S = self.S
        kc, nj = K // 128, N // 128
        scr = self.scratch([nj, 128, kc, 128], name=name)
        ds = S.dsem(name)
        ds.in_barrier = False
        t = Tile(name)
        for j in range(nj):
            src = w2d[:, j * 128:(j + 1) * 128].rearrange("(kc p) n -> p kc n", p=128)
            S.dma("pool", scr[j], src, ds, w=[t])
        t.writer = (ds.key, ds.total, "dma")
        return scr, t

    def conv_rhs(self, w2d, K, N, name):
        S = self.S
        kc = K // 128
        scr = self.scratch([128, kc, N], name=name)
        ds = S.dsem(name)
        ds.in_barrier = False
        t = Tile(name)
        for k in range(kc):
            S.dma("pool", scr[:, k, :], w2d[k * 128:(k + 1) * 128, :], ds, w=[t])
        t.writer = (ds.key, ds.total, "dma")
        return scr, t

    def build(self):
        nc, S, I = self.nc, self.S, self.I
        with ExitStack() as gst:
            cds = S.dsem("const")
            self.t_const = Tile("const")
            self.ident = self.sb(gst, [128, 128], BF16, "ident")
            S.dma("sp", self.ident, I["ident"], cds, w=[self.t_const])
            self.W = {}
            for i, L in enumerate(self.layers):
                self.convert_layer(L)
            self.x = self.sb(gst, [128, NT, D], F32, "x")
            self.xT = self.sb(gst, [128, NCH, S_LEN], BF16, "xT")
            self.x_t = [Tile(f"x{t}") for t in range(NT)]
            self.xT_t = [Tile(f"xT{t}") for t in range(NT)]
            self.oT_t = [Tile(f"oT{t}") for t in range(NT)]
            self.out_ds = [S.dsem("out") for _ in range(4)]
            self.xin_ds = [S.dsem("xin") for _ in range(4)]
            for s in range(self.nseq):
                self.load_x(s)
                for L in self.layers:
                    [self.layer_conv, self.layer_diff, self.layer_na, self.layer_mla][L](s)
                    if self.do_ffn:
                        self.ffn(L, s, last=(L == self.layers[-1]))
                if not self.do_ffn:
                    self.store_x(s)
                S.barrier()
            S.barrier()

    def convert_layer(self, L):
        I, W = self.I, self.W
        if L == 0:
            W["conv_in"] = self.conv_lhsT(I["conv_w_in"][0], 1024, 3072, "cwin")
            W["conv_out"] = self.conv_rhs(I["conv_w_out"][0], 1024, 1024, "cwout")
        elif L == 1:
            W["diff_qkv"] = self.conv_lhsT(I["diff_w_qkv"][0], 1024, 3072, "dqkv")
            W["diff_out"] = self.conv_rhs(I["diff_w_out"][0], 1024, 1024, "dwout")
        elif L == 2:
            W["na_qkv"] = self.conv_lhsT(I["na_w_qkv"][0], 1024, 3072, "nqkv")
            W["na_out"] = self.conv_rhs(I["na_w_out"][0], 1024, 1024, "nwout")
        elif L == 3:
            W["mla_a"] = self.conv_rhs(I["mla_w_a"][0], 1024, 416, "mwa")
            W["mla_uq"] = self.conv_rhs(I["mla_w_uq"][0], 256, 1536, "muq")
            W["mla_ukv"] = self.conv_rhs(I["mla_w_ukv"][0], 128, 2048, "mukv")
            W["mla_out"] = self.conv_rhs(I["mla_w_out"][0], 1024, 1024, "mwout")
        if self.do_ffn:
            S = self.S
            scr = self.scratch([NJ, 128, NCH, 256], name=f"wgu{L}")
            ds = S.dsem("wgu")
            ds.in_barrier = False
            t = Tile("wgu")
            w = I["ffn_w_gu"][L]
            for j in range(NJ):
                for half in range(2):
                    src = w[:, half * DFF + j * 128: half * DFF + (j + 1) * 128].rearrange("(kc p) n -> p kc n", p=128)
                    S.dma("pool", scr[j, :, :, half * 128:(half + 1) * 128], src, ds, w=[t])
            t.writer = (ds.key, ds.total, "dma")
            W[f"gu{L}"] = (scr, t)
            W[f"down{L}"] = self.conv_rhs(I["ffn_w_down"][L], DFF, 1024, f"wdn{L}")

    def load_lnp(self, st, L, which):
        S, I = self.S, self.I
        self.lnp = self.sb(st, [128, 2, D], F32, "lnp")
        self.t_lnp = Tile("lnp")
        ds = S.pds("lnp")
        for i, k in enumerate([f"ln{which}_g", f"ln{which}_b"]):
            S.dma("sp", self.lnp[:, i, :], I[k][L].partition_broadcast(128), ds, w=[self.t_lnp])

    def load_x(self, s):
        S, I = self.S, self.I
        xs = I["x"][s].rearrange("(t p) d -> p t d", p=128)
        for t in range(NT):
            S.dma("sp", self.x[:, t, :], xs[:, t, :], self.xin_ds[0], w=[self.x_t[t]])
        S.group_done(self.xin_ds[0], self.x_t)
        with ExitStack() as st:
            xb = [self.sb(st, [128, D], BF16, "xb") for _ in range(2)]
            xb_t = [Tile("xb") for _ in range(2)]
            tp = [self.ps(st, [128, NCH, 128], BF16, "tp") for _ in range(2)]
            tp_t = [Tile("tp") for _ in range(2)]
            for t in range(NT):
                b = t % 2
                self.to_featmajor(t, xb[b], xb_t[b], tp[b], tp_t[b], "act" if t % 2 else "dve")
            S.barrier()

    def to_featmajor(self, t, xb, xb_t, tp, tp_t, eng):
        S = self.S
        if eng == "act":
            S.op("act", lambda e: e.copy(out=xb, in_=self.x[:, t, :]), r=[self.x_t[t]], w=[xb_t])
        else:
            S.op("dve", lambda e: e.tensor_copy(out=xb, in_=self.x[:, t, :]), r=[self.x_t[t]], w=[xb_t])
        for c in range(NCH):
            S.op("pe", lambda e, c=c: e.transpose(out=tp[:, c, :], in_=xb[:, c * 128:(c + 1) * 128], identity=self.ident),
                 r=[xb_t, self.t_const], w=[tp_t])
        dst = self.xT[:, :, t * 128:(t + 1) * 128]
        if eng == "act":
            S.op("dve", lambda e: e.tensor_copy(out=dst, in_=tp), r=[tp_t], w=[self.xT_t[t]])
        else:
            S.op("act", lambda e: e.copy(out=dst, in_=tp), r=[tp_t], w=[self.xT_t[t]])

    def store_x(self, s):
        S = self.S
        os_ = self.out[s].rearrange("(t p) d -> p t d", p=128)
        for t in range(NT):
            S.dma("sp", os_[:, t, :], self.x[:, t, :], self.out_ds[t % 4], r=[self.x_t[t]])

    def ln_epilogue(self, t, y_ps, y_t, gi, L, W, store=None):
        S = self.S
        k = W["k"]
        W["k"] += 1
        b = k % 3
        z, z_t = W["z"][b], W["z_t"][b]
        st6, st6_t = W["st"][b], W["st_t"][b]
        xt = self.x[:, t, :]
        S.op("dve", lambda e: e.scalar_tensor_tensor(out=z, in0=xt, scalar=ALPHA, in1=y_ps, op0=ALU.mult, op1=ALU.add),
             r=[self.x_t[t]] + y_t, w=[z_t])
        for h in range(2):
            S.op("dve", lambda e, h=h: e.bn_stats(out=st6[:, h * 6:(h + 1) * 6], in_=z[:, h * 512:(h + 1) * 512]),
                 r=[z_t], w=[st6_t])
        S.op("dve", lambda e: e.bn_aggr(out=st6[:, 12:14], in_=st6[:, 0:12]), r=[st6_t], w=[st6_t])
        S.op("dve", lambda e: e.tensor_scalar(out=st6[:, 13:14], in0=st6[:, 13:14], scalar1=LN_EPS, scalar2=None,
                                              op0=ALU.add), r=[st6_t], w=[st6_t])
        S.op("act", lambda e: e.activation(out=st6[:, 14:15], in_=st6[:, 13:14], func=AF.Ln), r=[st6_t], w=[st6_t])
        S.op("act", lambda e: e.activation(out=st6[:, 14:15], in_=st6[:, 14:15], func=AF.Exp, scale=-0.5),
             r=[st6_t], w=[st6_t])
        S.op("dve", lambda e: e.scalar_tensor_tensor(out=st6[:, 15:16], in0=st6[:, 12:13], scalar=-1.0, in1=st6[:, 14:15],
                                                     op0=ALU.mult, op1=ALU.mult), r=[st6_t], w=[st6_t])
        S.op("act", lambda e: e.activation(out=z, in_=z, func=AF.Identity, bias=st6[:, 15:16], scale=st6[:, 14:15]),
             r=[z_t, st6_t], w=[z_t])
        g = self.lnp[:, 0, :]
        bb = self.lnp[:, 1, :]
        S.op("pool", lambda e: e.tensor_tensor(out=z, in0=z, in1=g, op=ALU.mult), r=[z_t, self.t_lnp], w=[z_t])
        S.op("dve", lambda e: e.tensor_tensor(out=xt, in0=z, in1=bb, op=ALU.add), r=[z_t, self.t_lnp], w=[self.x_t[t]])
        if store is not None:
            s, dsl = store
            os_ = self.out[s].rearrange("(t p) d -> p t d", p=128)
            S.dma("sp", os_[:, t, :], xt, dsl[t % 4], r=[self.x_t[t]])
            return None
        xb, xb_t = W["xb"][b], W["xb_t"][b]
        S.op("act", lambda e: e.copy(out=xb, in_=xt), r=[self.x_t[t]], w=[xb_t])

        def part_b():
            kb = W["kb"]
            W["kb"] += 1
            tp, tp_t = W["tp"][kb % 2], W["tp_t"][kb % 2]
            for c in range(NCH):
                S.op("pe", lambda e, c=c: e.transpose(out=tp[:, c, :], in_=xb[:, c * 128:(c + 1) * 128], identity=self.ident),
                     r=[xb_t, self.t_const], w=[tp_t])
            dst = self.xT[:, :, t * 128:(t + 1) * 128]
            if kb % 2:
                S.op("dve", lambda e: e.tensor_copy(out=dst, in_=tp), r=[tp_t], w=[self.xT_t[t]])
            else:
                S.op("act", lambda e: e.copy(out=dst, in_=tp), r=[tp_t], w=[self.xT_t[t]])
        return part_b

    def ln_scratch(self, st):
        W = {"k": 0, "kb": 0, "pending": []}
        W["z"] = [self.sb(st, [128, D], F32, "z") for _ in range(3)]
        W["z_t"] = [Tile("z") for _ in range(3)]
        W["st"] = [self.sb(st, [128, 16], F32, "st") for _ in range(3)]
        W["st_t"] = [Tile("st") for _ in range(3)]
        W["xb"] = [self.sb(st, [128, D], BF16, "xb") for _ in range(3)]
        W["xb_t"] = [Tile("xb") for _ in range(3)]
        W["tp"] = [self.ps(st, [128, NCH, 128], BF16, "tp") for _ in range(2)]
        W["tp_t"] = [Tile("tp") for _ in range(2)]
        return W

    def ln_push(self, W, pb):
        if pb is not None:
            W["pending"].append(pb)
        while len(W["pending"]) > 2:
            W["pending"].pop(0)()

    def ln_pop(self, W, n=1):
        for _ in range(n):
            if W["pending"]:
                W["pending"].pop(0)()

    def out_proj_ln(self, wkey, gi, L):
        S = self.S
        scr, wt = self.W[wkey]
        with ExitStack() as st:
            self.load_lnp(st, L, 1)
            w_sb = self.sb(st, [128, NCH, D], BF16, "wout")
            w_t = Tile("wout")
            ds = S.pds("wout")
            for c in range(0, NCH, 2):
                S.dma("sp", w_sb[:, c:c + 2, :], scr[:, c:c + 2, :], ds, r=[wt], w=[w_t])
            LW = self.ln_scratch(st)
            yps = [self.ps(st, [128, D], F32, "y") for _ in range(2)]
            y_t = [[Tile("y0"), Tile("y1")] for _ in range(2)]
            for t in range(NT):
                b = t % 2
                for h in range(2):
                    for c in range(NCH):
                        S.op("pe", lambda e, c=c, h=h: e.matmul(yps[b][:, h * 512:(h + 1) * 512],
                                                              self.oT[:, c, t * 128:(t + 1) * 128],
                                                              w_sb[:, c, h * 512:(h + 1) * 512],
                                                              start=(c == 0), stop=(c == NCH - 1)),
                             r=[self.oT_t[t], w_t], w=[y_t[b][h]])
                self.ln_push(LW, self.ln_epilogue(t, yps[b], y_t[b], gi, L, LW))
            self.ln_pop(LW, 2)
            S.barrier()

    def ffn(self, L, s, last):
        S = self.S
        gscr, gt = self.W[f"gu{L}"]
        dscr, dt_ = self.W[f"down{L}"]
        with ExitStack() as st:
            self.load_lnp(st, L, 2)
            wd = self.sb(st, [128, NJ, D], BF16, "wd")
            wd_t = Tile("wd")
            ds = S.pds("wd")
            for j in range(0, NJ, 2):
                S.dma("sp", wd[:, j:j + 2, :], dscr[:, j:j + 2, :], ds, r=[dt_], w=[wd_t])
            NSLOT = 3
            ring = [self.sb(st, [128, NCH, 256], BF16, "wgu") for _ in range(NSLOT)]
            ring_t = [Tile("wgu") for _ in range(NSLOT)]
            ring_ds = [S.pds("wgu") for _ in range(NSLOT)]
            hT = self.sb(st, [128, NJ, 512], BF16, "hT")
            hT_t = [Tile("hT") for _ in range(4)]
            sg = [self.sb(st, [128, 512], F32, "sg") for _ in range(2)]
            sg_t = [Tile("sg") for _ in range(2)]
            gps = self.ps(st, [128, 512], F32, "g")
            ups = self.ps(st, [128, 512], F32, "u")
            g_t, u_t = Tile("g"), Tile("u")
            LW = self.ln_scratch(st)
            yps = [self.ps(st, [128, D], F32, "y") for _ in range(2)]
            y_t = [[Tile("y0"), Tile("y1")] for _ in range(2)]
            nload = 0
            total = 4 * NJ

            def issue(i):
                j = i % NJ
                sl = i % NSLOT
                S.dma("sp", ring[sl], gscr[j], ring_ds[sl], r=[gt], w=[ring_t[sl]])

            for i in range(min(NSLOT - 1, total)):
                issue(i)
                nload += 1
            it = 0
            for blk in range(4):
                xts = self.xT_t[blk * 4:(blk + 1) * 4]
                for j in range(NJ):
                    if nload < total:
                        issue(nload)
                        nload += 1
                    sl = it % NSLOT
                    it += 1
                    for c in range(NCH):
                        S.op("pe", lambda e, c=c: e.matmul(gps, ring[sl][:, c, 0:128], self.xT[:, c, blk * 512:(blk + 1) * 512],
                                                         start=(c == 0), stop=(c == NCH - 1)),
                             r=[ring_t[sl]] + xts, w=[g_t])
                    for c in range(NCH):
                        S.op("pe", lambda e, c=c: e.matmul(ups, ring[sl][:, c, 128:256], self.xT[:, c, blk * 512:(blk + 1) * 512],
                                                         start=(c == 0), stop=(c == NCH - 1)),
                             r=[ring_t[sl]] + xts, w=[u_t])
                    if j in (2, 5):
                        self.ln_pop(LW, 1)
                    b = j % 2
                    S.op("act", lambda e: e.activation(out=sg[b], in_=gps, func=AF.Silu), r=[g_t], w=[sg_t[b]])
                    S.op("dve", lambda e: e.tensor_tensor(out=hT[:, j, :], in0=sg[b], in1=ups, op=ALU.mult),
                         r=[sg_t[b], u_t], w=hT_t)
                for tt in range(4):
                    t = blk * 4 + tt
                    b = t % 2
                    for h in range(2):
                        for j in range(NJ):
                            S.op("pe", lambda e, j=j, h=h: e.matmul(yps[b][:, h * 512:(h + 1) * 512],
                                                                  hT[:, j, tt * 128:(tt + 1) * 128],
                                                                  wd[:, j, h * 512:(h + 1) * 512],
                                                                  start=(j == 0), stop=(j == NJ - 1)),
                                 r=[hT_t[tt], wd_t], w=[y_t[b][h]])
                    self.ln_push(LW, self.ln_epilogue(t, yps[b], y_t[b], 2, L, LW, store=(s, self.out_ds) if last else None))
            self.ln_pop(LW, 2)
            S.barrier()

    def layer_conv(self, s):
        S, I = self.S, self.I
        scr, wt = self.W["conv_in"]
        with ExitStack() as ost:
          self.oT = self.sb(ost, [128, NCH, S_LEN], BF16, "oT")
          with ExitStack() as st:
            cw = self.sb(st, [128, 3, NCH], F32, "cw")
            cw_t = Tile("cw")
            ds = S.pds("cw")
            S.dma("sp", cw, I["conv_w"][0].rearrange("t (c p) -> p t c", p=128), ds, w=[cw_t],
                  allow_slow_non_contiguous=True)
            NSLOT = 6
            ring = [self.sb(st, [128, NCH, 128], BF16, "win") for _ in range(NSLOT)]
            ring_t = [Tile("win") for _ in range(NSLOT)]
            ring_ds = [S.pds("win") for _ in range(NSLOT)]
            u = [self.sb(st, [128, S_LEN + 2], F32, "u") for _ in range(2)]
            u_t = [Tile("u") for _ in range(2)]
            bg = [self.sb(st, [128, S_LEN], F32, "bg") for _ in range(2)]
            bg_t = [Tile("bg") for _ in range(2)]
            y = self.sb(st, [128, S_LEN], F32, "y")
            y_t = Tile("y")
            cgs = [self.sb(st, [128, 512], F32, "cgs") for _ in range(2)]
            cgs_t = [Tile("cgs") for _ in range(2)]
            pss = [[self.ps(st, [128, 512], F32, "cps") for _ in range(3)] for _ in range(2)]
            pss_t = [[Tile("cps") for _ in range(3)] for _ in range(2)]
            for b in range(2):
                S.op("pool", lambda e, b=b: e.memset(u[b][:, 0:1], 0.0), w=[u_t[b]])
                S.op("pool", lambda e, b=b: e.memset(u[b][:, S_LEN + 1:S_LEN + 2], 0.0), w=[u_t[b]])
            order = [(c, kind) for c in range(NCH) for kind in range(3)]

            def issue(i):
                c, kind = order[i]
                sl = i % NSLOT
                S.dma("sp", ring[sl], scr[kind * 8 + c], ring_ds[sl], r=[wt], w=[ring_t[sl]])

            nload = 0
            for i in range(NSLOT - 3):
                issue(i)
                nload += 1
            kk = 0
            for c in range(NCH):
                for _ in range(3):
                    if nload < len(order):
                        issue(nload)
                        nload += 1
                ub = c % 2
                for blk in range(4):
                    pb = kk % 2
                    kk += 1
                    xts = self.xT_t[blk * 4:(blk + 1) * 4]
                    for kind in range(3):
                        sl = (c * 3 + kind) % NSLOT
                        for k in range(NCH):
                            S.op("pe", lambda e, k=k, kind=kind, sl=sl: e.matmul(
                                pss[pb][kind], ring[sl][:, k, :], self.xT[:, k, blk * 512:(blk + 1) * 512],
                                start=(k == 0), stop=(k == NCH - 1)),
                                r=[ring_t[sl]] + xts, w=[pss_t[pb][kind]])
                    S.op("act", lambda e: e.copy(out=bg[ub][:, blk * 512:(blk + 1) * 512], in_=pss[pb][0]),
                         r=[pss_t[pb][0]], w=[bg_t[ub]])
                    S.op("act", lambda e: e.copy(out=cgs[pb], in_=pss[pb][1]), r=[pss_t[pb][1]], w=[cgs_t[pb]])
                    S.op("dve", lambda e: e.tensor_tensor(out=u[ub][:, 1 + blk * 512:1 + (blk + 1) * 512], in0=cgs[pb],
                                                          in1=pss[pb][2], op=ALU.mult),
                         r=[cgs_t[pb], pss_t[pb][2]], w=[u_t[ub]])
                uu = u[ub]
                S.op("act", lambda e: e.activation(out=y, in_=uu[:, 1:S_LEN + 1], func=AF.Copy, scale=cw[:, 1, c:c + 1]),
                     r=[u_t[ub], cw_t], w=[y_t])
                S.op("dve", lambda e: e.scalar_tensor_tensor(out=y, in0=uu[:, 0:S_LEN], scalar=cw[:, 0, c:c + 1], in1=y,
                                                             op0=ALU.mult, op1=ALU.add), r=[u_t[ub], cw_t, y_t], w=[y_t])
                S.op("dve", lambda e: e.scalar_tensor_tensor(out=y, in0=uu[:, 2:S_LEN + 2], scalar=cw[:, 2, c:c + 1], in1=y,
                                                             op0=ALU.mult, op1=ALU.add), r=[u_t[ub], cw_t, y_t], w=[y_t])
                S.op("pool", lambda e: e.tensor_tensor(out=self.oT[:, c, :], in0=bg[ub], in1=y, op=ALU.mult),
                     r=[bg_t[ub], y_t], w=self.oT_t)
            S.barrier()
          self.out_proj_ln("conv_out", 0, 0)

    def layer_diff(self, s):
        S, I = self.S, self.I
        scr, wt = self.W["diff_qkv"]
        lam_init = 0.8 - 0.6 * math.exp(-0.3 * 1)
        with ExitStack() as ost:
          self.oT = self.sb(ost, [128, NCH, S_LEN], BF16, "oT")
          with ExitStack() as st:
            cds = S.pds("dc")
            alibi = self.sb(st, [128, 3968], F32, "alibi")
            al_t = Tile("alibi")
            S.dma("sp", alibi, I["alibi"], cds, w=[al_t])
            lam_sb = self.sb(st, [128, 4, 64], F32, "lam")
            gsub = self.sb(st, [128, 1], F32, "gsub")
            sm = self.sb(st, [128, 8], F32, "sm")
            prm_t = Tile("prm")
            S.dma("sp", lam_sb, I["diff_lambda"][0].partition_broadcast(128), cds, w=[prm_t])
            S.dma("sp", gsub, I["diff_subln_g"][0].rearrange("(p o) -> p o", o=1), cds, w=[prm_t])
            S.group_done(cds, [al_t, prm_t])
            lp = self.sb(st, [128, 2, 64], F32, "lp")
            S.op("dve", lambda e: e.tensor_tensor(out=lp, in0=lam_sb[:, 0:4:2, :], in1=lam_sb[:, 1:4:2, :], op=ALU.mult),
                 r=[prm_t], w=[prm_t])
            S.op("dve", lambda e: e.reduce_sum(out=sm[:, 0:2], in_=lp, axis=mybir.AxisListType.X), r=[prm_t], w=[prm_t])
            S.op("act", lambda e: e.activation(out=sm[:, 2:4], in_=sm[:, 0:2], func=AF.Exp), r=[prm_t], w=[prm_t])
            S.op("dve", lambda e: e.tensor_tensor(out=sm[:, 4:5], in0=sm[:, 3:4], in1=sm[:, 2:3], op=ALU.subtract),
                 r=[prm_t], w=[prm_t])
            S.op("dve", lambda e: e.tensor_scalar(out=sm[:, 5:6], in0=sm[:, 4:5], scalar1=-lam_init, scalar2=None, op0=ALU.add),
                 r=[prm_t], w=[prm_t])
            S.op("dve", lambda e: e.tensor_scalar(out=gsub, in0=gsub, scalar1=1.0 - lam_init, scalar2=None, op0=ALU.mult),
                 r=[prm_t], w=[prm_t])
            neglam = sm[:, 5:6]
            gs_col = gsub[:, 0:1]

            NSLOT = 6
            ring = [self.sb(st, [128, NCH, 128], BF16, "wqkv") for _ in range(NSLOT)]
            ring_t = [Tile("wqkv") for _ in range(NSLOT)]
            ring_ds = [S.pds("wqkv") for _ in range(NSLOT)]
            qT = self.sb(st, [128, S_LEN], BF16, "qT")
            kT = self.sb(st, [128, S_LEN], BF16, "kT")
            vA = self.sb(st, [128, NT, 129], BF16, "vA")
            qT_t, kT_t, vA_t = Tile("qT"), Tile("kT"), Tile("vA")
            S.op("pool", lambda e: e.memset(vA[:, :, 128:129], 1.0), w=[vA_t])
            NSB = 4
            sbs = [self.sb(st, [128, 512], F32, "scs") for _ in range(NSB)]
            sbs_t = [Tile("scs") for _ in range(NSB)]
            NPT = 8
            pT = [self.sb(st, [128, 512], BF16, "pT") for _ in range(NPT)]
            pT_t = [Tile("pT") for _ in range(NPT)]
            scp = [self.ps(st, [128, 512], F32, "scp") for _ in range(4)]
            scp_t = [Tile("scp") for _ in range(4)]
            accO = [self.ps(st, [128, 512], F32, "accO") for _ in range(2)]
            accZ = [self.ps(st, [128, 512], F32, "accZ") for _ in range(2)]
            accO_t = [Tile("accO") for _ in range(2)]
            accZ_t = [Tile("accZ") for _ in range(2)]
            ones128 = self.sb(st, [128, 128], BF16, "ones128")
            ones_t = Tile("ones")
            S.op("pool", lambda e: e.memset(ones128, 1.0), w=[ones_t])
            z0s = self.sb(st, [128, 512], F32, "z0s")
            z1s = self.sb(st, [128, 512], F32, "z1s")
            t0 = self.sb(st, [128, 512], F32, "t0")
            t1 = self.sb(st, [128, 512], F32, "t1")
            sqb = self.sb(st, [128, 512], BF16, "sqb")
            ew_t, ew2_t, ew3_t, sq_t = Tile("ew"), Tile("ew2"), Tile("ew3"), Tile("sq")

            def issue(i):
                h, kind = divmod(i, 3)
                sl = i % NSLOT
                S.dma("sp", ring[sl], scr[kind * 8 + h], ring_ds[sl], r=[wt], w=[ring_t[sl]])

            for i in range(3):
                issue(i)
            rot = [0]

            def next_scp():
                i = rot[0] % 4
                rot[0] += 1
                return i

            for h in range(8):
                if h + 1 < 8:
                    for kind in range(3):
                        issue((h + 1) * 3 + kind)
                slq, slk, slv = (h * 3) % NSLOT, (h * 3 + 1) % NSLOT, (h * 3 + 2) % NSLOT
                slope = 2.0 ** (-(h + 1))
                for blk in range(4):
                    xts = self.xT_t[blk * 4:(blk + 1) * 4]
                    for (sl, dst, dst_t, sc) in ((slq, qT, qT_t, 0.125), (slk, kT, kT_t, 1.0)):
                        pi = next_scp()
                        for k in range(NCH):
                            S.op("pe", lambda e, k=k: e.matmul(scp[pi], ring[sl][:, k, :], self.xT[:, k, blk * 512:(blk + 1) * 512],
                                                             start=(k == 0), stop=(k == NCH - 1)),
                                 r=[ring_t[sl]] + xts, w=[scp_t[pi]])
                        S.op("act", lambda e: e.mul(out=dst[:, blk * 512:(blk + 1) * 512], in_=scp[pi], mul=sc),
                             r=[scp_t[pi]], w=[dst_t])
                    pi = next_scp()
                    for tt in range(4):
                        t = blk * 4 + tt
                        for k in range(NCH):
                            S.op("pe", lambda e, k=k: e.matmul(scp[pi][:, tt * 128:(tt + 1) * 128], self.xT[:, k, t * 128:(t + 1) * 128],
                                                             ring[slv][:, k, :], start=(k == 0), stop=(k == NCH - 1)),
                                 r=[ring_t[slv], self.xT_t[t]], w=[scp_t[pi]])
                    S.op("dve", lambda e: e.tensor_copy(out=vA[:, blk * 4:(blk + 1) * 4, 0:128],
                                                        in_=scp[pi].rearrange("p (a b) -> p a b", a=4)),
                         r=[scp_t[pi]], w=[vA_t])
                steps = [(qb, kc) for qb in range(4) for kc in range(16)]
                LAG = 3
                fifo = []
                cnt = 0
                for i in range(len(steps) + LAG):
                    if i < len(steps):
                        qb, kc = steps[i]
                        c0 = 512 * qb - 128 * kc + 1920
                        cur = []
                        for m in range(2):
                            pi = next_scp()
                            S.op("pe", lambda e: e.matmul(scp[pi], kT[m * 64:(m + 1) * 64, kc * 128:(kc + 1) * 128],
                                                          qT[m * 64:(m + 1) * 64, qb * 512:(qb + 1) * 512], start=True, stop=True),
                                 r=[kT_t, qT_t], w=[scp_t[pi]])
                            si = cnt % NSB
                            pj = cnt % NPT
                            cnt += 1
                            S.op("dve", lambda e: e.scalar_tensor_tensor(out=sbs[si], in0=alibi[:, c0:c0 + 512], scalar=-slope,
                                                                         in1=scp[pi], op0=ALU.mult, op1=ALU.add),
                                 r=[al_t, scp_t[pi]], w=[sbs_t[si]])
                            S.op("act", lambda e: e.activation(out=pT[pj], in_=sbs[si], func=AF.Exp), r=[sbs_t[si]], w=[pT_t[pj]])
                            cur.append(pj)
                        fifo.append((qb, kc, cur))
                    if i >= LAG:
                        pqb, pkc, pjs = fifo.pop(0)
                        for m in range(2):
                            pj = pjs[m]
                            S.op("pe", lambda e: e.matmul(accO[m], vA[:, pkc, 0:128], pT[pj], start=(pkc == 0), stop=(pkc == 15)),
                                 r=[pT_t[pj], vA_t], w=[accO_t[m]])
                            S.op("pe", lambda e: e.matmul(accZ[m], ones128, pT[pj], start=(pkc == 0), stop=(pkc == 15)),
                                 r=[pT_t[pj], ones_t], w=[accZ_t[m]])
                        if pkc == 15:
                            cols = slice(pqb * 512, (pqb + 1) * 512)
                            S.op("act", lambda e: e.copy(out=z0s, in_=accZ[0]), r=[accZ_t[0]], w=[ew_t])
                            S.op("act", lambda e: e.copy(out=z1s, in_=accZ[1]), r=[accZ_t[1]], w=[ew_t])
                            S.op("dve", lambda e: e.tensor_tensor(out=t0, in0=accO[0], in1=z1s, op=ALU.mult),
                                 r=[accO_t[0], ew_t], w=[ew2_t])
                            S.op("dve", lambda e: e.tensor_tensor(out=t1, in0=accO[1], in1=z0s, op=ALU.mult),
                                 r=[accO_t[1], ew_t], w=[ew3_t])
                            S.op("dve", lambda e: e.scalar_tensor_tensor(out=t0, in0=t1, scalar=neglam, in1=t0,
                                                                         op0=ALU.mult, op1=ALU.add),
                                 r=[ew2_t, ew3_t, prm_t], w=[ew2_t])
                            S.op("pool", lambda e: e.tensor_tensor(out=z0s, in0=z0s, in1=z1s, op=ALU.mult), r=[ew_t], w=[ew_t])
                            S.op("pool", lambda e: e.tensor_tensor(out=z0s, in0=z0s, in1=z0s, op=ALU.mult), r=[ew_t], w=[ew_t])
                            S.op("act", lambda e: e.activation(out=sqb, in_=t0, func=AF.Square), r=[ew2_t], w=[sq_t])
                            pi = next_scp()
                            S.op("pe", lambda e: e.matmul(scp[pi], ones128, sqb, start=True, stop=True),
                                 r=[sq_t, ones_t], w=[scp_t[pi]])
                            S.op("dve", lambda e: e.scalar_tensor_tensor(out=t1, in0=z0s, scalar=RMS_EPS * 128.0, in1=scp[pi],
                                                                         op0=ALU.mult, op1=ALU.add),
                                 r=[ew_t, scp_t[pi]], w=[ew3_t])
                            S.op("act", lambda e: e.activation(out=t1, in_=t1, func=AF.Ln, scale=1.0 / 128.0), r=[ew3_t], w=[ew3_t])
                            S.op("act", lambda e: e.activation(out=t1, in_=t1, func=AF.Exp, scale=-0.5), r=[ew3_t], w=[ew3_t])
                            S.op("dve", lambda e: e.scalar_tensor_tensor(out=self.oT[:, h, cols], in0=t0, scalar=gs_col, in1=t1,
                                                                         op0=ALU.mult, op1=ALU.mult),
                                 r=[ew2_t, ew3_t, prm_t], w=self.oT_t[pqb * 4:(pqb + 1) * 4])
            S.barrier()
          self.out_proj_ln("diff_out", 0, 1)

    def layer_na(self, s):
        S, I = self.S, self.I
        scr, wt = self.W["na_qkv"]
        rpbg = I["na_rpbg"]
        with ExitStack() as ost:
          self.oT = self.sb(ost, [128, NCH, S_LEN], BF16, "oT")
          with ExitStack() as st:
            cds = S.pds("nc")
            M2 = self.sb(st, [128, 14, 16, 64], BF16, "M2")
            M2_t = Tile("M2")
            mask = self.sb(st, [128, 64], F32, "mask")
            mask_t = Tile("mask")
            S.dma("sp", mask, I["na_mask"], cds, w=[mask_t])
            NSLOT = 6
            ring = [self.sb(st, [128, NCH, 128], BF16, "wqkv") for _ in range(NSLOT)]
            ring_t = [Tile("wqkv") for _ in range(NSLOT)]
            ring_ds = [S.pds("wqkv") for _ in range(NSLOT)]
            qbd = self.sb(st, [128, 32, 128], BF16, "qbd")
            kT = self.sb(st, [128, S_LEN], BF16, "kT")
            vE = self.sb(st, [128, NT, 2, 65], BF16, "vE")
            vO = self.sb(st, [128, NT - 1, 2, 65], BF16, "vO")
            qT_t, kT_t, vE_t, vO_t = Tile("qT"), Tile("kT"), Tile("vE"), Tile("vO")
            S.op("pool", lambda e: e.memset(qbd, 0.0), w=[qT_t])
            S.op("pool", lambda e: e.memset(vE[:, :, :, 64:65], 1.0), w=[vE_t])
            S.op("pool", lambda e: e.memset(vO[:, :, :, 64:65], 1.0), w=[vO_t])
            NPT = 3
            pT = [self.sb(st, [128, 4, 2, 64], BF16, "pT") for _ in range(NPT)]
            pT_t = [Tile("pT") for _ in range(NPT)]
            ob = [self.sb(st, [64, 8, 128], BF16, "ob") for _ in range(2)]
            ob_t = [Tile("ob") for _ in range(2)]
            rz = self.sb(st, [64, 4], F32, "rz")
            rz_t = Tile("rz")
            scp = [self.ps(st, [128, 512], F32, "scp") for _ in range(4)]
            scp_t = [Tile("scp") for _ in range(4)]
            ops_ = [self.ps(st, [128, 4, 128], F32, "ops")[0:64, 0:2, :] for _ in range(2)]
            ops_t = [Tile("ops") for _ in range(2)]
            tps = [self.ps(st, [128, 1024], BF16, "tps")[:, 0:512] for _ in range(2)]
            tps_t = [Tile("tps") for _ in range(2)]
            with ExitStack() as st2:
                stage = [self.sb(st2, [128, 2, 1024], F32, "stage") for _ in range(1)]
                stage_t = [Tile("stage") for _ in range(1)]
                sds = [S.pds("stage") for _ in range(1)]
                for i in range(7):
                    d0 = 2 * i
                    b = 0
                    S.dma("sp", stage[b][0:64], rpbg[d0:d0 + 2].rearrange("d t h c -> t d (h c)"), sds[b], w=[stage_t[b]])
                    S.dma("sp", stage[b][64:128], rpbg[d0 + 1:d0 + 3].rearrange("d t h c -> t d (h c)"), sds[b], w=[stage_t[b]])
                    S.op("dve", lambda e: e.tensor_tensor(
                        out=M2[:, d0:d0 + 2, :, :].rearrange("p d h c -> p (d h) c"),
                        in0=stage[b].rearrange("p d (h c) -> p (d h) c", c=64),
                        in1=mask.unsqueeze(1).to_broadcast([128, 32, 64]), op=ALU.add),
                        r=[stage_t[b], mask_t], w=[M2_t])
                S.barrier()

            def issue(i):
                c, kind = divmod(i, 3)
                sl = i % NSLOT
                S.dma("sp", ring[sl], scr[kind * 8 + c], ring_ds[sl], r=[wt], w=[ring_t[sl]])

            for i in range(3):
                issue(i)
            rot = [0]

            def next_scp():
                i = rot[0] % 4
                rot[0] += 1
                return i

            ident = self.ident
            M2v = M2.rearrange("p d h c -> p d (h c)")
            for c in range(NCH):
                if c + 1 < NCH:
                    for kind in range(3):
                        issue((c + 1) * 3 + kind)
                slq, slk, slv = (c * 3) % NSLOT, (c * 3 + 1) % NSLOT, (c * 3 + 2) % NSLOT
                for blk in range(4):
                    xts = self.xT_t[blk * 4:(blk + 1) * 4]
                    for isq, sl in ((True, slq), (False, slk)):
                        pi = next_scp()
                        for k in range(NCH):
                            S.op("pe", lambda e, k=k: e.matmul(scp[pi], ring[sl][:, k, :], self.xT[:, k, blk * 512:(blk + 1) * 512],
                                                             start=(k == 0), stop=(k == NCH - 1)),
                                 r=[ring_t[sl]] + xts, w=[scp_t[pi]])
                        if isq:
                            for hh in range(2):
                                S.op("act", lambda e, hh=hh: e.mul(
                                    out=qbd[hh * 64:(hh + 1) * 64, blk * 8:(blk + 1) * 8, hh * 64:(hh + 1) * 64],
                                    in_=scp[pi][hh * 64:(hh + 1) * 64, :].rearrange("p (r c) -> p r c", r=8), mul=0.125),
                                    r=[scp_t[pi]], w=[qT_t])
                        else:
                            S.op("act", lambda e: e.copy(out=kT[:, blk * 512:(blk + 1) * 512], in_=scp[pi]),
                                 r=[scp_t[pi]], w=[kT_t])
                    for (vbuf, vbuf_t, off, ntile) in ((vE, vE_t, 0, 4), (vO, vO_t, 64, 4 if blk < 3 else 3)):
                        pi = next_scp()
                        for tt in range(ntile):
                            t = blk * 4 + tt
                            tok0 = t * 128 + off
                            tl = sorted(set([tok0 // 128, (tok0 + 127) // 128]))
                            for k in range(NCH):
                                S.op("pe", lambda e, k=k: e.matmul(scp[pi][:, tt * 128:(tt + 1) * 128], self.xT[:, k, tok0:tok0 + 128],
                                                                 ring[slv][:, k, :], start=(k == 0), stop=(k == NCH - 1)),
                                     r=[ring_t[slv]] + [self.xT_t[i] for i in tl], w=[scp_t[pi]])
                        S.op("dve", lambda e: e.tensor_copy(
                            out=vbuf[:, blk * 4:blk * 4 + ntile, :, 0:64],
                            in_=scp[pi][:, 0:ntile * 128].rearrange("p (a h d) -> p a h d", a=ntile, h=2)),
                            r=[scp_t[pi]], w=[vbuf_t])
                def stage_a(r):
                    rs = min(max(r - 4, 0), 24)
                    d0b = rs - r + 7
                    pj = r % NPT
                    pi = next_scp()
                    scv = scp[pi].rearrange("p (j c) -> p j c", j=4)
                    S.op("pe", lambda e: e.matmul(scv, ident, M2v[:, d0b:d0b + 7:2, 2 * c * 64:2 * c * 64 + 128], start=True,
                                                  stop=False, skip_group_check=True),
                         r=[M2_t, self.t_const], w=[scp_t[pi]])
                    for j in range(4):
                        ks = (rs + 2 * j) * 64
                        S.op("pe", lambda e: e.matmul(scv[:, j, :], kT[:, ks:ks + 128], qbd[:, r, :], start=False,
                                                      stop=(j == 3), skip_group_check=True),
                             r=[kT_t, qT_t], w=[scp_t[pi]])
                    S.op("act", lambda e: e.activation(out=pT[pj].rearrange("p j h c -> p (j h c)"), in_=scp[pi], func=AF.Exp),
                         r=[scp_t[pi]], w=[pT_t[pj]])

                def stage_b(r):
                    rs = min(max(r - 4, 0), 24)
                    pj = r % NPT
                    oi = r % 2
                    for j in range(4):
                        kr = rs + 2 * j
                        if kr % 2 == 0:
                            vb, vb_t, vt = vE, vE_t, kr // 2
                        else:
                            vb, vb_t, vt = vO, vO_t, (kr - 1) // 2
                        for hh in range(2):
                            S.op("pe", lambda e: e.matmul(ops_[oi][:, hh, 0:65], pT[pj][:, j, hh, :], vb[:, vt, hh, :],
                                                          start=(j == 0 and hh == 0), stop=(j == 3 and hh == 1), skip_group_check=True),
                                 r=[pT_t[pj], vb_t], w=[ops_t[oi]])
                    g8, r8 = divmod(r, 8)
                    bi = g8 % 2
                    S.op("dve", lambda e: e.reciprocal(out=rz[:, 2 * oi:2 * oi + 2], in_=ops_[oi][:, :, 64]), r=[ops_t[oi]], w=[rz_t])
                    S.op("dve", lambda e: e.tensor_tensor(out=ob[bi][:, r8, :].rearrange("p (h d) -> p h d", h=2),
                                                          in0=ops_[oi][:, :, 0:64],
                                                          in1=rz[:, 2 * oi:2 * oi + 2].unsqueeze(2).to_broadcast([64, 2, 64]), op=ALU.mult),
                         r=[ops_t[oi], rz_t], w=[ob_t[bi]])

                def stage_c(r):
                    g8, r8 = divmod(r, 8)
                    bi = g8 % 2
                    S.op("pe", lambda e: e.transpose(out=tps[bi][:, r8 * 64:(r8 + 1) * 64], in_=ob[bi][:, r8, :],
                                                     identity=ident[0:64, 0:64]),
                         r=[ob_t[bi], self.t_const], w=[tps_t[bi]])
                    if r8 == 7:
                        S.op("act", lambda e: e.copy(out=self.oT[:, c, g8 * 512:(g8 + 1) * 512], in_=tps[bi]),
                             r=[tps_t[bi]], w=self.oT_t[g8 * 4:(g8 + 1) * 4])

                for i in range(32 + 2):
                    if i < 32:
                        stage_a(i)
                    if 1 <= i <= 32:
                        stage_b(i - 1)
                    if i >= 2:
                        stage_c(i - 2)
            S.barrier()
          self.out_proj_ln("na_out", 0, 2)

    def layer_mla(self, s):
        S, I = self.S, self.I
        ident = self.ident
        sm_scale = 96.0 ** -0.5
        self.oT, self.oT_t = self.xT, self.xT_t
        with ExitStack() as st:
            cds = S.pds("mc")
            cnT = self.sb(st, [128, 3, S_LEN], BF16, "cnT")
            cqnT = cnT[:, 0:2, :]
            ckvnT = cnT[:, 2, :]
            kropeT = self.sb(st, [128, S_LEN], BF16, "kropeT")
            cq_t, ckv_t, krt_t = Tile("cqnT"), Tile("ckvnT"), Tile("kropeT")
            wuq = self.sb(st, [128, 2, 1568], BF16, "wuq")
            wq2 = self.sb(st, [128, 2, 16, 128], BF16, "wq2")
            wukv = self.sb(st, [128, 2048], BF16, "wukv")
            w_t = Tile("mlaw")
            wds = S.pds("mw")
            S.op("pool", lambda e: e.memset(wuq, 0.0), w=[w_t])
            S.dma("sp", wuq[:, :, 0:1536], self.W["mla_uq"][0], wds, r=[self.W["mla_uq"][1]], w=[w_t])
            S.dma("sp", wukv, self.W["mla_ukv"][0][:, 0, :], wds, r=[self.W["mla_ukv"][1]], w=[w_t])
            wuqv = wuq[:, :, 0:1536].rearrange("p k (h d) -> p k h d", h=16)
            S.op("pool", lambda e: e.memset(wq2, 0.0), w=[w_t])
            S.op("dve", lambda e: e.tensor_scalar(out=wq2[:, :, :, 64:80], in0=wuqv[:, :, :, 80:96], scalar1=-1.0, scalar2=None,
                                                  op0=ALU.mult), r=[w_t], w=[w_t])
            S.op("dve", lambda e: e.tensor_copy(out=wq2[:, :, :, 80:96], in_=wuqv[:, :, :, 64:80]), r=[w_t], w=[w_t])
            cosf = self.sb(st, [128, S_LEN], F32, "cosf")
            sinf = self.sb(st, [128, S_LEN], F32, "sinf")
            rp_t = Tile("ropef")
            S.dma("sp", cosf, I["rope_cos_fm"], cds, w=[rp_t])
            S.dma("sp", sinf, I["rope_sin_fm"], cds, w=[rp_t])
            cds_tiles = [rp_t]
            scp = [self.ps(st, [128, 512], F32, "scp") for _ in range(4)]
            scp_t = [Tile("scp") for _ in range(4)]
            rot = [0]

            def next_scp():
                i = rot[0] % 4
                rot[0] += 1
                return i

            with ExitStack() as st2:
                wa = self.sb(st2, [128, NCH, 416], BF16, "wa")
                wa_t = Tile("wa")
                S.dma("sp", wa, self.W["mla_a"][0], cds, r=[self.W["mla_a"][1]], w=[wa_t])
                gq = self.sb(st2, [128, 256], F32, "gq")
                gkv = self.sb(st2, [128, 128], F32, "gkv")
                g_t = Tile("g")
                S.dma("sp", gq, I["mla_g_q"][0].partition_broadcast(128), cds, w=[g_t])
                S.dma("sp", gkv, I["mla_g_kv"][0].partition_broadcast(128), cds, w=[g_t])
                cos_tm = self.sb(st2, [128, NT, 16], F32, "cos_tm")
                sin_tm = self.sb(st2, [128, NT, 16], F32, "sin_tm")
                S.dma("sp", cos_tm, I["rope_cos_tm"], cds, w=[g_t])
                S.dma("sp", sin_tm, I["rope_sin_tm"], cds, w=[g_t])
                S.group_done(cds, [rp_t, wa_t, g_t])
                KRraw = self.sb(st2, [128, NT, 32], F32, "KRraw")
                KR = self.sb(st2, [128, NT, 128], BF16, "KR")
                kr_t = Tile("KR")
                S.op("pool", lambda e: e.memset(KR, 0.0), w=[kr_t])
                junk = self.sb(st2, [128, 256], F32, "junk")
                junk2 = self.sb(st2, [128, 128], F32, "junk2")
                ssq = [self.sb(st2, [128, 8], F32, "ssq") for _ in range(2)]
                ssq_t = [Tile("ssq") for _ in range(2)]
                nb = [self.sb(st2, [128, 384], BF16, "nb") for _ in range(2)]
                nb_t = [Tile("nb") for _ in range(2)]
                tpa = [self.ps(st2, [128, 8, 128], BF16, "tpa") for _ in range(2)]
                tpa_t = [Tile("tpa") for _ in range(2)]
                ADBG = int(os.environ.get("MLA_A", "9"))
                for t in range(NT if ADBG >= 2 else 0):
                    b = t % 2
                    pi = next_scp()
                    aps = scp[pi]
                    for k in range(NCH):
                        S.op("pe", lambda e, k=k: e.matmul(aps[:, 0:416], self.xT[:, k, t * 128:(t + 1) * 128], wa[:, k, :],
                                                         start=(k == 0), stop=(k == NCH - 1)),
                             r=[self.xT_t[t], wa_t], w=[scp_t[pi]])
                    LDBG = int(os.environ.get("MLA_L", "9"))
                    q = ssq[b]
                    if LDBG < 2:
                        continue
                    S.op("act", lambda e: e.activation(out=junk[:, 0:256], in_=aps[:, 0:256], func=AF.Square, accum_out=q[:, 0:1]),
                         r=[scp_t[pi]], w=[ssq_t[b]])
                    S.op("act", lambda e: e.activation(out=junk2, in_=aps[:, 256:384], func=AF.Square, accum_out=q[:, 1:2]),
                         r=[scp_t[pi]], w=[ssq_t[b]])
                    if LDBG < 3:
                        continue
                    S.op("dve", lambda e: e.tensor_scalar(out=q[:, 2:3], in0=q[:, 0:1], scalar1=1.0 / 256.0, scalar2=RMS_EPS,
                                                          op0=ALU.mult, op1=ALU.add), r=[ssq_t[b]], w=[ssq_t[b]])
                    S.op("dve", lambda e: e.tensor_scalar(out=q[:, 3:4], in0=q[:, 1:2], scalar1=1.0 / 128.0, scalar2=RMS_EPS,
                                                          op0=ALU.mult, op1=ALU.add), r=[ssq_t[b]], w=[ssq_t[b]])
                    S.op("act", lambda e: e.activation(out=q[:, 4:6], in_=q[:, 2:4], func=AF.Ln), r=[ssq_t[b]], w=[ssq_t[b]])
                    S.op("act", lambda e: e.activation(out=q[:, 6:8], in_=q[:, 4:6], func=AF.Exp, scale=-0.5),
                         r=[ssq_t[b]], w=[ssq_t[b]])
                    if LDBG < 4:
                        continue
                    S.op("dve", lambda e: e.scalar_tensor_tensor(out=nb[b][:, 0:256], in0=aps[:, 0:256], scalar=q[:, 6:7], in1=gq,
                                                                 op0=ALU.mult, op1=ALU.mult),
                         r=[scp_t[pi], ssq_t[b], g_t], w=[nb_t[b]])
                    S.op("dve", lambda e: e.scalar_tensor_tensor(out=nb[b][:, 256:384], in0=aps[:, 256:384], scalar=q[:, 7:8], in1=gkv,
                                                                 op0=ALU.mult, op1=ALU.mult),
                         r=[scp_t[pi], ssq_t[b], g_t], w=[nb_t[b]])
                    S.op("dve", lambda e: e.tensor_copy(out=KRraw[:, t, :], in_=aps[:, 384:416]), r=[scp_t[pi]], w=[kr_t])
                    if LDBG < 5:
                        continue
                    for i in range(3):
                        S.op("pe", lambda e, i=i: e.transpose(out=tpa[b][:, i, :], in_=nb[b][:, i * 128:(i + 1) * 128], identity=ident),
                             r=[nb_t[b], self.t_const], w=[tpa_t[b]])
                    S.op("act", lambda e: e.copy(out=cnT[:, :, t * 128:(t + 1) * 128], in_=tpa[b][:, 0:3, :]),
                         r=[tpa_t[b]], w=[cq_t, ckv_t])
                ra = self.sb(st2, [128, NT, 16], F32, "ra")
                rb = self.sb(st2, [128, NT, 16], F32, "rb")
                t1, t2 = KRraw[:, :, 0:16], KRraw[:, :, 16:32]
                if ADBG < 3:
                    S.barrier()
                    raise_skip = True
                else:
                    raise_skip = False
                if not raise_skip:
                    S.op("dve", lambda e: e.tensor_tensor(out=ra, in0=t1, in1=cos_tm, op=ALU.mult), r=[kr_t, g_t], w=[kr_t])
                    S.op("dve", lambda e: e.tensor_tensor(out=rb, in0=t2, in1=sin_tm, op=ALU.mult), r=[kr_t, g_t], w=[kr_t])
                    S.op("dve", lambda e: e.tensor_tensor(out=KR[:, :, 64:80], in0=ra, in1=rb, op=ALU.subtract), r=[kr_t], w=[kr_t])
                    S.op("dve", lambda e: e.tensor_tensor(out=ra, in0=t1, in1=sin_tm, op=ALU.mult), r=[kr_t, g_t], w=[kr_t])
                    S.op("dve", lambda e: e.tensor_tensor(out=rb, in0=t2, in1=cos_tm, op=ALU.mult), r=[kr_t, g_t], w=[kr_t])
                    S.op("dve", lambda e: e.tensor_tensor(out=KR[:, :, 80:96], in0=ra, in1=rb, op=ALU.add), r=[kr_t], w=[kr_t])
                    for g4 in range(4):
                        b = g4 % 2
                        for i in range(4):
                            t = g4 * 4 + i
                            S.op("pe", lambda e, i=i: e.transpose(out=tpa[b][:, i, :], in_=KR[:, t, :], identity=ident),
                                 r=[kr_t, self.t_const], w=[tpa_t[b]])
                        S.op("act", lambda e: e.copy(out=kropeT[64:96, g4 * 512:(g4 + 1) * 512],
                                                     in_=tpa[b][64:96, 0:4, :].rearrange("p a b -> p (a b)")),
                             r=[tpa_t[b]], w=[krt_t])
                S.barrier()

            qT = [self.sb(st, [128, S_LEN], BF16, "qT") for _ in range(2)]
            kT = [self.sb(st, [128, S_LEN], BF16, "kT") for _ in range(2)]
            vA = [self.sb(st, [128, NT, 128], BF16, "vP") for _ in range(2)]
            ones128 = self.sb(st, [128, 128], BF16, "ones128")
            ones_t = Tile("ones")
            S.op("pool", lambda e: e.memset(ones128, 1.0), w=[ones_t])
            qT_t = [Tile("qT") for _ in range(2)]
            kT_t = [Tile("kT") for _ in range(2)]
            vA_t = [Tile("vA") for _ in range(2)]
            for b in range(2):
                S.op("pool", lambda e, b=b: e.memset(vA[b], 0.0), w=[vA_t[b]])
                S.op("pool", lambda e, b=b: e.memset(qT[b], 0.0), w=[qT_t[b]])
                S.op("pool", lambda e, b=b: e.memset(kT[b], 0.0), w=[kT_t[b]])
            tm1 = [self.sb(st, [128, 512], F32, "tm1") for _ in range(2)]
            tm2 = [self.sb(st, [128, 512], F32, "tm2") for _ in range(2)]
            tm_t = [Tile("tm") for _ in range(2)]
            NPT = 4
            pT = [self.sb(st, [128, 512], BF16, "pT") for _ in range(NPT)]
            pT_t = [Tile("pT") for _ in range(NPT)]
            accO = [self.ps(st, [128, 512], F32, "accO") for _ in range(2)]
            accZ = [self.ps(st, [128, 512], F32, "accZ") for _ in range(2)]
            accO_t = [Tile("accO") for _ in range(2)]
            accZ_t = [Tile("accZ") for _ in range(2)]
            rzf = self.sb(st, [128, 512], F32, "rzf")
            rz_t = Tile("rz")
            cnt = 0
            tmk = 0
            qbk = 0
            for h in range(16):
                hb = h % 2
                c = h // 2
                for blk in range(4):
                    cols = slice(blk * 512, (blk + 1) * 512)
                    pa, pb = next_scp(), next_scp()
                    for k in range(2):
                        S.op("pe", lambda e, k=k: e.matmul(scp[pa], wuq[:, k, h * 96:h * 96 + 128], cqnT[:, k, cols],
                                                         start=(k == 0), stop=(k == 1)), r=[w_t, cq_t], w=[scp_t[pa]])
                    for k in range(2):
                        S.op("pe", lambda e, k=k: e.matmul(scp[pb], wq2[:, k, h, :], cqnT[:, k, cols],
                                                         start=(k == 0), stop=(k == 1)), r=[w_t, cq_t], w=[scp_t[pb]])
                    S.op("dve", lambda e: e.tensor_copy(out=qT[hb][0:64, cols], in_=scp[pa][0:64]), r=[scp_t[pa]], w=[qT_t[hb]])
                    tb = tmk % 2
                    tmk += 1
                    S.op("dve", lambda e: e.tensor_tensor(out=tm1[tb][64:96], in0=scp[pa][64:96], in1=cosf[64:96, cols], op=ALU.mult),
                         r=[scp_t[pa], rp_t], w=[tm_t[tb]])
                    S.op("dve", lambda e: e.tensor_tensor(out=tm2[tb][64:96], in0=scp[pb][64:96], in1=sinf[64:96, cols], op=ALU.mult),
                         r=[scp_t[pb], rp_t], w=[tm_t[tb]])
                    S.op("pool", lambda e: e.tensor_tensor(out=qT[hb][64:96, cols], in0=tm1[tb][64:96], in1=tm2[tb][64:96], op=ALU.add),
                         r=[tm_t[tb]], w=[qT_t[hb]])
                    pk = next_scp()
                    S.op("pe", lambda e: e.matmul(scp[pk][0:64], wukv[:, h * 128:h * 128 + 64], ckvnT[:, cols], start=True, stop=True),
                         r=[w_t, ckv_t], w=[scp_t[pk]])
                    S.op("act", lambda e: e.copy(out=kT[hb][0:64, cols], in_=scp[pk][0:64]), r=[scp_t[pk]], w=[kT_t[hb]])
                S.op("pool", lambda e: e.tensor_copy(out=kT[hb][64:96, :], in_=kropeT[64:96, :]), r=[krt_t], w=[kT_t[hb]])
                for half in range(2):
                    pv = next_scp()
                    for i in range(8):
                        t = half * 8 + i
                        S.op("pe", lambda e, i=i: e.matmul(scp[pv][:, i * 64:(i + 1) * 64], ckvnT[:, t * 128:(t + 1) * 128],
                                                         wukv[:, h * 128 + 64:h * 128 + 128], start=True, stop=True),
                             r=[w_t, ckv_t], w=[scp_t[pv]])
                    S.op("dve", lambda e: e.tensor_copy(out=vA[hb][:, half * 8:(half + 1) * 8, hb * 64:(hb + 1) * 64],
                                                        in_=scp[pv].rearrange("p (a d) -> p a d", a=8)),
                         r=[scp_t[pv]], w=[vA_t[hb]])
                steps = [(qb, kc) for qb in range(4) for kc in range(16)]
                LAG = 3
                fifo = []
                for i in range(len(steps) + LAG):
                    if i < len(steps):
                        qb, kc = steps[i]
                        pi = next_scp()
                        S.op("pe", lambda e: e.matmul(scp[pi], kT[hb][:, kc * 128:(kc + 1) * 128],
                                                      qT[hb][:, qb * 512:(qb + 1) * 512], start=True, stop=True),
                             r=[kT_t[hb], qT_t[hb]], w=[scp_t[pi]])
                        pj = cnt % NPT
                        cnt += 1
                        S.op("act", lambda e: e.activation(out=pT[pj], in_=scp[pi], func=AF.Exp, scale=sm_scale),
                             r=[scp_t[pi]], w=[pT_t[pj]])
                        fifo.append((qb, kc, pj))
                    if i >= LAG:
                        pqb, pkc, pj = fifo.pop(0)
                        ab = (qbk + pqb) % 2
                        S.op("pe", lambda e: e.matmul(accO[ab], vA[hb][:, pkc, :], pT[pj], start=(pkc == 0), stop=(pkc == 15)),
                             r=[pT_t[pj], vA_t[hb]], w=[accO_t[ab]])
                        S.op("pe", lambda e: e.matmul(accZ[ab], ones128, pT[pj], start=(pkc == 0), stop=(pkc == 15)),
                             r=[pT_t[pj], ones_t], w=[accZ_t[ab]])
                        if pkc == 15 and os.environ.get("MLA_NOEP", "0") != "1":
                            rows = slice(hb * 64, (hb + 1) * 64)
                            S.op("dve", lambda e: e.reciprocal(out=rzf, in_=accZ[ab]), r=[accZ_t[ab]], w=[rz_t])
                            S.op("dve", lambda e: e.tensor_tensor(out=self.oT[rows, c, pqb * 512:(pqb + 1) * 512], in0=accO[ab][rows],
                                                                  in1=rzf[rows], op=ALU.mult),
                                 r=[accO_t[ab], rz_t], w=self.oT_t[pqb * 4:(pqb + 1) * 4])
                qbk += 4
            S.barrier()
        self.out_proj_ln("mla_out", 0, 3)


_CONSTS = None


def _run(prog, inputs, x_shards):
    global _CONSTS
    if _CONSTS is None:
        _CONSTS = _consts()
    col = np.arange(64)
    dc = np.clip(col[:, None] - col[None, :], -15, 15) + 15
    rpb = np.asarray(inputs["na_rpb"], dtype=np.float32)[0]
    rpbg = np.ascontiguousarray(np.transpose(rpb[:, :, dc], (1, 2, 0, 3)))
    in_maps = []
    for xs in x_shards:
        m = {"x": np.ascontiguousarray(xs, dtype=np.float32)}
        for k in INPUT_SHAPES:
            m[k] = np.ascontiguousarray(inputs[k], dtype=np.float32)
        for k in CONST_SPECS:
            m["c_" + k] = _CONSTS[k]
        m["na_rpbg"] = rpbg
        in_maps.append(m)
    res = run_bass_kernel_spmd(prog.nc, in_maps, core_ids=list(range(len(x_shards))))
    return [np.asarray(r["out"]) for r in res.results]


def kernel(**inputs):
    x = np.asarray(inputs["x"], dtype=np.float32)
    prog = Prog()
    shards = [x[i * SEQ_PER_CORE:(i + 1) * SEQ_PER_CORE] for i in range(NCORES)]
    outs = _run(prog, inputs, shards)
    return np.concatenate(outs, axis=0).astype(np.float32)
```
